# Optimizing a Trainium2 kernel written in Bass

```python
import math
import jax, jax.numpy as jnp
from jax import lax
import numpy as np

D_MODEL = 2048
BATCH = 1
SEQ = 8192
DEPTH = 4
DEC_BATCH = 4
DEC_SEQ = 4096
PAST_LEN = 128

S5_WIDTH = D_MODEL // 2
S5_GROUP = 16
S5_GROUPS = S5_WIDTH // S5_GROUP
S5_STATE = 64
LRU_WIDTH = D_MODEL // 2
LRU_BLOCKS = 16
LRU_BLOCK = LRU_WIDTH // LRU_BLOCKS
LRU_C = 8.0
LRU_CONV = 4
LRU_CONV_LEFT = 2
FFN_HIDDEN = 3 * D_MODEL
FFN_CONV = 3
FFN_CONV_LEFT = 1
OFF_LRU_X = S5_WIDTH
OFF_LRU_G = S5_WIDTH + LRU_WIDTH
OFF_GATE_A = S5_WIDTH + 2 * LRU_WIDTH
OFF_GATE_B = OFF_GATE_A + D_MODEL
IN_COLS = OFF_GATE_B + D_MODEL
EPS = 1e-6
DT_MIN = 0.001
DT_MAX = 0.1

kernel_name = "hybrid_s5_rglru_bidir_encoder"


def _rmsnorm(x, g):
    xf = x.astype(jnp.float32)
    y = xf * lax.rsqrt(jnp.mean(xf * xf, axis=-1, keepdims=True) + EPS)
    return (y * g.astype(jnp.float32)).astype(x.dtype)


def _depthwise_conv(x, w, left):
    k = w.shape[0]
    s = x.shape[1]
    xp = jnp.pad(x, ((0, 0), (left, k - 1 - left), (0, 0)))
    y = xp[:, 0:s] * w[0]
    for j in range(1, k):
        y = y + xp[:, j:j + s] * w[j]
    return y


def _linear_scan_combine(e1, e2):
    a1, b1 = e1
    a2, b2 = e2
    return a1 * a2, a2 * b1 + b2


def _s5_direction(ug, a_re, a_im, log_dt, b_re, b_im, c_re, c_im, reverse):
    f32 = jnp.float32
    lam = lax.complex(a_re.astype(f32), a_im.astype(f32))
    dt = jnp.exp(log_dt.astype(f32))[:, None]
    a_bar = jnp.exp(lam * dt)
    b_bar = ((a_bar - 1.0) / lam)[:, :, None] * lax.complex(b_re.astype(f32), b_im.astype(f32))
    bu = jnp.einsum('bsgh,gph->bsgp', ug.astype(jnp.complex64), b_bar)
    a_all = jnp.broadcast_to(a_bar, bu.shape)
    _, h = lax.associative_scan(_linear_scan_combine, (a_all, bu), axis=1, reverse=reverse)
    c = lax.complex(c_re.astype(f32), c_im.astype(f32))
    return jnp.einsum('bsgp,ghp->bsgh', h, c).real


def _s5_branch(u, a_re, a_im, log_dt, b_re, b_im, c_re, c_im, d, w_glu):
    bsz, s, _ = u.shape
    uf = u.astype(jnp.float32)
    ug = uf.reshape(bsz, s, S5_GROUPS, S5_GROUP)
    y_f = _s5_direction(ug, a_re[0], a_im[0], log_dt[0], b_re[0], b_im[0], c_re[0], c_im[0], False)
    y_b = _s5_direction(ug, a_re[1], a_im[1], log_dt[1], b_re[1], b_im[1], c_re[1], c_im[1], True)
    y = (y_f + y_b).reshape(bsz, s, S5_WIDTH) + d.astype(jnp.float32) * uf
    y = jax.nn.gelu(y)
    y = y * jax.nn.sigmoid(y @ w_glu.astype(jnp.float32))
    return y.astype(u.dtype)


def _rglru_direction(xc, w_a, b_a, w_x, b_x, lam, reverse):
    f32 = jnp.float32
    bsz, s, _ = xc.shape
    xb = xc.reshape(bsz, s, LRU_BLOCKS, LRU_BLOCK)
    r = jax.nn.sigmoid(jnp.einsum('bshi,hij->bshj', xb, w_a.astype(f32)).reshape(bsz, s, LRU_WIDTH) + b_a.astype(f32))
    gi = jax.nn.sigmoid(jnp.einsum('bshi,hij->bshj', xb, w_x.astype(f32)).reshape(bsz, s, LRU_WIDTH) + b_x.astype(f32))
    log_a = -LRU_C * r * jax.nn.softplus(-lam.astype(f32))
    a = jnp.exp(log_a)
    b = jnp.sqrt(-jnp.expm1(2.0 * log_a)) * (gi * xc)
    _, h = lax.associative_scan(_linear_scan_combine, (a, b), axis=1, reverse=reverse)
    return h


def _rglru_branch(xr, gr, conv_w, conv_b, w_a, b_a, w_x, b_x, lam):
    xf = xr.astype(jnp.float32)
    xc = _depthwise_conv(xf, conv_w.astype(jnp.float32), LRU_CONV_LEFT) + conv_b.astype(jnp.float32)
    h = (_rglru_direction(xc, w_a[0], b_a[0], w_x[0], b_x[0], lam[0], False)
         + _rglru_direction(xc, w_a[1], b_a[1], w_x[1], b_x[1], lam[1], True))
    y = h * jax.nn.gelu(gr.astype(jnp.float32))
    return y.astype(xr.dtype)


def _mixer(xn, w_in, s5_a_re, s5_a_im, s5_log_dt, s5_b_re, s5_b_im, s5_c_re, s5_c_im, s5_d, s5_w_glu,
           lru_conv_w, lru_conv_b, lru_w_a, lru_b_a, lru_w_x, lru_b_x, lru_lambda,
           w_proj_a, w_proj_b, w_out):
    proj = xn @ w_in
    u = proj[..., :OFF_LRU_X]
    xr = proj[..., OFF_LRU_X:OFF_LRU_G]
    gr = proj[..., OFF_LRU_G:OFF_GATE_A]
    ga = proj[..., OFF_GATE_A:OFF_GATE_B]
    gb = proj[..., OFF_GATE_B:]
    ya = _s5_branch(u, s5_a_re, s5_a_im, s5_log_dt, s5_b_re, s5_b_im, s5_c_re, s5_c_im, s5_d, s5_w_glu)
    yb = _rglru_branch(xr, gr, lru_conv_w, lru_conv_b, lru_w_a, lru_b_a, lru_w_x, lru_b_x, lru_lambda)
    merged = jax.nn.sigmoid(ga) * (ya @ w_proj_a) + jax.nn.sigmoid(gb) * (yb @ w_proj_b)
    return merged @ w_out


def _conv_ffn(xn, w_up, conv_w, w_down):
    h = _depthwise_conv(xn @ w_up, conv_w, FFN_CONV_LEFT)
    gate = h[..., :FFN_HIDDEN]
    val = h[..., FFN_HIDDEN:]
    return (jax.nn.gelu(gate) * val) @ w_down


def _trunk(x, norm1_g, w_in, s5_a_re, s5_a_im, s5_log_dt, s5_b_re, s5_b_im, s5_c_re, s5_c_im, s5_d, s5_w_glu,
           lru_conv_w, lru_conv_b, lru_w_a, lru_b_a, lru_w_x, lru_b_x, lru_lambda,
           w_proj_a, w_proj_b, w_out, norm2_g, ffn_w_up, ffn_conv_w, ffn_w_down, final_g):
    for l in range(DEPTH):
        x = x + _mixer(_rmsnorm(x, norm1_g[l]), w_in[l], s5_a_re[l], s5_a_im[l], s5_log_dt[l],
                       s5_b_re[l], s5_b_im[l], s5_c_re[l], s5_c_im[l], s5_d[l], s5_w_glu[l],
                       lru_conv_w[l], lru_conv_b[l], lru_w_a[l], lru_b_a[l], lru_w_x[l], lru_b_x[l],
                       lru_lambda[l], w_proj_a[l], w_proj_b[l], w_out[l])
        x = x + _conv_ffn(_rmsnorm(x, norm2_g[l]), ffn_w_up[l], ffn_conv_w[l], ffn_w_down[l])
    return _rmsnorm(x, final_g)


def setup_inputs(seed: int = 0) -> dict:
    key = jax.random.key(seed)
    ks = jax.random.split(key, 32)
    f32 = jnp.float32
    L, D, G, P, H = DEPTH, D_MODEL, S5_GROUPS, S5_STATE, S5_GROUP
    W, NB, BLK, F = LRU_WIDTH, LRU_BLOCKS, LRU_BLOCK, FFN_HIDDEN

    def nrm(k, shape, scale):
        return jax.random.normal(k, shape, f32) * scale

    x_prompt = jax.random.normal(ks[0], (BATCH, SEQ, D), f32)
    x_sample = jax.random.normal(ks[1], (DEC_BATCH, DEC_SEQ, D), f32)
    norm1_g = 1.0 + nrm(ks[2], (L, D), 0.01)
    w_in = nrm(ks[3], (L, D, IN_COLS), D ** -0.5)
    n_idx = jnp.arange(P, dtype=f32)
    s5_a_re = -0.5 + nrm(ks[4], (L, 2, G, P), 0.01)
    s5_a_im = math.pi * n_idx + nrm(ks[5], (L, 2, G, P), 0.01)
    s5_log_dt = jax.random.uniform(ks[6], (L, 2, G), f32, math.log(DT_MIN), math.log(DT_MAX))
    s5_b_re = nrm(ks[7], (L, 2, G, P, H), (2.0 * H) ** -0.5)
    s5_b_im = nrm(ks[8], (L, 2, G, P, H), (2.0 * H) ** -0.5)
    s5_c_re = nrm(ks[9], (L, 2, G, H, P), (2.0 * P) ** -0.5)
    s5_c_im = nrm(ks[10], (L, 2, G, H, P), (2.0 * P) ** -0.5)
    s5_d = nrm(ks[11], (L, S5_WIDTH), 1.0)
    s5_w_glu = nrm(ks[12], (L, S5_WIDTH, S5_WIDTH), S5_WIDTH ** -0.5)
    lru_conv_w = nrm(ks[13], (L, LRU_CONV, W), LRU_CONV ** -0.5)
    lru_conv_b = nrm(ks[14], (L, W), 0.01)
    lru_w_a = nrm(ks[15], (L, 2, NB, BLK, BLK), BLK ** -0.5)
    lru_b_a = nrm(ks[16], (L, 2, W), 0.01)
    lru_w_x = nrm(ks[17], (L, 2, NB, BLK, BLK), BLK ** -0.5)
    lru_b_x = nrm(ks[18], (L, 2, W), 0.01)
    a_target = jax.random.uniform(ks[19], (L, 2, W), f32, 0.9, 0.999)
    s = a_target ** (1.0 / LRU_C)
    lru_lambda = jnp.log(s) - jnp.log1p(-s)
    w_proj_a = nrm(ks[20], (L, S5_WIDTH, D), S5_WIDTH ** -0.5)
    w_proj_b = nrm(ks[21], (L, W, D), W ** -0.5)
    w_out = nrm(ks[22], (L, D, D), D ** -0.5)
    norm2_g = 1.0 + nrm(ks[23], (L, D), 0.01)
    ffn_w_up = nrm(ks[24], (L, D, 2 * F), D ** -0.5)
    ffn_conv_w = nrm(ks[25], (L, FFN_CONV, 2 * F), FFN_CONV ** -0.5)
    ffn_w_down = nrm(ks[26], (L, F, D), F ** -0.5)
    final_g = 1.0 + nrm(ks[27], (D,), 0.01)
    return {"x_prompt": x_prompt, "x_sample": x_sample, "norm1_g": norm1_g, "w_in": w_in,
            "s5_a_re": s5_a_re, "s5_a_im": s5_a_im, "s5_log_dt": s5_log_dt,
            "s5_b_re": s5_b_re, "s5_b_im": s5_b_im, "s5_c_re": s5_c_re, "s5_c_im": s5_c_im,
            "s5_d": s5_d, "s5_w_glu": s5_w_glu, "lru_conv_w": lru_conv_w, "lru_conv_b": lru_conv_b,
            "lru_w_a": lru_w_a, "lru_b_a": lru_b_a, "lru_w_x": lru_w_x, "lru_b_x": lru_b_x,
            "lru_lambda": lru_lambda, "w_proj_a": w_proj_a, "w_proj_b": w_proj_b, "w_out": w_out,
            "norm2_g": norm2_g, "ffn_w_up": ffn_w_up, "ffn_conv_w": ffn_conv_w, "ffn_w_down": ffn_w_down,
            "final_g": final_g}


def reference(x_prompt, x_sample, norm1_g, w_in, s5_a_re, s5_a_im, s5_log_dt, s5_b_re, s5_b_im,
              s5_c_re, s5_c_im, s5_d, s5_w_glu, lru_conv_w, lru_conv_b, lru_w_a, lru_b_a, lru_w_x,
              lru_b_x, lru_lambda, w_proj_a, w_proj_b, w_out, norm2_g, ffn_w_up, ffn_conv_w,
              ffn_w_down, final_g):
    weights = (norm1_g, w_in, s5_a_re, s5_a_im, s5_log_dt, s5_b_re, s5_b_im, s5_c_re, s5_c_im, s5_d,
               s5_w_glu, lru_conv_w, lru_conv_b, lru_w_a, lru_b_a, lru_w_x, lru_b_x, lru_lambda,
               w_proj_a, w_proj_b, w_out, norm2_g, ffn_w_up, ffn_conv_w, ffn_w_down, final_g)
    y_prompt = _trunk(x_prompt, *weights)
    y_sample = _trunk(x_sample, *weights)
    return (y_prompt, y_sample)
```

```python
import math
from contextlib import ExitStack
import numpy as np
import ml_dtypes
import concourse.bass as bass
import concourse.mybir as mybir
from concourse.bass_utils import run_bass_kernel_spmd

F32, BF16 = mybir.dt.float32, mybir.dt.bfloat16
AF, ALU = mybir.ActivationFunctionType, mybir.AluOpType

D = 2048; DEPTH = 4; S5W = 1024; LRW = 1024; FH = 6144; INC = 7168
Q = 8
EPS = 1e-6
ARENA = 168 * 1024
C_ID, C_SW, C_BD, C_RM, C_SGN, C_NSG, C_ONE, C_NPI, C_DEPS, C_CM = 0, 128, 256, 384, 392, 393, 394, 395, 396, 400
NCONST = 400 + 1024


def make_consts():
    c = np.zeros((128, NCONST), np.float32)
    k = np.arange(128)
    c[k, C_ID + k] = 1.0
    c[k, C_SW + (k + 64) % 128] = 1.0
    c[:, C_BD:C_BD + 128] = (k[:, None] // 16 == k[None, :] // 16)
    c[:, C_RM:C_RM + 8] = (k[:, None] // 16 == np.arange(8)[None, :])
    c[:, C_SGN] = np.where(k < 64, -1.0, 1.0)
    c[:, C_NSG] = np.where(k < 64, 1.0, -1.0)
    c[:, C_ONE] = 1.0
    c[:, C_NPI] = -math.pi
    c[:, C_DEPS] = D * EPS
    for g in range(8):
        c[:, C_CM + g * 128:C_CM + (g + 1) * 128] = (k[None, :] // 16 == g)
    return c


class Ctx:
    def __init__(s, nc, stack):
        s.nc, s.stack = nc, stack
        s.eng = {'pe': nc.tensor, 'act': nc.scalar, 'dve': nc.vector, 'pool': nc.gpsimd, 'sp': nc.sync}
        s.sem, s.cnt, s.st = {}, {}, {}
        s.seen = {e: {} for e in s.eng}
        s.nins = 0

    def getsem(s, key):
        if key not in s.sem:
            s.sem[key] = s.stack.enter_context(s.nc.semaphore("sm%d" % len(s.sem)))
            s.cnt[key] = 0
        return s.sem[key]

    def _wait(s, e, key, val):
        if e == 'pe' and key == 'pe':
            return
        if s.seen[e].get(key, 0) >= val:
            return
        s.eng[e].wait_ge(s.sem[key], val)
        s.seen[e][key] = val

    def _deps(s, e, reads, writes):
        for k in reads:
            for key, val in s.st.setdefault(k, ({}, {}))[0].items():
                s._wait(e, key, val)
        for k in writes:
            w, r = s.st.setdefault(k, ({}, {}))
            for key, val in w.items():
                s._wait(e, key, val)
            for key, val in r.items():
                s._wait(e, key, val)

    def _mark(s, reads, writes, key, val):
        for k in reads:
            r = s.st[k][1]
            r[key] = max(r.get(key, 0), val)
        for k in writes:
            w = s.st[k][0]
            w[key] = max(w.get(key, 0), val)

    def op(s, e, fn, reads=(), writes=(), signal=True):
        s._deps(e, reads, writes)
        ins = fn(s.eng[e])
        s.getsem(e)
        if signal:
            s.cnt[e] += 1
            ins.then_inc(s.sem[e], 1)
            val = s.cnt[e]
        else:
            val = s.cnt[e] + 1
        s._mark(reads, writes, e, val)
        s.nins += 1

    def dma(s, q, out, in_, semkey, reads=(), writes=(), **kw):
        s._deps(q, reads, writes)
        s.getsem(semkey)
        ins = s.eng[q].dma_start(out=out, in_=in_, **kw)
        s.cnt[semkey] += 16
        ins.then_inc(s.sem[semkey], 16)
        s._mark(reads, writes, semkey, s.cnt[semkey])
        s.nins += 1

    def barrier(s):
        for e in s.eng:
            for key, val in s.cnt.items():
                if val > 0:
                    s._wait(e, key, val)

    def finish(s):
        for key, val in s.cnt.items():
            if val > 0:
                s._wait('sp', key, val)


def build(T, L, debug=False):
    assert T % 512 == 0
    NT = T // 512
    NC_ = T // Q
    BW = min(512, NC_)
    NB = NC_ // BW
    NL = int(round(math.log2(NC_)))
    assert 2 ** NL == NC_
    HP = max(BW, NC_ // 2)
    CH = min(2048, T)
    NCH = T // CH
    nc = bass.Bass("TRN2", target_bir_lowering=False)
    def dr(name, shape, dt, kind="ExternalInput"):
        if debug and kind == "Internal" and not name.endswith("_t"):
            kind = "ExternalOutput"
        return nc.dram_tensor(name, list(shape), dt, kind=kind).ap()
    xT = dr("xT", [D, T + 2], F32)
    maskd = dr("mask", [128, T + 2], F32)
    constd = dr("consts", [128, NCONST], F32)
    W = {}
    for name, shape in [("norm1_g", [L, D]), ("w_in", [L, D, INC]), ("s5_a_re", [L, 2, 64, 64]), ("s5_a_im", [L, 2, 64, 64]),
                        ("s5_log_dt", [L, 2, 64]), ("s5_b_re", [L, 2, 64, 64, 16]), ("s5_b_im", [L, 2, 64, 64, 16]),
                        ("s5_c_re", [L, 2, 64, 16, 64]), ("s5_c_im", [L, 2, 64, 16, 64]), ("s5_d", [L, S5W]),
                        ("s5_w_glu", [L, S5W, S5W]), ("lru_conv_w", [L, 4, LRW]), ("lru_conv_b", [L, LRW]),
                        ("lru_w_a", [L, 2, 16, 64, 64]), ("lru_b_a", [L, 2, LRW]), ("lru_w_x", [L, 2, 16, 64, 64]),
                        ("lru_b_x", [L, 2, LRW]), ("lru_lambda", [L, 2, LRW]), ("w_proj_a", [L, S5W, D]),
                        ("w_proj_b", [L, LRW, D]), ("w_out", [L, D, D]), ("norm2_g", [L, D]), ("ffn_w_up", [L, D, 2 * FH]),
                        ("ffn_conv_w", [L, 3, 2 * FH]), ("ffn_w_down", [L, FH, D]), ("final_g", [D])]:
        W[name] = dr(name, shape, F32)
    yT = dr("yT", [D, T], F32, kind="ExternalOutput")
    xs = dr("x_scr", [D, T + 2], F32, "Internal")
    xs2 = dr("x_scr2", [D, T + 2], F32, "Internal")
    xs_k, xs2_k = 'xs', 'xs2'
    u_s = dr("u_scr", [S5W, T], BF16, "Internal")
    xr_s = dr("xr_scr", [LRW, T], BF16, "Internal")
    gg_s = dr("gg_scr", [LRW, T], BF16, "Internal")
    sga_s = dr("sga_scr", [D, T], BF16, "Internal")
    sgb_s = dr("sgb_scr", [D, T], BF16, "Internal")
    ya_s = dr("ya_scr", [S5W, T], BF16, "Internal")
    yb_s = dr("yb_scr", [LRW, T], BF16, "Internal")
    WT = {}
    for name, (J, KT) in {"w_in": (56, 16), "s5_w_glu": (8, 8), "w_proj_a": (16, 8), "w_proj_b": (16, 8),
                          "w_out": (16, 16), "ffn_w_up": (96, 16), "ffn_w_down": (16, 48)}.items():
        WT[name] = dr(name + "_t", [L, J, 128, KT, 128], BF16, "Internal")

    with ExitStack() as stack:
        sb = lambda name, shape, dt: stack.enter_context(nc.sbuf_tensor(name, list(shape), dt))
        K = Ctx(nc, stack)
        cst = sb("cst", [128, NCONST], F32)
        idb = sb("idb", [128, 128], BF16); swb = sb("swb", [128, 128], BF16); oneb = sb("oneb", [128, 128], BF16)
        g1c = sb("g1c", [128, L, 16], F32); g2c = sb("g2c", [128, L, 16], F32); gfc = sb("gfc", [128, 16], F32)
        s5dc = sb("s5dc", [128, L, 8], F32)
        cwc = sb("cwc", [128, L, 4, 8], F32); cbc = sb("cbc", [128, L, 8], F32)
        bac = sb("bac", [128, L, 2, 8], F32); bxc = sb("bxc", [128, L, 2, 8], F32)
        lamc = sb("lamc", [128, L, 2, 8], F32); sc1 = sb("sc1", [128, L, 2, 8], F32); sc2 = sb("sc2", [128, L, 2, 8], F32)
        fcw = sb("fcw", [128, L, 3, 96], F32)
        NWS = 6
        wslot = [sb("wslot%d" % i, [128, 16, 128], BF16) for i in range(NWS)]
        big = sb("big", [128, ARENA // 4], F32)
        psb = [stack.enter_context(nc.psum_tensor("ps%d" % i, [128, 512], F32)) for i in range(8)]
        psk = ["ps%d" % i for i in range(8)]

        def carve(specs):
            K.barrier()
            off = 0
            out = {}
            for name, shape, dt in specs:
                n = int(np.prod(shape))
                nb = n * (4 if dt == F32 else 2)
                nb4 = (nb + 3) // 4
                ap = big[:, off:off + nb4]
                if dt != F32:
                    ap = ap.bitcast(BF16)[:, 0:n]
                if len(shape) > 1:
                    names = ["d%d" % i for i in range(len(shape))]
                    ap = ap.rearrange("p (%s) -> p %s" % (" ".join(names), " ".join(names)),
                                      **{names[i]: shape[i] for i in range(len(shape) - 1)})
                out[name] = ap
                off += nb4
            assert off * 4 <= ARENA, off * 4
            return out

        K.dma('sp', cst[:], constd[:, :], 'cst', writes=['cst'])
        K.op('act', lambda e: e.copy(out=idb[:], in_=cst[:, C_ID:C_ID + 128]), reads=['cst'], writes=['idb'])
        K.op('act', lambda e: e.copy(out=swb[:], in_=cst[:, C_SW:C_SW + 128]), reads=['cst'], writes=['swb'])
        K.op('dve', lambda e: e.memset(oneb[:], 1.0), writes=['oneb'])
        onec = cst[:, C_ONE:C_ONE + 1]
        sgnc = cst[:, C_SGN:C_SGN + 1]
        nsgc = cst[:, C_NSG:C_NSG + 1]
        npic = cst[:, C_NPI:C_NPI + 1]

        def ldsmall(dst, src, key):
            K.dma('sp', dst, src, key, writes=[key], allow_slow_non_contiguous=True)
        for l in range(L):
            ldsmall(g1c[:, l, :], W["norm1_g"][l].rearrange("(kt p) -> p kt", p=128), 'g1c')
            ldsmall(g2c[:, l, :], W["norm2_g"][l].rearrange("(kt p) -> p kt", p=128), 'g2c')
            ldsmall(s5dc[:, l, :], W["s5_d"][l].rearrange("(kt p) -> p kt", p=128), 's5dc')
            ldsmall(cbc[:, l, :], W["lru_conv_b"][l].rearrange("(kt p) -> p kt", p=128), 'cbc')
            for k in range(4):
                ldsmall(cwc[:, l, k, :], W["lru_conv_w"][l, k].rearrange("(kt p) -> p kt", p=128), 'cwc')
            for d in range(2):
                ldsmall(bac[:, l, d, :], W["lru_b_a"][l, d].rearrange("(kt p) -> p kt", p=128), 'bac')
                ldsmall(bxc[:, l, d, :], W["lru_b_x"][l, d].rearrange("(kt p) -> p kt", p=128), 'bxc')
                ldsmall(lamc[:, l, d, :], W["lru_lambda"][l, d].rearrange("(kt p) -> p kt", p=128), 'lamc')
            for k in range(3):
                ldsmall(fcw[:, l, k, :], W["ffn_conv_w"][l, k].rearrange("(kt p) -> p kt", p=128), 'fcw')
        ldsmall(gfc[:], W["final_g"].rearrange("(kt p) -> p kt", p=128), 'gfc')
        sD = math.sqrt(D)
        K.op('act', lambda e: e.mul(out=g1c[:], in_=g1c[:], mul=sD), reads=['g1c'], writes=['g1c'])
        K.op('act', lambda e: e.mul(out=g2c[:], in_=g2c[:], mul=sD), reads=['g2c'], writes=['g2c'])
        K.op('act', lambda e: e.mul(out=gfc[:], in_=gfc[:], mul=sD), reads=['gfc'], writes=['gfc'])
        K.op('act', lambda e: e.activation(out=sc1[:], in_=lamc[:], func=AF.Exp, scale=-1.0), reads=['lamc'], writes=['sc1'])
        K.op('act', lambda e: e.activation(out=sc1[:], in_=sc1[:], func=AF.Ln, bias=onec, scale=1.0), reads=['sc1', 'cst'], writes=['sc1'])
        K.op('act', lambda e: e.mul(out=sc2[:], in_=sc1[:], mul=-16.0), reads=['sc1'], writes=['sc2'])
        K.op('act', lambda e: e.mul(out=sc1[:], in_=sc1[:], mul=-8.0), reads=['sc1', 'sc2'], writes=['sc1'])

        K.dma('sp', xs[:, :], xT[:, :], 'xcopy', writes=['xs'])
        K.dma('sp', xs2[:, :], xT[:, :], 'xcopy', writes=['xs2'])
        for l in range(L):
            for name, (J, KT) in {"w_in": (56, 16), "s5_w_glu": (8, 8), "w_proj_a": (16, 8), "w_proj_b": (16, 8),
                                  "w_out": (16, 16), "ffn_w_up": (96, 16), "ffn_w_down": (16, 48)}.items():
                for j in range(J):
                    src = W[name][l][:, j * 128:(j + 1) * 128].rearrange("(kt p) n -> p kt n", p=128)
                    K.dma('pool', WT[name][l, j], src, 'wconv%d' % l, writes=[('wt', l)])

        wctr = [0]

        def wload(name, l, j, KT):
            i = wctr[0] % NWS
            wctr[0] += 1
            key = 'wslot%d' % i
            K.dma('sp', wslot[i][:, 0:KT, :], WT[name][l, j], key, reads=[('wt', l)], writes=[key])
            return wslot[i], key

        pctr = [0]

        def nextps():
            i = pctr[0] % 8
            pctr[0] += 1
            return psb[i], psk[i]

        def rmsnorm(xt, xkey, ncols, gcol, xn, xnkey, tmp, tmpkey, rs, rskey, out_f32=None):
            K.op('act', lambda e: e.activation(out=tmp[:, :, 0:ncols], in_=xt[:, :, 0:ncols], func=AF.Square),
                 reads=[xkey], writes=[tmpkey])
            c0 = 0
            while c0 < ncols:
                cw = min(512, ncols - c0)
                ps, pk = nextps()
                for kt in range(16):
                    K.op('pe', lambda e, kt=kt, c0=c0, cw=cw, ps=ps: e.matmul(ps[:, 0:cw], lhsT=oneb[:], rhs=tmp[:, kt, c0:c0 + cw],
                                                                   start=(kt == 0), stop=(kt == 15)),
                         reads=[tmpkey, 'oneb'], writes=[pk], signal=(kt == 15))
                K.op('act', lambda e, c0=c0, cw=cw, ps=ps: e.activation(out=rs[:, c0:c0 + cw], in_=ps[:, 0:cw], func=AF.Ln,
                                                                 bias=cst[:, C_DEPS:C_DEPS + 1], scale=1.0),
                     reads=[pk, 'cst'], writes=[rskey])
                K.op('act', lambda e, c0=c0, cw=cw: e.activation(out=rs[:, c0:c0 + cw], in_=rs[:, c0:c0 + cw], func=AF.Exp, scale=-0.5),
                     reads=[rskey], writes=[rskey])
                c0 += cw
            for kt in range(16):
                eng = 'dve'
                dst = xn if out_f32 is None else out_f32
                K.op(eng, lambda e, kt=kt, dst=dst: e.scalar_tensor_tensor(out=dst[:, kt, 0:ncols], in0=xt[:, kt, 0:ncols], scalar=gcol[:, kt:kt + 1],
                                                                  in1=rs[:, 0:ncols], op0=ALU.mult, op1=ALU.mult),
                     reads=[xkey, rskey, 'g1c', 'g2c', 'gfc'], writes=[xnkey])

        for l in range(L):
            A = carve([("xt0", [16, 512], F32), ("xt1", [16, 512], F32), ("tmp", [16, 512], BF16), ("rs", [512], F32),
                       ("xn", [16, 512], BF16), ("og0", [8, 512], BF16), ("og1", [8, 512], BF16)])
            for tt in range(NT):
                c0 = 1 + tt * 512
                xt = A["xt%d" % (tt % 2)]; xkey = 'A.xt%d' % (tt % 2)
                K.dma('sp', xt, xs[:, c0:c0 + 512].rearrange("(kt p) c -> p kt c", p=128), xkey, reads=[xs_k], writes=[xkey])
                rmsnorm(xt, xkey, 512, g1c[:, l, :], A["xn"], 'A.xn', A["tmp"], 'A.tmp', A["rs"], 'A.rs')
                for jg in range(7):
                    og = A["og%d" % (jg % 2)]; ogk = 'A.og%d' % (jg % 2)
                    for jj in range(8):
                        j = jg * 8 + jj
                        wt, wk = wload("w_in", l, j, 16)
                        ps, pk = nextps()
                        for kt in range(16):
                            K.op('pe', lambda e, kt=kt, wt=wt, ps=ps: e.matmul(ps[:, :], lhsT=wt[:, kt, :], rhs=A["xn"][:, kt, :],
                                                                      start=(kt == 0), stop=(kt == 15)),
                                 reads=[wk, 'A.xn'], writes=[pk], signal=(kt == 15))
                        func = AF.Copy if jg < 2 else (AF.Gelu if jg == 2 else AF.Sigmoid)
                        K.op('act', lambda e, jj=jj, og=og, ps=ps, func=func: e.activation(out=og[:, jj, :], in_=ps[:, :], func=func),
                             reads=[pk], writes=[ogk])
                    dst = [u_s, xr_s, gg_s, sga_s[0:1024], sga_s[1024:2048], sgb_s[0:1024], sgb_s[1024:2048]][jg]
                    dkey = ['u_s', 'xr_s', 'gg_s', 'sga_s', 'sga_s', 'sgb_s', 'sgb_s'][jg]
                    K.dma('act', dst[:, tt * 512:(tt + 1) * 512].rearrange("(j p) c -> p j c", p=128), og, ogk, reads=[ogk], writes=[dkey])

            NS0 = 4
            B = carve([("u", [T], BF16), ("ya", [T], BF16),
                       ("H", [18, HP + NC_], BF16),
                       ("CAL", [8 * Q * 2, 128], BF16),
                       ("S0L", [NS0, Q, 128], BF16), ("RP", [4, 128], BF16), ("KB", [2 * Q + 1, 128], BF16),
                       ("E", [2, Q, 128], BF16), ("CA", [2, Q + 1, 128], BF16),
                       ("b1", [128], F32), ("b2", [128], F32), ("c1", [128], F32), ("c2", [128], F32),
                       ("nat", [128], F32), ("nat2", [128], F32),
                       ("sm", [40, 8], F32), ("pw", [2, Q + 1, 2, 8], F32), ("zc", [2, Q, 2, 8], F32), ("rp", [2, NL, 2, 8], F32),
                       ("t1", [128], F32), ("t2", [128], F32)])
            K.op('pool', lambda e: e.memset(B["H"][:, :, :], 0.0), writes=['B.H%d' % i for i in range(18)])
            SM = lambda i: B["sm"][:, i, :]
            PE_ = 'dve'

            def tt_(out, a, b, op, rk, wk):
                K.op(PE_, lambda e: e.tensor_tensor(out=out, in0=a, in1=b, op=op), reads=rk, writes=wk)

            def cmul(ore, oim, are, aim, bre, bim, rk, wk):
                tt_(SM(38), are, bre, ALU.mult, rk, ['B.t38'])
                tt_(SM(39), aim, bim, ALU.mult, rk, ['B.t39'])
                tt_(SM(37), are, bim, ALU.mult, rk, ['B.t37'])
                tt_(SM(36), aim, bre, ALU.mult, rk, ['B.t36'])
                tt_(ore, SM(38), SM(39), ALU.subtract, ['B.t38', 'B.t39'], wk)
                tt_(oim, SM(37), SM(36), ALU.add, ['B.t37', 'B.t36'], wk)

            for gt in range(8):
                g0 = gt * 8
                K.dma('sp', B["u"], u_s[gt * 128:(gt + 1) * 128, :], 'B.u', reads=['u_s'], writes=['B.u'])
                for d in range(2):
                    for nm, dst in (("s5_a_re", 0), ("s5_a_im", 1)):
                        for h in range(2):
                            K.dma('sp', B["nat"][0:8, h * 64:(h + 1) * 64], W[nm][l, d, g0:g0 + 8, :], 'B.nat', writes=['B.nat'])
                        ps, pk = nextps()
                        K.op('pe', lambda e, ps=ps: e.transpose(out=ps[:, 0:8], in_=B["nat"][0:8, :], identity=cst[0:8, C_ID:C_ID + 8]),
                             reads=['B.nat', 'cst'], writes=[pk])
                        K.op('act', lambda e, ps=ps, dst=dst: e.copy(out=SM(dst), in_=ps[:, 0:8]), reads=[pk], writes=['B.sm%d' % dst])
                    K.dma('sp', SM(2), W["s5_log_dt"][l, d, g0:g0 + 8].partition_broadcast(128), 'B.sm2', writes=['B.sm2'])
                    K.op('act', lambda e: e.activation(out=SM(2), in_=SM(2), func=AF.Exp), reads=['B.sm2'], writes=['B.sm2'])
                    k0, k1, k2 = ['B.sm0'], ['B.sm1'], ['B.sm2']
                    tt_(SM(3), SM(0), SM(2), ALU.mult, k0 + k2, ['B.sm3'])
                    tt_(SM(4), SM(1), SM(2), ALU.mult, k1 + k2, ['B.sm4'])
                    K.op('act', lambda e: e.activation(out=SM(5), in_=SM(3), func=AF.Exp), reads=['B.sm3'], writes=['B.sm5'])
                    K.op('act', lambda e: e.activation(out=SM(6), in_=SM(4), func=AF.Sin, scale=1.0 / 16), reads=['B.sm4'], writes=['B.sm6'])
                    K.op('act', lambda e: e.activation(out=SM(7), in_=SM(4), func=AF.Sin, scale=1.0 / 32), reads=['B.sm4'], writes=['B.sm7'])
                    tt_(SM(7), SM(7), SM(7), ALU.mult, ['B.sm7'], ['B.sm7'])
                    K.op(PE_, lambda e: e.tensor_scalar(out=SM(7), in0=SM(7), scalar1=-2.0, scalar2=1.0, op0=ALU.mult, op1=ALU.add),
                         reads=['B.sm7'], writes=['B.sm7'])
                    for _sq in range(4):
                        cmul(SM(7), SM(6), SM(7), SM(6), SM(7), SM(6), ['B.sm7', 'B.sm6'], ['B.sm7', 'B.sm6'])
                    PW = lambda k, c: B["pw"][:, d, k, c, :]
                    pwk = lambda k: 'B.pw%d_%d' % (d, k)
                    tt_(PW(1, 0), SM(5), SM(7), ALU.mult, ['B.sm5', 'B.sm7'], [pwk(1)])
                    tt_(PW(1, 1), SM(5), SM(6), ALU.mult, ['B.sm5', 'B.sm6'], [pwk(1)])
                    K.op(PE_, lambda e: e.memset(PW(0, 0), 1.0), writes=[pwk(0)])
                    K.op(PE_, lambda e: e.memset(PW(0, 1), 0.0), writes=[pwk(0)])
                    K.op(PE_, lambda e: e.tensor_scalar_add(out=SM(8), in0=PW(1, 0), scalar1=-1.0), reads=[pwk(1)], writes=['B.sm8'])
                    tt_(SM(9), SM(0), SM(0), ALU.mult, k0, ['B.sm9'])
                    tt_(SM(10), SM(1), SM(1), ALU.mult, k1, ['B.sm10'])
                    tt_(SM(9), SM(9), SM(10), ALU.add, ['B.sm9', 'B.sm10'], ['B.sm9'])
                    K.op(PE_, lambda e: e.reciprocal(out=SM(9), in_=SM(9)), reads=['B.sm9'], writes=['B.sm9'])
                    tt_(SM(11), SM(0), SM(9), ALU.mult, k0 + ['B.sm9'], ['B.sm11'])
                    tt_(SM(12), SM(1), SM(9), ALU.mult, k1 + ['B.sm9'], ['B.sm12'])
                    K.op(PE_, lambda e: e.tensor_scalar_mul(out=SM(12), in0=SM(12), scalar1=-1.0), reads=['B.sm12'], writes=['B.sm12'])
                    cmul(SM(13), SM(14), SM(8), PW(1, 1), SM(11), SM(12), ['B.sm8', pwk(1), 'B.sm11', 'B.sm12'], ['B.coef'])
                    for k in range(2, Q + 1):
                        cmul(PW(k, 0), PW(k, 1), PW(k - 1, 0), PW(k - 1, 1), PW(1, 0), PW(1, 1), [pwk(k - 1), pwk(1)], [pwk(k)])
                    ZC = lambda k, c: B["zc"][:, d, k, c, :]
                    for k in range(Q):
                        cmul(ZC(k, 0), ZC(k, 1), PW(k, 0), PW(k, 1), SM(13), SM(14), [pwk(k), 'B.coef'], ['B.zc%d' % d])
                        K.op(PE_, lambda e, k=k: e.tensor_scalar_mul(out=ZC(k, 1), in0=ZC(k, 1), scalar1=sgnc), reads=['B.zc%d' % d, 'cst'], writes=['B.zc%d' % d])
                    RPc = lambda lv, c: B["rp"][:, d, lv, c, :]
                    K.op(PE_, lambda e: e.tensor_copy(out=RPc(0, 0), in_=PW(Q, 0)), reads=[pwk(Q)], writes=['B.rp%d' % d])
                    K.op(PE_, lambda e: e.tensor_copy(out=RPc(0, 1), in_=PW(Q, 1)), reads=[pwk(Q)], writes=['B.rp%d' % d])
                    for lv in range(1, NL):
                        cmul(RPc(lv, 0), RPc(lv, 1), RPc(lv - 1, 0), RPc(lv - 1, 1), RPc(lv - 1, 0), RPc(lv - 1, 1), ['B.rp%d' % d], ['B.rp%d' % d])
                    K.op(PE_, lambda e: e.tensor_scalar_mul(out=B["rp"][:, d, :, 1, :], in0=B["rp"][:, d, :, 1, :], scalar1=nsgc),
                         reads=['B.rp%d' % d, 'cst'], writes=['B.rp%d' % d])
                    bre = W["s5_b_re"][l, d, g0:g0 + 8].rearrange("g p j -> p g j")
                    bim = W["s5_b_im"][l, d, g0:g0 + 8].rearrange("g p j -> p g j")
                    v3 = lambda ap: ap.rearrange("p (g j) -> p g j", g=8)
                    K.dma('sp', v3(B["b1"])[0:64], bre, 'B.b1', writes=['B.b1'])
                    K.dma('sp', v3(B["b1"])[64:128], bim, 'B.b1', writes=['B.b1'])
                    K.dma('sp', v3(B["b2"])[0:64], bim, 'B.b2', writes=['B.b2'])
                    K.dma('sp', v3(B["b2"])[64:128], bre, 'B.b2', writes=['B.b2'])
                    cre = W["s5_c_re"][l, d, g0:g0 + 8].rearrange("g i p -> (g i) p")
                    cim = W["s5_c_im"][l, d, g0:g0 + 8].rearrange("g i p -> (g i) p")
                    for (first, second, dst, nat, nk) in ((cre, cim, "c1", "nat", 'B.nat'), (cim, cre, "c2", "nat2", 'B.nat2')):
                        K.dma('sp', B[nat][:, 0:64], first, nk, writes=[nk])
                        K.dma('sp', B[nat][:, 64:128], second, nk, writes=[nk])
                        ps, pk = nextps()
                        K.op('pe', lambda e, ps=ps, nat=nat: e.transpose(out=ps[:, 0:128], in_=B[nat][:, :], identity=cst[:, C_ID:C_ID + 128]),
                             reads=[nk, 'cst'], writes=[pk])
                        K.op('act', lambda e, ps=ps, dst=dst: e.copy(out=B[dst], in_=ps[:, 0:128]), reads=[pk], writes=['B.' + dst])
                    bc = lambda col: col.unsqueeze(2).to_broadcast([128, 8, 16])
                    for k in range(Q):
                        tt_(v3(B["t1"]), v3(B["b1"]), bc(ZC(k, 0)), ALU.mult, ['B.b1', 'B.zc%d' % d], ['B.t1'])
                        tt_(v3(B["t2"]), v3(B["b2"]), bc(ZC(k, 1)), ALU.mult, ['B.b2', 'B.zc%d' % d], ['B.t2'])
                        tt_(B["E"][:, d, k, :], B["t1"], B["t2"], ALU.add, ['B.t1', 'B.t2'], ['B.E%d' % d])
                    for tau in range(Q + 1):
                        K.op(PE_, lambda e, tau=tau: e.tensor_scalar_mul(out=SM(15), in0=PW(tau, 0), scalar1=nsgc), reads=[pwk(tau), 'cst'], writes=['B.sm15'])
                        tt_(v3(B["t1"]), v3(B["c1"]), bc(SM(15)), ALU.mult, ['B.c1', 'B.sm15'], ['B.t1'])
                        tt_(v3(B["t2"]), v3(B["c2"]), bc(PW(tau, 1)), ALU.mult, ['B.c2', pwk(tau)], ['B.t2'])
                        tt_(B["CA"][:, d, tau, :], B["t1"], B["t2"], ALU.subtract, ['B.t1', 'B.t2'], ['B.CA%d' % d])
                    for tau in range(1, Q + 1):
                        for g in range(8):
                            idx = (g * Q + (tau - 1)) * 2 + d
                            K.op('pool', lambda e, idx=idx, tau=tau, g=g: e.tensor_tensor(out=B["CAL"][:, idx, :], in0=B["CA"][:, d, tau, :],
                                                                                   in1=cst[:, C_CM + g * 128:C_CM + (g + 1) * 128], op=ALU.mult),
                                 reads=['B.CA%d' % d, 'cst'], writes=['B.CAL'])
                    for tau in range(Q):
                        ps, pk = nextps()
                        K.op('pe', lambda e, ps=ps, tau=tau: e.matmul(ps[:, 0:128], lhsT=B["E"][:, d, 0, :], rhs=B["CA"][:, d, tau, :], start=True, stop=True),
                             reads=['B.E%d' % d, 'B.CA%d' % d], writes=[pk])
                        K.op('dve', lambda e, ps=ps, tau=tau: e.tensor_tensor(out=B["KB"][:, tau * 2 + d, :], in0=ps[:, 0:128], in1=cst[:, C_BD:C_BD + 128], op=ALU.mult),
                             reads=[pk, 'cst'], writes=['B.KB'])
                K.op('dve', lambda e: e.tensor_scalar_mul(out=B["t1"], in0=cst[:, C_ID:C_ID + 128], scalar1=s5dc[:, l, gt:gt + 1]), reads=['cst', 's5dc'], writes=['B.t1'])
                tt_(B["t2"], B["KB"][:, 0, :], B["KB"][:, 1, :], ALU.add, ['B.KB'], ['B.t2'])
                tt_(B["KB"][:, 2 * Q, :], B["t1"], B["t2"], ALU.add, ['B.t1', 'B.t2'], ['B.KB'])

                s0ctr = 0
                rpctr = 0
                for d in range(2):
                    for g in range(8):
                        sl = s0ctr % NS0; s0ctr += 1
                        slk = 'B.S0L%d' % sl
                        for k in range(Q):
                            ps, pk = nextps()
                            K.op('pe', lambda e, ps=ps, k=k: e.transpose(out=ps[:, 0:128].bitcast(BF16)[:, 0:128], in_=B["E"][:, d, k, :], identity=idb[:]),
                                 reads=['B.E%d' % d, 'idb'], writes=[pk])
                            K.op('dve', lambda e, ps=ps, k=k, sl=sl, g=g: e.tensor_scalar_mul(out=B["S0L"][:, sl, k, :], in0=ps[:, 0:128].bitcast(BF16)[:, 0:128],
                                                                                  scalar1=cst[:, C_RM + g:C_RM + g + 1]),
                                 reads=[pk, 'cst'], writes=[slk])
                        hi = g * 2 + d
                        Hf = B["H"][:, hi, :]; Hfk = 'B.H%d' % hi
                        Hs = B["H"][:, 16 + d, :]; Hsk = 'B.H%d' % (16 + d)
                        off = HP if d == 0 else 0
                        for blk in range(NB):
                            ps, pk = nextps()
                            for s in range(Q):
                                kk = (Q - 1 - s) if d == 0 else s
                                c_lo = blk * BW
                                K.op('pe', lambda e, ps=ps, s=s, kk=kk, c_lo=c_lo, sl=sl: e.matmul(ps[:, 0:BW], lhsT=B["S0L"][:, sl, kk, :],
                                                                                      rhs=B["u"][:, s + Q * c_lo: s + Q * (c_lo + BW - 1) + 1: Q],
                                                                                      start=(s == 0), stop=(s == Q - 1)),
                                     reads=[slk, 'B.u'], writes=[pk], signal=(s == Q - 1))
                            K.op('act', lambda e, ps=ps, blk=blk, Hf=Hf, off=off: e.copy(out=Hf[:, off + blk * BW: off + (blk + 1) * BW], in_=ps[:, 0:BW]),
                                 reads=[pk], writes=[Hfk])
                        src, srck, dst, dstk = Hf, Hfk, Hs, Hsk
                        for lv in range(NL):
                            sh = 2 ** lv
                            ri = rpctr % 4; rpctr += 1
                            rk = 'B.RP%d' % ri
                            K.op('pool', lambda e, lv=lv, g=g: e.tensor_scalar_mul(out=B["t2"].bitcast(BF16)[:, 0:128], in0=swb[:], scalar1=B["rp"][:, d, lv, 1, g:g + 1]),
                                 reads=['swb', 'B.rp%d' % d], writes=['B.t2'])
                            K.op('dve', lambda e, lv=lv, g=g, ri=ri: e.scalar_tensor_tensor(out=B["RP"][:, ri, :], in0=idb[:], scalar=B["rp"][:, d, lv, 0, g:g + 1],
                                                                                 in1=B["t2"].bitcast(BF16)[:, 0:128], op0=ALU.mult, op1=ALU.add),
                                 reads=['idb', 'B.rp%d' % d, 'B.t2'], writes=[rk])
                            for blk in range(NB):
                                ps, pk = nextps()
                                a0 = off + blk * BW
                                sa = a0 - sh if d == 0 else a0 + sh
                                K.op('pe', lambda e, ps=ps, a0=a0, src=src: e.matmul(ps[:, 0:BW], lhsT=idb[:], rhs=src[:, a0:a0 + BW], start=True, stop=False),
                                     reads=['idb', srck], writes=[pk], signal=False)
                                K.op('pe', lambda e, ps=ps, sa=sa, src=src, ri=ri: e.matmul(ps[:, 0:BW], lhsT=B["RP"][:, ri, :], rhs=src[:, sa:sa + BW], start=False, stop=True),
                                     reads=[rk, srck], writes=[pk])
                                ee = 'act' if blk % 2 == 0 else 'dve'
                                if ee == 'act':
                                    K.op('act', lambda e, ps=ps, a0=a0, dst=dst: e.copy(out=dst[:, a0:a0 + BW], in_=ps[:, 0:BW]), reads=[pk], writes=[dstk])
                                else:
                                    K.op('dve', lambda e, ps=ps, a0=a0, dst=dst: e.tensor_copy(out=dst[:, a0:a0 + BW], in_=ps[:, 0:BW]), reads=[pk], writes=[dstk])
                            src, srck, dst, dstk = dst, dstk, src, srck
                        if NL % 2 == 1:
                            K.op('pool', lambda e, Hf=Hf, Hs=Hs, off=off: e.tensor_copy(out=Hf[:, off:off + NC_], in_=Hs[:, off:off + NC_]), reads=[Hsk], writes=[Hfk])
                for r in range(Q):
                    for blk in range(NB):
                        ps, pk = nextps()
                        c_lo = blk * BW
                        first = True
                        for g in range(8):
                            idx = (g * Q + r) * 2 + 0
                            Hf = B["H"][:, g * 2, :]
                            K.op('pe', lambda e, ps=ps, idx=idx, Hf=Hf, c_lo=c_lo, first=first: e.matmul(ps[:, 0:BW], lhsT=B["CAL"][:, idx, :], rhs=Hf[:, HP + c_lo - 1: HP + c_lo - 1 + BW],
                                                                                          start=first, stop=False),
                                 reads=['B.CAL', 'B.H%d' % (g * 2)], writes=[pk], signal=False)
                            first = False
                            idx = (g * Q + (Q - r - 1)) * 2 + 1
                            Hb = B["H"][:, g * 2 + 1, :]
                            K.op('pe', lambda e, ps=ps, idx=idx, Hb=Hb, c_lo=c_lo: e.matmul(ps[:, 0:BW], lhsT=B["CAL"][:, idx, :], rhs=Hb[:, c_lo + 1: c_lo + 1 + BW],
                                                                              start=False, stop=False),
                                 reads=['B.CAL', 'B.H%d' % (g * 2 + 1)], writes=[pk], signal=False)
                        for s in range(Q):
                            kb = (r - s) * 2 if s < r else ((s - r) * 2 + 1 if s > r else 2 * Q)
                            K.op('pe', lambda e, ps=ps, kb=kb, s=s, c_lo=c_lo: e.matmul(ps[:, 0:BW], lhsT=B["KB"][:, kb, :],
                                                                           rhs=B["u"][:, s + Q * c_lo: s + Q * (c_lo + BW - 1) + 1: Q], start=False, stop=(s == Q - 1)),
                                 reads=['B.KB', 'B.u'], writes=[pk], signal=(s == Q - 1))
                        K.op('act', lambda e, ps=ps, r=r, c_lo=c_lo: e.activation(out=B["ya"][:, r + Q * c_lo: r + Q * (c_lo + BW - 1) + 1: Q], in_=ps[:, 0:BW], func=AF.Gelu),
                             reads=[pk], writes=['B.ya'])
                K.dma('act', ya_s[gt * 128:(gt + 1) * 128, :], B["ya"], 'B.ya', reads=['B.ya'], writes=['ya_s'])

            Cc = carve([("xp", [T + 3], BF16), ("xc", [T], F32), ("xcb", [T], BF16), ("hf", [T], BF16), ("mk", [T], BF16),
                        ("wst", [4, 128], F32), ("wg", [4, 128], BF16),
                        ("r", [CH], F32), ("gi", [CH], F32), ("a", [CH], F32), ("e2", [CH], F32), ("b", [CH], F32),
                        ("h0", [CH], F32), ("h1", [CH], F32), ("gg", [CH], BF16), ("yb", [CH], BF16), ("mf", [512], F32)])
            K.op('pool', lambda e: e.memset(Cc["xp"], 0.0), writes=['C.xp'])
            K.op('pool', lambda e: e.memset(Cc["wst"], 0.0), writes=['C.wst'])
            for tt in range(NT):
                K.dma('sp', Cc["mf"], maskd[:, 1 + tt * 512: 1 + (tt + 1) * 512], 'C.mf', writes=['C.mf'])
                K.op('act', lambda e, tt=tt: e.copy(out=Cc["mk"][:, tt * 512:(tt + 1) * 512], in_=Cc["mf"]), reads=['C.mf'], writes=['C.mk'])
            for ct in range(8):
                K.dma('sp', Cc["xp"][:, 2:2 + T], xr_s[ct * 128:(ct + 1) * 128, :], 'C.xp', reads=['xr_s'], writes=['C.xp'])
                for d in range(2):
                    for wi, nm in enumerate(("lru_w_a", "lru_w_x")):
                        m = d * 2 + wi
                        for hb in range(2):
                            K.dma('sp', Cc["wst"][hb * 64:(hb + 1) * 64, m, hb * 64:(hb + 1) * 64], W[nm][l, d, 2 * ct + hb], 'C.wst', writes=['C.wst'])
                K.op('act', lambda e: e.copy(out=Cc["wg"], in_=Cc["wst"]), reads=['C.wst'], writes=['C.wg'])
                for cc in range(NCH):
                    sl = slice(cc * CH, (cc + 1) * CH)
                    eng = 'dve'
                    K.op(eng, lambda e, cc=cc, sl=sl: e.tensor_scalar(out=Cc["xc"][:, sl], in0=Cc["xp"][:, cc * CH: cc * CH + CH], scalar1=cwc[:, l, 0, ct:ct + 1],
                                                            scalar2=cbc[:, l, ct:ct + 1], op0=ALU.mult, op1=ALU.add),
                         reads=['C.xp', 'cwc', 'cbc'], writes=['C.xc%d' % cc])
                    for k in range(1, 4):
                        K.op(eng, lambda e, cc=cc, sl=sl, k=k: e.scalar_tensor_tensor(out=Cc["xc"][:, sl], in0=Cc["xp"][:, cc * CH + k: cc * CH + k + CH],
                                                                            scalar=cwc[:, l, k, ct:ct + 1], in1=Cc["xc"][:, sl], op0=ALU.mult, op1=ALU.add),
                             reads=['C.xp', 'cwc', 'C.xc%d' % cc], writes=['C.xc%d' % cc])
                    K.op(eng, lambda e, sl=sl: e.tensor_tensor(out=Cc["xc"][:, sl], in0=Cc["xc"][:, sl], in1=Cc["mk"][:, sl], op=ALU.mult),
                         reads=['C.xc%d' % cc, 'C.mk'], writes=['C.xc%d' % cc])
                    K.op('act', lambda e, sl=sl: e.copy(out=Cc["xcb"][:, sl], in_=Cc["xc"][:, sl]), reads=['C.xc%d' % cc], writes=['C.xcb%d' % cc])
                hctr = 0
                for d in range(2):
                    order = range(NCH) if d == 0 else range(NCH - 1, -1, -1)
                    prev = None
                    for cc in order:
                        sl = slice(cc * CH, (cc + 1) * CH)
                        for wi, (dstn, bcol) in enumerate((("r", bac), ("gi", bxc))):
                            for sb_ in range(CH // 512):
                                ps, pk = nextps()
                                K.op('pe', lambda e, ps=ps, wi=wi, cc=cc, sb_=sb_: e.matmul(ps[:, :], lhsT=Cc["wg"][:, d * 2 + wi, :],
                                                                                 rhs=Cc["xcb"][:, cc * CH + sb_ * 512: cc * CH + (sb_ + 1) * 512], start=True, stop=True),
                                     reads=['C.wg', 'C.xcb%d' % cc], writes=[pk])
                                K.op('act', lambda e, ps=ps, dstn=dstn, bcol=bcol, sb_=sb_: e.activation(out=Cc[dstn][:, sb_ * 512:(sb_ + 1) * 512], in_=ps[:, :], func=AF.Sigmoid,
                                                                                                bias=bcol[:, l, d, ct:ct + 1], scale=1.0),
                                     reads=[pk, 'bac', 'bxc'], writes=['C.' + dstn])
                        K.op('act', lambda e: e.activation(out=Cc["a"], in_=Cc["r"], func=AF.Exp, scale=sc1[:, l, d, ct:ct + 1]), reads=['C.r', 'sc1'], writes=['C.a'])
                        K.op('act', lambda e: e.activation(out=Cc["e2"], in_=Cc["r"], func=AF.Exp, scale=sc2[:, l, d, ct:ct + 1]), reads=['C.r', 'sc2'], writes=['C.e2'])
                        K.op('act', lambda e: e.activation(out=Cc["e2"], in_=Cc["e2"], func=AF.Sqrt, bias=onec, scale=-1.0), reads=['C.e2', 'cst'], writes=['C.e2'])
                        K.op('pool', lambda e: e.tensor_tensor(out=Cc["b"], in0=Cc["gi"], in1=Cc["e2"], op=ALU.mult), reads=['C.gi', 'C.e2'], writes=['C.b'])
                        K.op('pool', lambda e, sl=sl: e.tensor_tensor(out=Cc["b"], in0=Cc["b"], in1=Cc["xc"][:, sl], op=ALU.mult), reads=['C.b', 'C.xc%d' % cc], writes=['C.b'])
                        hn = "h%d" % (hctr % 2); hctr += 1
                        hk = 'C.' + hn
                        if d == 0:
                            init = 0.0 if prev is None else Cc[prev][:, CH - 1:CH]
                            K.op('dve', lambda e, hn=hn, init=init: e.tensor_tensor_scan(out=Cc[hn], data0=Cc["a"], data1=Cc["b"], initial=init, op0=ALU.mult, op1=ALU.add),
                                 reads=['C.a', 'C.b'] + ([] if prev is None else ['C.' + prev]), writes=[hk])
                            K.op('act', lambda e, hn=hn, sl=sl: e.copy(out=Cc["hf"][:, sl], in_=Cc[hn]), reads=[hk], writes=['C.hf'])
                        else:
                            init = 0.0 if prev is None else Cc[prev][:, 0:1]
                            K.op('dve', lambda e, hn=hn, init=init: e.tensor_tensor_scan(out=Cc[hn][:, ::-1], data0=Cc["a"][:, ::-1], data1=Cc["b"][:, ::-1], initial=init,
                                                                               op0=ALU.mult, op1=ALU.add),
                                 reads=['C.a', 'C.b'] + ([] if prev is None else ['C.' + prev]), writes=[hk])
                            K.dma('sp', Cc["gg"], gg_s[ct * 128:(ct + 1) * 128, sl], 'C.gg', reads=['gg_s'], writes=['C.gg'])
                            K.op('pool', lambda e, hn=hn, sl=sl: e.tensor_tensor(out=Cc["b"], in0=Cc[hn], in1=Cc["hf"][:, sl], op=ALU.add), reads=[hk, 'C.hf'], writes=['C.b'])
                            K.op('pool', lambda e: e.tensor_tensor(out=Cc["yb"], in0=Cc["b"], in1=Cc["gg"], op=ALU.mult), reads=['C.b', 'C.gg'], writes=['C.yb'])
                            K.dma('pool', yb_s[ct * 128:(ct + 1) * 128, sl], Cc["yb"], 'C.yb', reads=['C.yb'], writes=['yb_s'])
                        prev = hn

            E_ = carve([("ya", [8, 512], BF16), ("yb", [8, 512], BF16), ("yg", [8, 512], BF16), ("z", [512], BF16),
                        ("sga", [16, 512], BF16), ("sgb", [16, 512], BF16), ("mg", [16, 512], BF16),
                        ("xt", [16, 512], F32), ("mf", [512], F32), ("t1", [512], F32), ("t2", [512], F32)])
            for tt in range(NT):
                cs = slice(tt * 512, (tt + 1) * 512)
                K.dma('sp', E_["ya"], ya_s[:, cs].rearrange("(j p) c -> p j c", p=128), 'E.ya', reads=['ya_s'], writes=['E.ya'])
                K.dma('sp', E_["yb"], yb_s[:, cs].rearrange("(j p) c -> p j c", p=128), 'E.yb', reads=['yb_s'], writes=['E.yb'])
                K.dma('sp', E_["sga"], sga_s[:, cs].rearrange("(j p) c -> p j c", p=128), 'E.sga', reads=['sga_s'], writes=['E.sga'])
                K.dma('sp', E_["sgb"], sgb_s[:, cs].rearrange("(j p) c -> p j c", p=128), 'E.sgb', reads=['sgb_s'], writes=['E.sgb'])
                K.dma('sp', E_["xt"], xs[:, 1 + tt * 512: 1 + (tt + 1) * 512].rearrange("(kt p) c -> p kt c", p=128), 'E.xt', reads=[xs_k], writes=['E.xt'])
                K.dma('sp', E_["mf"], maskd[:, 1 + tt * 512: 1 + (tt + 1) * 512], 'E.mf', writes=['E.mf'])
                for j in range(8):
                    wt, wk = wload("s5_w_glu", l, j, 8)
                    ps, pk = nextps()
                    for kt in range(8):
                        K.op('pe', lambda e, ps=ps, wt=wt, kt=kt: e.matmul(ps[:, :], lhsT=wt[:, kt, :], rhs=E_["ya"][:, kt, :], start=(kt == 0), stop=(kt == 7)),
                             reads=[wk, 'E.ya'], writes=[pk], signal=(kt == 7))
                    K.op('act', lambda e, ps=ps: e.activation(out=E_["z"], in_=ps[:, :], func=AF.Sigmoid), reads=[pk], writes=['E.z'])
                    K.op('dve', lambda e, j=j: e.tensor_tensor(out=E_["yg"][:, j, :], in0=E_["ya"][:, j, :], in1=E_["z"], op=ALU.mult), reads=['E.ya', 'E.z'], writes=['E.yg'])
                for n in range(16):
                    wa, wak = wload("w_proj_a", l, n, 8)
                    wb_, wbk = wload("w_proj_b", l, n, 8)
                    psa, pka = nextps()
                    for kt in range(8):
                        K.op('pe', lambda e, psa=psa, wa=wa, kt=kt: e.matmul(psa[:, :], lhsT=wa[:, kt, :], rhs=E_["yg"][:, kt, :], start=(kt == 0), stop=(kt == 7)),
                             reads=[wak, 'E.yg'], writes=[pka], signal=(kt == 7))
                    psb_, pkb = nextps()
                    for kt in range(8):
                        K.op('pe', lambda e, psb_=psb_, wb_=wb_, kt=kt: e.matmul(psb_[:, :], lhsT=wb_[:, kt, :], rhs=E_["yb"][:, kt, :], start=(kt == 0), stop=(kt == 7)),
                             reads=[wbk, 'E.yb'], writes=[pkb], signal=(kt == 7))
                    K.op('dve', lambda e, psa=psa, n=n: e.tensor_tensor(out=E_["t1"], in0=psa[:, :], in1=E_["sga"][:, n, :], op=ALU.mult), reads=[pka, 'E.sga'], writes=['E.t1'])
                    K.op('dve', lambda e, psb_=psb_, n=n: e.tensor_tensor(out=E_["t2"], in0=psb_[:, :], in1=E_["sgb"][:, n, :], op=ALU.mult), reads=[pkb, 'E.sgb'], writes=['E.t2'])
                    K.op('pool', lambda e, n=n: e.tensor_tensor(out=E_["mg"][:, n, :], in0=E_["t1"], in1=E_["t2"], op=ALU.add), reads=['E.t1', 'E.t2'], writes=['E.mg'])
                for n in range(16):
                    wt, wk = wload("w_out", l, n, 16)
                    ps, pk = nextps()
                    for kt in range(16):
                        K.op('pe', lambda e, ps=ps, wt=wt, kt=kt: e.matmul(ps[:, :], lhsT=wt[:, kt, :], rhs=E_["mg"][:, kt, :], start=(kt == 0), stop=(kt == 15)),
                             reads=[wk, 'E.mg'], writes=[pk], signal=(kt == 15))
                    K.op('dve', lambda e, ps=ps: e.tensor_tensor(out=E_["t1"], in0=ps[:, :], in1=E_["mf"], op=ALU.mult), reads=[pk, 'E.mf'], writes=['E.t1'])
                    K.op('pool', lambda e, n=n: e.tensor_tensor(out=E_["xt"][:, n, :], in0=E_["xt"][:, n, :], in1=E_["t1"], op=ALU.add), reads=['E.t1', 'E.xt'], writes=['E.xt'])
                K.dma('pool', xs[:, 1 + tt * 512: 1 + (tt + 1) * 512].rearrange("(kt p) c -> p kt c", p=128), E_["xt"], 'E.xt', reads=['E.xt'], writes=[xs_k])

            Fh = carve([("xt", [16, 514], F32), ("tmp", [16, 514], BF16), ("rs", [514], F32), ("xn", [16, 514], BF16),
                        ("act", [48, 512], BF16), ("hg", [514], F32), ("hv", [514], F32), ("cg", [512], F32), ("cv", [512], F32),
                        ("wdn0", [48, 128], BF16), ("wdn1", [48, 128], BF16), ("mf", [512], F32), ("t1", [512], F32)])
            wdn = [Fh["wdn0"], Fh["wdn1"]]
            for tt in range(NT):
                c0 = tt * 512
                K.dma('sp', Fh["xt"], xs[:, c0:c0 + 514].rearrange("(kt p) c -> p kt c", p=128), 'F.xt', reads=[xs_k], writes=['F.xt'])
                K.dma('sp', Fh["mf"], maskd[:, 1 + tt * 512: 1 + (tt + 1) * 512], 'F.mf', writes=['F.mf'])
                rmsnorm(Fh["xt"], 'F.xt', 514, g2c[:, l, :], Fh["xn"], 'F.xn', Fh["tmp"], 'F.tmp', Fh["rs"], 'F.rs')
                for f in range(48):
                    res = []
                    for (jj, hn) in ((f, "hg"), (48 + f, "hv")):
                        wt, wk = wload("ffn_w_up", l, jj, 16)
                        ps, pk = nextps()
                        for kt in range(16):
                            K.op('pe', lambda e, ps=ps, wt=wt, kt=kt: e.matmul(ps[:, :], lhsT=wt[:, kt, :], rhs=Fh["xn"][:, kt, 1:513], start=(kt == 0), stop=(kt == 15)),
                                 reads=[wk, 'F.xn'], writes=[pk], signal=(kt == 15))
                        ps2, pk2 = nextps()
                        for kt in range(16):
                            K.op('pe', lambda e, ps2=ps2, wt=wt, kt=kt: e.matmul(ps2[:, 0:2], lhsT=wt[:, kt, :], rhs=Fh["xn"][:, kt, 0:514:513], start=(kt == 0), stop=(kt == 15)),
                                 reads=[wk, 'F.xn'], writes=[pk2], signal=(kt == 15))
                        K.op('act', lambda e, ps=ps, hn=hn: e.copy(out=Fh[hn][:, 1:513], in_=ps[:, :]), reads=[pk], writes=['F.' + hn])
                        K.op('act', lambda e, ps2=ps2, hn=hn: e.copy(out=Fh[hn][:, 0:514:513], in_=ps2[:, 0:2]), reads=[pk2], writes=['F.' + hn])
                    for (hn, cn, jj, eng) in (("hg", "cg", f, 'dve'), ("hv", "cv", 48 + f, 'dve')):
                        K.op(eng, lambda e, hn=hn, cn=cn, jj=jj: e.tensor_scalar_mul(out=Fh[cn], in0=Fh[hn][:, 0:512], scalar1=fcw[:, l, 0, jj:jj + 1]),
                             reads=['F.' + hn, 'fcw'], writes=['F.' + cn])
                        for k in (1, 2):
                            K.op(eng, lambda e, hn=hn, cn=cn, jj=jj, k=k: e.scalar_tensor_tensor(out=Fh[cn], in0=Fh[hn][:, k:k + 512], scalar=fcw[:, l, k, jj:jj + 1],
                                                                                   in1=Fh[cn], op0=ALU.mult, op1=ALU.add),
                                 reads=['F.' + hn, 'fcw', 'F.' + cn], writes=['F.' + cn])
                    K.op('act', lambda e: e.activation(out=Fh["cg"], in_=Fh["cg"], func=AF.Gelu), reads=['F.cg'], writes=['F.cg'])
                    K.op('dve', lambda e, f=f: e.tensor_tensor(out=Fh["act"][:, f, :], in0=Fh["cg"], in1=Fh["cv"], op=ALU.mult), reads=['F.cg', 'F.cv'], writes=['F.act'])
                for n in range(16):
                    wi = n % 2
                    wdk = 'wdn%d' % wi
                    K.dma('sp', wdn[wi], WT["ffn_w_down"][l, n], wdk, reads=[('wt', l)], writes=[wdk])
                    ps, pk = nextps()
                    for kt in range(48):
                        K.op('pe', lambda e, ps=ps, wi=wi, kt=kt: e.matmul(ps[:, :], lhsT=wdn[wi][:, kt, :], rhs=Fh["act"][:, kt, :], start=(kt == 0), stop=(kt == 47)),
                             reads=[wdk, 'F.act'], writes=[pk], signal=(kt == 47))
                    K.op('dve', lambda e, ps=ps: e.tensor_tensor(out=Fh["t1"], in0=ps[:, :], in1=Fh["mf"], op=ALU.mult), reads=[pk, 'F.mf'], writes=['F.t1'])
                    K.op('pool', lambda e, n=n: e.tensor_tensor(out=Fh["xt"][:, n, 1:513], in0=Fh["xt"][:, n, 1:513], in1=Fh["t1"], op=ALU.add), reads=['F.t1', 'F.xt'], writes=['F.xt'])
                K.dma('pool', xs2[:, 1 + tt * 512: 1 + (tt + 1) * 512].rearrange("(kt p) c -> p kt c", p=128), Fh["xt"][:, :, 1:513], 'F.xt', reads=['F.xt'], writes=[xs2_k])
            xs, xs2 = xs2, xs
            xs_k, xs2_k = xs2_k, xs_k
        G = carve([("xt0", [16, 512], F32), ("xt1", [16, 512], F32), ("tmp", [16, 512], BF16), ("rs", [512], F32),
                   ("o0", [16, 512], F32), ("o1", [16, 512], F32)])
        for tt in range(NT):
            xt = G["xt%d" % (tt % 2)]; xkey = 'G.xt%d' % (tt % 2)
            o = G["o%d" % (tt % 2)]; okey = 'G.o%d' % (tt % 2)
            K.dma('sp', xt, xs[:, 1 + tt * 512: 1 + (tt + 1) * 512].rearrange("(kt p) c -> p kt c", p=128), xkey, reads=[xs_k], writes=[xkey])
            rmsnorm(xt, xkey, 512, gfc[:, :], None, okey, G["tmp"], 'G.tmp', G["rs"], 'G.rs', out_f32=o)
            K.dma('sp', yT[:, tt * 512:(tt + 1) * 512].rearrange("(kt p) c -> p kt c", p=128), o, okey, reads=[okey], writes=['yT'])
        K.finish()
        print("instructions emitted:", K.nins, "sems:", len(K.sem))
    return nc


WNAMES = ["norm1_g", "w_in", "s5_a_re", "s5_a_im", "s5_log_dt", "s5_b_re", "s5_b_im", "s5_c_re", "s5_c_im", "s5_d",
          "s5_w_glu", "lru_conv_w", "lru_conv_b", "lru_w_a", "lru_b_a", "lru_w_x", "lru_b_x", "lru_lambda",
          "w_proj_a", "w_proj_b", "w_out", "norm2_g", "ffn_w_up", "ffn_conv_w", "ffn_w_down", "final_g"]


def run_seqs(seqs, weights, T, L, n_cores, debug=False):
    nc = build(T, L, debug)
    consts = make_consts()
    wd = {k: np.ascontiguousarray(np.asarray(weights[k], np.float32)) for k in WNAMES}
    in_maps = []
    for sq in seqs:
        S = sq.shape[0]
        xT = np.zeros((D, T + 2), np.float32)
        xT[:, 1:1 + S] = sq.T
        mask = np.zeros((128, T + 2), np.float32)
        mask[:, 1:1 + S] = 1.0
        m = {"xT": xT, "mask": mask, "consts": consts}
        m.update(wd)
        in_maps.append(m)
    res = run_bass_kernel_spmd(nc, in_maps, core_ids=list(range(n_cores)))
    if debug:
        return res.results
    outs = []
    for sq, r in zip(seqs, res.results):
        S = sq.shape[0]
        outs.append(np.ascontiguousarray(np.asarray(r["yT"])[:, :S].T).astype(np.float32))
    return outs


def kernel(**inputs):
    xp = np.asarray(inputs["x_prompt"], np.float32)
    xsm = np.asarray(inputs["x_sample"], np.float32)
    seqs = [xp[0]] + [xsm[i] for i in range(4)] + [xsm[3]] * 3
    outs = run_seqs(seqs, inputs, 8192, DEPTH, 8)
    y_prompt = outs[0][None]
    y_sample = np.stack(outs[1:5], axis=0)
    return (y_prompt, y_sample)
```

```python
import math
from contextlib import ExitStack
import numpy as np
import ml_dtypes
import concourse.bass as bass
import concourse.mybir as mybir
from concourse.bass_utils import run_bass_kernel_spmd

F32, BF16 = mybir.dt.float32, mybir.dt.bfloat16
AF, ALU = mybir.ActivationFunctionType, mybir.AluOpType

D = 2048; DEPTH = 4; S5W = 1024; LRW = 1024; FH = 6144; INC = 7168
Q = 8
EPS = 1e-6
ARENA = 168 * 1024
C_ID, C_SW, C_BD, C_RM, C_SGN, C_NSG, C_ONE, C_NPI, C_DEPS, C_CM = 0, 128, 256, 384, 392, 393, 394, 395, 396, 400
NCONST = 400 + 1024


def make_consts():
    c = np.zeros((128, NCONST), np.float32)
    k = np.arange(128)
    c[k, C_ID + k] = 1.0
    c[k, C_SW + (k + 64) % 128] = 1.0
    c[:, C_BD:C_BD + 128] = (k[:, None] // 16 == k[None, :] // 16)
    c[:, C_RM:C_RM + 8] = (k[:, None] // 16 == np.arange(8)[None, :])
    c[:, C_SGN] = np.where(k < 64, -1.0, 1.0)
    c[:, C_NSG] = np.where(k < 64, 1.0, -1.0)
    c[:, C_ONE] = 1.0
    c[:, C_NPI] = -math.pi
    c[:, C_DEPS] = D * EPS
    for g in range(8):
        c[:, C_CM + g * 128:C_CM + (g + 1) * 128] = (k[None, :] // 16 == g)
    return c


class Ctx:
    def __init__(s, nc, stack):
        s.nc, s.stack = nc, stack
        s.eng = {'pe': nc.tensor, 'act': nc.scalar, 'dve': nc.vector, 'pool': nc.gpsimd, 'sp': nc.sync}
        s.sem, s.cnt, s.st = {}, {}, {}
        s.seen = {e: {} for e in s.eng}
        s.nins = 0

    def getsem(s, key):
        if key not in s.sem:
            s.sem[key] = s.stack.enter_context(s.nc.semaphore("sm%d" % len(s.sem)))
            s.cnt[key] = 0
        return s.sem[key]

    def _wait(s, e, key, val):
        if e == 'pe' and key == 'pe':
            return
        if s.seen[e].get(key, 0) >= val:
            return
        s.eng[e].wait_ge(s.sem[key], val)
        s.seen[e][key] = val

    def _deps(s, e, reads, writes):
        for k in reads:
            for key, val in s.st.setdefault(k, ({}, {}))[0].items():
                s._wait(e, key, val)
        for k in writes:
            w, r = s.st.setdefault(k, ({}, {}))
            for key, val in w.items():
                s._wait(e, key, val)
            for key, val in r.items():
                s._wait(e, key, val)

    def _mark(s, reads, writes, key, val):
        for k in reads:
            r = s.st[k][1]
            r[key] = max(r.get(key, 0), val)
        for k in writes:
            w = s.st[k][0]
            w[key] = max(w.get(key, 0), val)

    def op(s, e, fn, reads=(), writes=(), signal=True):
        s._deps(e, reads, writes)
        ins = fn(s.eng[e])
        s.getsem(e)
        if signal:
            s.cnt[e] += 1
            ins.then_inc(s.sem[e], 1)
            val = s.cnt[e]
        else:
            val = s.cnt[e] + 1
        s._mark(reads, writes, e, val)
        s.nins += 1

    def dma(s, q, out, in_, semkey, reads=(), writes=(), **kw):
        s._deps(q, reads, writes)
        s.getsem(semkey)
        ins = s.eng[q].dma_start(out=out, in_=in_, **kw)
        s.cnt[semkey] += 16
        ins.then_inc(s.sem[semkey], 16)
        s._mark(reads, writes, semkey, s.cnt[semkey])
        s.nins += 1

    def barrier(s):
        for e in s.eng:
            for key, val in s.cnt.items():
                if val > 0:
                    s._wait(e, key, val)

    def finish(s):
        for key, val in s.cnt.items():
            if val > 0:
                s._wait('sp', key, val)


def build(T, L, debug=False):
    assert T % 512 == 0
    NT = T // 512
    NC_ = T // Q
    BW = min(512, NC_)
    NB = NC_ // BW
    NL = int(round(math.log2(NC_)))
    assert 2 ** NL == NC_
    HP = max(BW, NC_ // 2)
    CH = min(2048, T)
    NCH = T // CH
    nc = bass.Bass("TRN2", target_bir_lowering=False)
    def dr(name, shape, dt, kind="ExternalInput"):
        if debug and kind == "Internal" and not name.endswith("_t"):
            kind = "ExternalOutput"
        return nc.dram_tensor(name, list(shape), dt, kind=kind).ap()
    xT = dr("xT", [D, T + 2], F32)
    maskd = dr("mask", [128, T + 2], F32)
    constd = dr("consts", [128, NCONST], F32)
    W = {}
    for name, shape in [("norm1_g", [L, D]), ("w_in", [L, D, INC]), ("s5_a_re", [L, 2, 64, 64]), ("s5_a_im", [L, 2, 64, 64]),
                        ("s5_log_dt", [L, 2, 64]), ("s5_b_re", [L, 2, 64, 64, 16]), ("s5_b_im", [L, 2, 64, 64, 16]),
                        ("s5_c_re", [L, 2, 64, 16, 64]), ("s5_c_im", [L, 2, 64, 16, 64]), ("s5_d", [L, S5W]),
                        ("s5_w_glu", [L, S5W, S5W]), ("lru_conv_w", [L, 4, LRW]), ("lru_conv_b", [L, LRW]),
                        ("lru_w_a", [L, 2, 16, 64, 64]), ("lru_b_a", [L, 2, LRW]), ("lru_w_x", [L, 2, 16, 64, 64]),
                        ("lru_b_x", [L, 2, LRW]), ("lru_lambda", [L, 2, LRW]), ("w_proj_a", [L, S5W, D]),
                        ("w_proj_b", [L, LRW, D]), ("w_out", [L, D, D]), ("norm2_g", [L, D]), ("ffn_w_up", [L, D, 2 * FH]),
                        ("ffn_conv_w", [L, 3, 2 * FH]), ("ffn_w_down", [L, FH, D]), ("final_g", [D])]:
        W[name] = dr(name, shape, F32)
    yT = dr("yT", [D, T], F32, kind="ExternalOutput")
    xs = dr("x_scr", [D, T + 2], F32, "Internal")
    xs2 = dr("x_scr2", [D, T + 2], F32, "Internal")
    xs_k, xs2_k = 'xs', 'xs2'
    u_s = dr("u_scr", [S5W, T], BF16, "Internal")
    xr_s = dr("xr_scr", [LRW, T], BF16, "Internal")
    gg_s = dr("gg_scr", [LRW, T], BF16, "Internal")
    sga_s = dr("sga_scr", [D, T], BF16, "Internal")
    sgb_s = dr("sgb_scr", [D, T], BF16, "Internal")
    ya_s = dr("ya_scr", [S5W, T], BF16, "Internal")
    yb_s = dr("yb_scr", [LRW, T], BF16, "Internal")
    WT = {}
    for name, (J, KT) in {"w_in": (56, 16), "s5_w_glu": (8, 8), "w_proj_a": (16, 8), "w_proj_b": (16, 8),
                          "w_out": (16, 16), "ffn_w_up": (96, 16), "ffn_w_down": (16, 48)}.items():
        WT[name] = dr(name + "_t", [L, J, 128, KT, 128], BF16, "Internal")

    with ExitStack() as stack:
        sb = lambda name, shape, dt: stack.enter_context(nc.sbuf_tensor(name, list(shape), dt))
        K = Ctx(nc, stack)
        cst = sb("cst", [128, NCONST], F32)
        idb = sb("idb", [128, 128], BF16); swb = sb("swb", [128, 128], BF16); oneb = sb("oneb", [128, 128], BF16)
        g1c = sb("g1c", [128, L, 16], F32); g2c = sb("g2c", [128, L, 16], F32); gfc = sb("gfc", [128, 16], F32)
        s5dc = sb("s5dc", [128, L, 8], F32)
        cwc = sb("cwc", [128, L, 4, 8], F32); cbc = sb("cbc", [128, L, 8], F32)
        bac = sb("bac", [128, L, 2, 8], F32); bxc = sb("bxc", [128, L, 2, 8], F32)
        lamc = sb("lamc", [128, L, 2, 8], F32); sc1 = sb("sc1", [128, L, 2, 8], F32); sc2 = sb("sc2", [128, L, 2, 8], F32)
        fcw = sb("fcw", [128, L, 3, 96], F32)
        NWS = 6
        wslot = [sb("wslot%d" % i, [128, 16, 128], BF16) for i in range(NWS)]
        big = sb("big", [128, ARENA // 4], F32)
        psb = [stack.enter_context(nc.psum_tensor("ps%d" % i, [128, 512], F32)) for i in range(8)]
        psk = ["ps%d" % i for i in range(8)]

        def carve(specs):
            K.barrier()
            off = 0
            out = {}
            for name, shape, dt in specs:
                n = int(np.prod(shape))
                nb = n * (4 if dt == F32 else 2)
                nb4 = (nb + 3) // 4
                ap = big[:, off:off + nb4]
                if dt != F32:
                    ap = ap.bitcast(BF16)[:, 0:n]
                if len(shape) > 1:
                    names = ["d%d" % i for i in range(len(shape))]
                    ap = ap.rearrange("p (%s) -> p %s" % (" ".join(names), " ".join(names)),
                                      **{names[i]: shape[i] for i in range(len(shape) - 1)})
                out[name] = ap
                off += nb4
            assert off * 4 <= ARENA, off * 4
            return out

        K.dma('sp', cst[:], constd[:, :], 'cst', writes=['cst'])
        K.op('act', lambda e: e.copy(out=idb[:], in_=cst[:, C_ID:C_ID + 128]), reads=['cst'], writes=['idb'])
        K.op('act', lambda e: e.copy(out=swb[:], in_=cst[:, C_SW:C_SW + 128]), reads=['cst'], writes=['swb'])
        K.op('dve', lambda e: e.memset(oneb[:], 1.0), writes=['oneb'])
        onec = cst[:, C_ONE:C_ONE + 1]
        sgnc = cst[:, C_SGN:C_SGN + 1]
        nsgc = cst[:, C_NSG:C_NSG + 1]
        npic = cst[:, C_NPI:C_NPI + 1]

        def ldsmall(dst, src, key):
            K.dma('sp', dst, src, key, writes=[key], allow_slow_non_contiguous=True)
        for l in range(L):
            ldsmall(g1c[:, l, :], W["norm1_g"][l].rearrange("(kt p) -> p kt", p=128), 'g1c')
            ldsmall(g2c[:, l, :], W["norm2_g"][l].rearrange("(kt p) -> p kt", p=128), 'g2c')
            ldsmall(s5dc[:, l, :], W["s5_d"][l].rearrange("(kt p) -> p kt", p=128), 's5dc')
            ldsmall(cbc[:, l, :], W["lru_conv_b"][l].rearrange("(kt p) -> p kt", p=128), 'cbc')
            for k in range(4):
                ldsmall(cwc[:, l, k, :], W["lru_conv_w"][l, k].rearrange("(kt p) -> p kt", p=128), 'cwc')
            for d in range(2):
                ldsmall(bac[:, l, d, :], W["lru_b_a"][l, d].rearrange("(kt p) -> p kt", p=128), 'bac')
                ldsmall(bxc[:, l, d, :], W["lru_b_x"][l, d].rearrange("(kt p) -> p kt", p=128), 'bxc')
                ldsmall(lamc[:, l, d, :], W["lru_lambda"][l, d].rearrange("(kt p) -> p kt", p=128), 'lamc')
            for k in range(3):
                ldsmall(fcw[:, l, k, :], W["ffn_conv_w"][l, k].rearrange("(kt p) -> p kt", p=128), 'fcw')
        ldsmall(gfc[:], W["final_g"].rearrange("(kt p) -> p kt", p=128), 'gfc')
        sD = math.sqrt(D)
        K.op('act', lambda e: e.mul(out=g1c[:], in_=g1c[:], mul=sD), reads=['g1c'], writes=['g1c'])
        K.op('act', lambda e: e.mul(out=g2c[:], in_=g2c[:], mul=sD), reads=['g2c'], writes=['g2c'])
        K.op('act', lambda e: e.mul(out=gfc[:], in_=gfc[:], mul=sD), reads=['gfc'], writes=['gfc'])
        K.op('act', lambda e: e.activation(out=sc1[:], in_=lamc[:], func=AF.Exp, scale=-1.0), reads=['lamc'], writes=['sc1'])
        K.op('act', lambda e: e.activation(out=sc1[:], in_=sc1[:], func=AF.Ln, bias=onec, scale=1.0), reads=['sc1', 'cst'], writes=['sc1'])
        K.op('act', lambda e: e.mul(out=sc2[:], in_=sc1[:], mul=-16.0), reads=['sc1'], writes=['sc2'])
        K.op('act', lambda e: e.mul(out=sc1[:], in_=sc1[:], mul=-8.0), reads=['sc1', 'sc2'], writes=['sc1'])

        K.dma('sp', xs[:, :], xT[:, :], 'xcopy', writes=['xs'])
        K.dma('sp', xs2[:, :], xT[:, :], 'xcopy', writes=['xs2'])
        CONV = [("w_in", 56, 16), ("s5_w_glu", 8, 8), ("w_proj_a", 16, 8), ("w_proj_b", 16, 8), ("w_out", 16, 16),
                ("ffn_w_up", 96, 16), ("ffn_w_down", 16, 48)]

        def conv_layer(l):
            for name, J, KT in CONV:
                for j in range(J):
                    src = W[name][l][:, j * 128:(j + 1) * 128].rearrange("(kt p) n -> p kt n", p=128)
                    K.dma('pool', WT[name][l, j], src, 'wconv_%s_%d' % (name, l), writes=[('wt', name, l)])
        for _l in range(L):
            conv_layer(_l)

        wctr = [0]

        def wload(name, l, j, KT):
            i = wctr[0] % NWS
            wctr[0] += 1
            key = 'wslot%d' % i
            K.dma('sp', wslot[i][:, 0:KT, :], WT[name][l, j], key, reads=[('wt', name, l)], writes=[key])
            return wslot[i], key

        pctr = [0]

        def nextps():
            i = pctr[0] % 8
            pctr[0] += 1
            return psb[i], psk[i]

        def rmsnorm(xt, xkey, ncols, gcol, xn, xnkey, tmp, tmpkey, rs, rskey, out_f32=None):
            K.op('act', lambda e: e.activation(out=tmp[:, :, 0:ncols], in_=xt[:, :, 0:ncols], func=AF.Square),
                 reads=[xkey], writes=[tmpkey])
            c0 = 0
            while c0 < ncols:
                cw = min(512, ncols - c0)
                ps, pk = nextps()
                for kt in range(16):
                    K.op('pe', lambda e, kt=kt, c0=c0, cw=cw, ps=ps: e.matmul(ps[:, 0:cw], lhsT=oneb[:], rhs=tmp[:, kt, c0:c0 + cw],
                                                                   start=(kt == 0), stop=(kt == 15)),
                         reads=[tmpkey, 'oneb'], writes=[pk], signal=(kt == 15))
                K.op('act', lambda e, c0=c0, cw=cw, ps=ps: e.activation(out=rs[:, c0:c0 + cw], in_=ps[:, 0:cw], func=AF.Ln,
                                                                 bias=cst[:, C_DEPS:C_DEPS + 1], scale=1.0),
                     reads=[pk, 'cst'], writes=[rskey])
                K.op('act', lambda e, c0=c0, cw=cw: e.activation(out=rs[:, c0:c0 + cw], in_=rs[:, c0:c0 + cw], func=AF.Exp, scale=-0.5),
                     reads=[rskey], writes=[rskey])
                c0 += cw
            for kt in range(16):
                eng = 'dve'
                dst = xn if out_f32 is None else out_f32
                K.op(eng, lambda e, kt=kt, dst=dst: e.scalar_tensor_tensor(out=dst[:, kt, 0:ncols], in0=xt[:, kt, 0:ncols], scalar=gcol[:, kt:kt + 1],
                                                                  in1=rs[:, 0:ncols], op0=ALU.mult, op1=ALU.mult),
                     reads=[xkey, rskey, 'g1c', 'g2c', 'gfc'], writes=[xnkey])

        for l in range(L):
            A = carve([("xt0", [16, 512], F32), ("xt1", [16, 512], F32), ("tmp", [16, 512], BF16), ("rs", [512], F32),
                       ("xn", [16, 512], BF16), ("og0", [8, 512], BF16), ("og1", [8, 512], BF16)])
            for tt in range(NT):
                c0 = 1 + tt * 512
                xt = A["xt%d" % (tt % 2)]; xkey = 'A.xt%d' % (tt % 2)
                K.dma('sp', xt, xs[:, c0:c0 + 512].rearrange("(kt p) c -> p kt c", p=128), xkey, reads=[xs_k], writes=[xkey])
                rmsnorm(xt, xkey, 512, g1c[:, l, :], A["xn"], 'A.xn', A["tmp"], 'A.tmp', A["rs"], 'A.rs')
                for jg in range(7):
                    og = A["og%d" % (jg % 2)]; ogk = 'A.og%d' % (jg % 2)
                    for jj in range(8):
                        j = jg * 8 + jj
                        wt, wk = wload("w_in", l, j, 16)
                        ps, pk = nextps()
                        for kt in range(16):
                            K.op('pe', lambda e, kt=kt, wt=wt, ps=ps: e.matmul(ps[:, :], lhsT=wt[:, kt, :], rhs=A["xn"][:, kt, :],
                                                                      start=(kt == 0), stop=(kt == 15)),
                                 reads=[wk, 'A.xn'], writes=[pk], signal=(kt == 15))
                        func = AF.Copy if jg < 2 else (AF.Gelu if jg == 2 else AF.Sigmoid)
                        K.op('act', lambda e, jj=jj, og=og, ps=ps, func=func: e.activation(out=og[:, jj, :], in_=ps[:, :], func=func),
                             reads=[pk], writes=[ogk])
                    dst = [u_s, xr_s, gg_s, sga_s[0:1024], sga_s[1024:2048], sgb_s[0:1024], sgb_s[1024:2048]][jg]
                    dkey = ['u_s', 'xr_s', 'gg_s', 'sga_s', 'sga_s', 'sgb_s', 'sgb_s'][jg]
                    K.dma('act', dst[:, tt * 512:(tt + 1) * 512].rearrange("(j p) c -> p j c", p=128), og, ogk, reads=[ogk], writes=[dkey])

            NS0 = 4
            B = carve([("u", [T], BF16), ("ya", [T], BF16),
                       ("H", [20, HP + NC_], BF16),
                       ("CAL", [8 * Q * 2, 128], BF16),
                       ("S0L", [NS0, Q, 128], BF16), ("RP", [8, 128], BF16), ("RT", [8, 128], BF16), ("KB", [2 * Q + 1, 128], BF16),
                       ("E", [2, Q, 128], BF16), ("CA", [2, Q + 1, 128], BF16),
                       ("b1", [128], F32), ("b2", [128], F32), ("c1", [128], F32), ("c2", [128], F32),
                       ("nat", [128], F32), ("nat2", [128], F32),
                       ("sm", [40, 8], F32), ("pw", [2, Q + 1, 2, 8], F32), ("zc", [2, Q, 2, 8], F32), ("rp", [2, NL, 2, 8], F32),
                       ("t1", [128], F32), ("t2", [128], F32)])
            K.op('pool', lambda e: e.memset(B["H"][:, :, :], 0.0), writes=['B.H%d' % i for i in range(20)])
            SM = lambda i: B["sm"][:, i, :]
            PE_ = 'dve'

            def tt_(out, a, b, op, rk, wk):
                K.op(PE_, lambda e: e.tensor_tensor(out=out, in0=a, in1=b, op=op), reads=rk, writes=wk)

            def cmul(ore, oim, are, aim, bre, bim, rk, wk):
                tt_(SM(38), are, bre, ALU.mult, rk, ['B.t38'])
                tt_(SM(39), aim, bim, ALU.mult, rk, ['B.t39'])
                tt_(SM(37), are, bim, ALU.mult, rk, ['B.t37'])
                tt_(SM(36), aim, bre, ALU.mult, rk, ['B.t36'])
                tt_(ore, SM(38), SM(39), ALU.subtract, ['B.t38', 'B.t39'], wk)
                tt_(oim, SM(37), SM(36), ALU.add, ['B.t37', 'B.t36'], wk)

            rpctr = 0
            for gt in range(8):
                g0 = gt * 8
                K.dma('sp', B["u"], u_s[gt * 128:(gt + 1) * 128, :], 'B.u', reads=['u_s'], writes=['B.u'])
                for d in range(2):
                    for nm, dst in (("s5_a_re", 0), ("s5_a_im", 1)):
                        for h in range(2):
                            K.dma('sp', B["nat"][0:8, h * 64:(h + 1) * 64], W[nm][l, d, g0:g0 + 8, :], 'B.nat', writes=['B.nat'])
                        ps, pk = nextps()
                        K.op('pe', lambda e, ps=ps: e.transpose(out=ps[:, 0:8], in_=B["nat"][0:8, :], identity=cst[0:8, C_ID:C_ID + 8]),
                             reads=['B.nat', 'cst'], writes=[pk])
                        K.op('act', lambda e, ps=ps, dst=dst: e.copy(out=SM(dst), in_=ps[:, 0:8]), reads=[pk], writes=['B.sm%d' % dst])
                    K.dma('sp', SM(2), W["s5_log_dt"][l, d, g0:g0 + 8].partition_broadcast(128), 'B.sm2', writes=['B.sm2'])
                    K.op('act', lambda e: e.activation(out=SM(2), in_=SM(2), func=AF.Exp), reads=['B.sm2'], writes=['B.sm2'])
                    k0, k1, k2 = ['B.sm0'], ['B.sm1'], ['B.sm2']
                    tt_(SM(3), SM(0), SM(2), ALU.mult, k0 + k2, ['B.sm3'])
                    tt_(SM(4), SM(1), SM(2), ALU.mult, k1 + k2, ['B.sm4'])
                    K.op('act', lambda e: e.activation(out=SM(5), in_=SM(3), func=AF.Exp), reads=['B.sm3'], writes=['B.sm5'])
                    K.op('act', lambda e: e.activation(out=SM(6), in_=SM(4), func=AF.Sin, scale=1.0 / 16), reads=['B.sm4'], writes=['B.sm6'])
                    K.op('act', lambda e: e.activation(out=SM(7), in_=SM(4), func=AF.Sin, scale=1.0 / 32), reads=['B.sm4'], writes=['B.sm7'])
                    tt_(SM(7), SM(7), SM(7), ALU.mult, ['B.sm7'], ['B.sm7'])
                    K.op(PE_, lambda e: e.tensor_scalar(out=SM(7), in0=SM(7), scalar1=-2.0, scalar2=1.0, op0=ALU.mult, op1=ALU.add),
                         reads=['B.sm7'], writes=['B.sm7'])
                    for _sq in range(4):
                        cmul(SM(7), SM(6), SM(7), SM(6), SM(7), SM(6), ['B.sm7', 'B.sm6'], ['B.sm7', 'B.sm6'])
                    PW = lambda k, c: B["pw"][:, d, k, c, :]
                    pwk = lambda k: 'B.pw%d_%d' % (d, k)
                    tt_(PW(1, 0), SM(5), SM(7), ALU.mult, ['B.sm5', 'B.sm7'], [pwk(1)])
                    tt_(PW(1, 1), SM(5), SM(6), ALU.mult, ['B.sm5', 'B.sm6'], [pwk(1)])
                    K.op(PE_, lambda e: e.memset(PW(0, 0), 1.0), writes=[pwk(0)])
                    K.op(PE_, lambda e: e.memset(PW(0, 1), 0.0), writes=[pwk(0)])
                    K.op(PE_, lambda e: e.tensor_scalar_add(out=SM(8), in0=PW(1, 0), scalar1=-1.0), reads=[pwk(1)], writes=['B.sm8'])
                    tt_(SM(9), SM(0), SM(0), ALU.mult, k0, ['B.sm9'])
                    tt_(SM(10), SM(1), SM(1), ALU.mult, k1, ['B.sm10'])
                    tt_(SM(9), SM(9), SM(10), ALU.add, ['B.sm9', 'B.sm10'], ['B.sm9'])
                    K.op(PE_, lambda e: e.reciprocal(out=SM(9), in_=SM(9)), reads=['B.sm9'], writes=['B.sm9'])
                    tt_(SM(11), SM(0), SM(9), ALU.mult, k0 + ['B.sm9'], ['B.sm11'])
                    tt_(SM(12), SM(1), SM(9), ALU.mult, k1 + ['B.sm9'], ['B.sm12'])
                    K.op(PE_, lambda e: e.tensor_scalar_mul(out=SM(12), in0=SM(12), scalar1=-1.0), reads=['B.sm12'], writes=['B.sm12'])
                    cmul(SM(13), SM(14), SM(8), PW(1, 1), SM(11), SM(12), ['B.sm8', pwk(1), 'B.sm11', 'B.sm12'], ['B.coef'])
                    for k in range(2, Q + 1):
                        cmul(PW(k, 0), PW(k, 1), PW(k - 1, 0), PW(k - 1, 1), PW(1, 0), PW(1, 1), [pwk(k - 1), pwk(1)], [pwk(k)])
                    ZC = lambda k, c: B["zc"][:, d, k, c, :]
                    for k in range(Q):
                        cmul(ZC(k, 0), ZC(k, 1), PW(k, 0), PW(k, 1), SM(13), SM(14), [pwk(k), 'B.coef'], ['B.zc%d' % d])
                        K.op(PE_, lambda e, k=k: e.tensor_scalar_mul(out=ZC(k, 1), in0=ZC(k, 1), scalar1=sgnc), reads=['B.zc%d' % d, 'cst'], writes=['B.zc%d' % d])
                    RPc = lambda lv, c: B["rp"][:, d, lv, c, :]
                    K.op(PE_, lambda e: e.tensor_copy(out=RPc(0, 0), in_=PW(Q, 0)), reads=[pwk(Q)], writes=['B.rp%d' % d])
                    K.op(PE_, lambda e: e.tensor_copy(out=RPc(0, 1), in_=PW(Q, 1)), reads=[pwk(Q)], writes=['B.rp%d' % d])
                    for lv in range(1, NL):
                        cmul(RPc(lv, 0), RPc(lv, 1), RPc(lv - 1, 0), RPc(lv - 1, 1), RPc(lv - 1, 0), RPc(lv - 1, 1), ['B.rp%d' % d], ['B.rp%d' % d])
                    K.op(PE_, lambda e: e.tensor_scalar_mul(out=B["rp"][:, d, :, 1, :], in0=B["rp"][:, d, :, 1, :], scalar1=nsgc),
                         reads=['B.rp%d' % d, 'cst'], writes=['B.rp%d' % d])
                    bre = W["s5_b_re"][l, d, g0:g0 + 8].rearrange("g p j -> p g j")
                    bim = W["s5_b_im"][l, d, g0:g0 + 8].rearrange("g p j -> p g j")
                    v3 = lambda ap: ap.rearrange("p (g j) -> p g j", g=8)
                    K.dma('sp', v3(B["b1"])[0:64], bre, 'B.b1', writes=['B.b1'])
                    K.dma('sp', v3(B["b1"])[64:128], bim, 'B.b1', writes=['B.b1'])
                    K.dma('sp', v3(B["b2"])[0:64], bim, 'B.b2', writes=['B.b2'])
                    K.dma('sp', v3(B["b2"])[64:128], bre, 'B.b2', writes=['B.b2'])
                    cre = W["s5_c_re"][l, d, g0:g0 + 8].rearrange("g i p -> (g i) p")
                    cim = W["s5_c_im"][l, d, g0:g0 + 8].rearrange("g i p -> (g i) p")
                    for (first, second, dst, nat, nk) in ((cre, cim, "c1", "nat", 'B.nat'), (cim, cre, "c2", "nat2", 'B.nat2')):
                        K.dma('sp', B[nat][:, 0:64], first, nk, writes=[nk])
                        K.dma('sp', B[nat][:, 64:128], second, nk, writes=[nk])
                        ps, pk = nextps()
                        K.op('pe', lambda e, ps=ps, nat=nat: e.transpose(out=ps[:, 0:128], in_=B[nat][:, :], identity=cst[:, C_ID:C_ID + 128]),
                             reads=[nk, 'cst'], writes=[pk])
                        K.op('act', lambda e, ps=ps, dst=dst: e.copy(out=B[dst], in_=ps[:, 0:128]), reads=[pk], writes=['B.' + dst])
                    bc = lambda col: col.unsqueeze(2).to_broadcast([128, 8, 16])
                    for k in range(Q):
                        tt_(v3(B["t1"]), v3(B["b1"]), bc(ZC(k, 0)), ALU.mult, ['B.b1', 'B.zc%d' % d], ['B.t1'])
                        tt_(v3(B["t2"]), v3(B["b2"]), bc(ZC(k, 1)), ALU.mult, ['B.b2', 'B.zc%d' % d], ['B.t2'])
                        tt_(B["E"][:, d, k, :], B["t1"], B["t2"], ALU.add, ['B.t1', 'B.t2'], ['B.E%d' % d])
                    for tau in range(Q + 1):
                        K.op(PE_, lambda e, tau=tau: e.tensor_scalar_mul(out=SM(15), in0=PW(tau, 0), scalar1=nsgc), reads=[pwk(tau), 'cst'], writes=['B.sm15'])
                        tt_(v3(B["t1"]), v3(B["c1"]), bc(SM(15)), ALU.mult, ['B.c1', 'B.sm15'], ['B.t1'])
                        tt_(v3(B["t2"]), v3(B["c2"]), bc(PW(tau, 1)), ALU.mult, ['B.c2', pwk(tau)], ['B.t2'])
                        tt_(B["CA"][:, d, tau, :], B["t1"], B["t2"], ALU.subtract, ['B.t1', 'B.t2'], ['B.CA%d' % d])
                    for tau in range(1, Q + 1):
                        for g in range(8):
                            idx = (g * Q + (tau - 1)) * 2 + d
                            K.op('pool', lambda e, idx=idx, tau=tau, g=g: e.tensor_tensor(out=B["CAL"][:, idx, :], in0=B["CA"][:, d, tau, :],
                                                                                   in1=cst[:, C_CM + g * 128:C_CM + (g + 1) * 128], op=ALU.mult),
                                 reads=['B.CA%d' % d, 'cst'], writes=['B.CAL'])
                    for tau in range(Q):
                        ps, pk = nextps()
                        K.op('pe', lambda e, ps=ps, tau=tau: e.matmul(ps[:, 0:128], lhsT=B["E"][:, d, 0, :], rhs=B["CA"][:, d, tau, :], start=True, stop=True),
                             reads=['B.E%d' % d, 'B.CA%d' % d], writes=[pk])
                        K.op('dve', lambda e, ps=ps, tau=tau: e.tensor_tensor(out=B["KB"][:, tau * 2 + d, :], in0=ps[:, 0:128], in1=cst[:, C_BD:C_BD + 128], op=ALU.mult),
                             reads=[pk, 'cst'], writes=['B.KB'])
                K.op('dve', lambda e: e.tensor_scalar_mul(out=B["t1"], in0=cst[:, C_ID:C_ID + 128], scalar1=s5dc[:, l, gt:gt + 1]), reads=['cst', 's5dc'], writes=['B.t1'])
                tt_(B["t2"], B["KB"][:, 0, :], B["KB"][:, 1, :], ALU.add, ['B.KB'], ['B.t2'])
                tt_(B["KB"][:, 2 * Q, :], B["t1"], B["t2"], ALU.add, ['B.t1', 'B.t2'], ['B.KB'])

                NCHN = 4
                for d in range(2):
                    off = HP if d == 0 else 0
                    padlo = 0 if d == 0 else NC_
                    K.op('pool', lambda e, padlo=padlo: e.memset(B["H"][:, 16:16 + NCHN, padlo:padlo + HP], 0.0),
                         writes=['B.H%d' % (16 + i) for i in range(NCHN)])
                    for gg in range(8 // NCHN):
                        chains = []
                        for gi in range(NCHN):
                            g = gg * NCHN + gi
                            sl = gi
                            slk = 'B.S0L%d' % sl
                            for k in range(Q):
                                ps, pk = nextps()
                                K.op('pe', lambda e, ps=ps, k=k: e.transpose(out=ps[:, 0:128].bitcast(BF16)[:, 0:128], in_=B["E"][:, d, k, :], identity=idb[:]),
                                     reads=['B.E%d' % d, 'idb'], writes=[pk])
                                K.op('dve', lambda e, ps=ps, k=k, sl=sl, g=g: e.tensor_scalar_mul(out=B["S0L"][:, sl, k, :], in0=ps[:, 0:128].bitcast(BF16)[:, 0:128],
                                                                                      scalar1=cst[:, C_RM + g:C_RM + g + 1]),
                                     reads=[pk, 'cst'], writes=[slk])
                            hi = g * 2 + d
                            Hf = B["H"][:, hi, :]; Hfk = 'B.H%d' % hi
                            Hs = B["H"][:, 16 + gi, :]; Hsk = 'B.H%d' % (16 + gi)
                            for blk in range(NB):
                                ps, pk = nextps()
                                for s_ in range(Q):
                                    kk = (Q - 1 - s_) if d == 0 else s_
                                    c_lo = blk * BW
                                    K.op('pe', lambda e, ps=ps, s_=s_, kk=kk, c_lo=c_lo, sl=sl: e.matmul(ps[:, 0:BW], lhsT=B["S0L"][:, sl, kk, :],
                                                                                          rhs=B["u"][:, s_ + Q * c_lo: s_ + Q * (c_lo + BW - 1) + 1: Q],
                                                                                          start=(s_ == 0), stop=(s_ == Q - 1)),
                                         reads=[slk, 'B.u'], writes=[pk], signal=(s_ == Q - 1))
                                K.op('act', lambda e, ps=ps, blk=blk, Hf=Hf: e.copy(out=Hf[:, off + blk * BW: off + (blk + 1) * BW], in_=ps[:, 0:BW]),
                                     reads=[pk], writes=[Hfk])
                            chains.append([g, Hf, Hfk, Hs, Hsk])
                        evc = 0
                        for lv in range(NL):
                            sh = 2 ** lv
                            for chn in chains:
                                g, src, srck, dst, dstk = chn
                                ri = rpctr % 8; rpctr += 1
                                rk = 'B.RP%d' % ri
                                rtk = 'B.RT%d' % ri
                                K.op('pool', lambda e, lv=lv, g=g, ri=ri: e.tensor_scalar_mul(out=B["RT"][:, ri, :], in0=swb[:], scalar1=B["rp"][:, d, lv, 1, g:g + 1]),
                                     reads=['swb', 'B.rp%d' % d], writes=[rtk])
                                K.op('dve', lambda e, lv=lv, g=g, ri=ri: e.scalar_tensor_tensor(out=B["RP"][:, ri, :], in0=idb[:], scalar=B["rp"][:, d, lv, 0, g:g + 1],
                                                                                     in1=B["RT"][:, ri, :], op0=ALU.mult, op1=ALU.add),
                                     reads=['idb', 'B.rp%d' % d, rtk], writes=[rk])
                                for blk in range(NB):
                                    ps, pk = nextps()
                                    a0 = off + blk * BW
                                    sa = a0 - sh if d == 0 else a0 + sh
                                    K.op('pe', lambda e, ps=ps, a0=a0, src=src: e.matmul(ps[:, 0:BW], lhsT=idb[:], rhs=src[:, a0:a0 + BW], start=True, stop=False),
                                         reads=['idb', srck], writes=[pk], signal=False)
                                    K.op('pe', lambda e, ps=ps, sa=sa, src=src, ri=ri: e.matmul(ps[:, 0:BW], lhsT=B["RP"][:, ri, :], rhs=src[:, sa:sa + BW], start=False, stop=True),
                                         reads=[rk, srck], writes=[pk])
                                    evc += 1
                                    if evc % 2 == 0:
                                        K.op('act', lambda e, ps=ps, a0=a0, dst=dst: e.copy(out=dst[:, a0:a0 + BW], in_=ps[:, 0:BW]), reads=[pk], writes=[dstk])
                                    else:
                                        K.op('dve', lambda e, ps=ps, a0=a0, dst=dst: e.tensor_copy(out=dst[:, a0:a0 + BW], in_=ps[:, 0:BW]), reads=[pk], writes=[dstk])
                                chn[1], chn[2], chn[3], chn[4] = dst, dstk, src, srck
                        if NL % 2 == 1:
                            for chn in chains:
                                g, cur, curk, oth, othk = chn
                                K.op('pool', lambda e, cur=cur, oth=oth: e.tensor_copy(out=oth[:, off:off + NC_], in_=cur[:, off:off + NC_]), reads=[curk], writes=[othk])
                for r in range(Q):
                    for blk in range(NB):
                        ps, pk = nextps()
                        c_lo = blk * BW
                        first = True
                        for g in range(8):
                            idx = (g * Q + r) * 2 + 0
                            Hf = B["H"][:, g * 2, :]
                            K.op('pe', lambda e, ps=ps, idx=idx, Hf=Hf, c_lo=c_lo, first=first: e.matmul(ps[:, 0:BW], lhsT=B["CAL"][:, idx, :], rhs=Hf[:, HP + c_lo - 1: HP + c_lo - 1 + BW],
                                                                                          start=first, stop=False),
                                 reads=['B.CAL', 'B.H%d' % (g * 2)], writes=[pk], signal=False)
                            first = False
                            idx = (g * Q + (Q - r - 1)) * 2 + 1
                            Hb = B["H"][:, g * 2 + 1, :]
                            K.op('pe', lambda e, ps=ps, idx=idx, Hb=Hb, c_lo=c_lo: e.matmul(ps[:, 0:BW], lhsT=B["CAL"][:, idx, :], rhs=Hb[:, c_lo + 1: c_lo + 1 + BW],
                                                                              start=False, stop=False),
                                 reads=['B.CAL', 'B.H%d' % (g * 2 + 1)], writes=[pk], signal=False)
                        for s in range(Q):
                            kb = (r - s) * 2 if s < r else ((s - r) * 2 + 1 if s > r else 2 * Q)
                            K.op('pe', lambda e, ps=ps, kb=kb, s=s, c_lo=c_lo: e.matmul(ps[:, 0:BW], lhsT=B["KB"][:, kb, :],
                                                                           rhs=B["u"][:, s + Q * c_lo: s + Q * (c_lo + BW - 1) + 1: Q], start=False, stop=(s == Q - 1)),
                                 reads=['B.KB', 'B.u'], writes=[pk], signal=(s == Q - 1))
                        K.op('act', lambda e, ps=ps, r=r, c_lo=c_lo: e.activation(out=B["ya"][:, r + Q * c_lo: r + Q * (c_lo + BW - 1) + 1: Q], in_=ps[:, 0:BW], func=AF.Gelu),
                             reads=[pk], writes=['B.ya'])
                K.dma('act', ya_s[gt * 128:(gt + 1) * 128, :], B["ya"], 'B.ya', reads=['B.ya'], writes=['ya_s'])

            Cc = carve([("xp", [T + 3], BF16), ("xc", [T], F32), ("xcb", [T], BF16), ("hf", [T], BF16), ("mk", [T], BF16),
                        ("wst", [4, 128], F32), ("wg", [4, 128], BF16),
                        ("r", [CH], F32), ("gi", [CH], F32), ("a", [CH], F32), ("e2", [CH], F32), ("b", [CH], F32),
                        ("h0", [CH], F32), ("h1", [CH], F32), ("gg", [CH], BF16), ("yb", [CH], BF16), ("mf", [512], F32)])
            K.op('pool', lambda e: e.memset(Cc["xp"], 0.0), writes=['C.xp'])
            K.op('pool', lambda e: e.memset(Cc["wst"], 0.0), writes=['C.wst'])
            for tt in range(NT):
                K.dma('sp', Cc["mf"], maskd[:, 1 + tt * 512: 1 + (tt + 1) * 512], 'C.mf', writes=['C.mf'])
                K.op('act', lambda e, tt=tt: e.copy(out=Cc["mk"][:, tt * 512:(tt + 1) * 512], in_=Cc["mf"]), reads=['C.mf'], writes=['C.mk'])
            for ct in range(8):
                K.dma('sp', Cc["xp"][:, 2:2 + T], xr_s[ct * 128:(ct + 1) * 128, :], 'C.xp', reads=['xr_s'], writes=['C.xp'])
                for d in range(2):
                    for wi, nm in enumerate(("lru_w_a", "lru_w_x")):
                        m = d * 2 + wi
                        for hb in range(2):
                            K.dma('sp', Cc["wst"][hb * 64:(hb + 1) * 64, m, hb * 64:(hb + 1) * 64], W[nm][l, d, 2 * ct + hb], 'C.wst', writes=['C.wst'])
                K.op('act', lambda e: e.copy(out=Cc["wg"], in_=Cc["wst"]), reads=['C.wst'], writes=['C.wg'])
                for cc in range(NCH):
                    sl = slice(cc * CH, (cc + 1) * CH)
                    eng = 'dve'
                    K.op(eng, lambda e, cc=cc, sl=sl: e.tensor_scalar(out=Cc["xc"][:, sl], in0=Cc["xp"][:, cc * CH: cc * CH + CH], scalar1=cwc[:, l, 0, ct:ct + 1],
                                                            scalar2=cbc[:, l, ct:ct + 1], op0=ALU.mult, op1=ALU.add),
                         reads=['C.xp', 'cwc', 'cbc'], writes=['C.xc%d' % cc])
                    for k in range(1, 4):
                        K.op(eng, lambda e, cc=cc, sl=sl, k=k: e.scalar_tensor_tensor(out=Cc["xc"][:, sl], in0=Cc["xp"][:, cc * CH + k: cc * CH + k + CH],
                                                                            scalar=cwc[:, l, k, ct:ct + 1], in1=Cc["xc"][:, sl], op0=ALU.mult, op1=ALU.add),
                             reads=['C.xp', 'cwc', 'C.xc%d' % cc], writes=['C.xc%d' % cc])
                    K.op(eng, lambda e, sl=sl: e.tensor_tensor(out=Cc["xc"][:, sl], in0=Cc["xc"][:, sl], in1=Cc["mk"][:, sl], op=ALU.mult),
                         reads=['C.xc%d' % cc, 'C.mk'], writes=['C.xc%d' % cc])
                    K.op('act', lambda e, sl=sl: e.copy(out=Cc["xcb"][:, sl], in_=Cc["xc"][:, sl]), reads=['C.xc%d' % cc], writes=['C.xcb%d' % cc])
                hctr = 0
                for d in range(2):
                    order = range(NCH) if d == 0 else range(NCH - 1, -1, -1)
                    prev = None
                    for cc in order:
                        sl = slice(cc * CH, (cc + 1) * CH)
                        for wi, (dstn, bcol) in enumerate((("r", bac), ("gi", bxc))):
                            for sb_ in range(CH // 512):
                                ps, pk = nextps()
                                K.op('pe', lambda e, ps=ps, wi=wi, cc=cc, sb_=sb_: e.matmul(ps[:, :], lhsT=Cc["wg"][:, d * 2 + wi, :],
                                                                                 rhs=Cc["xcb"][:, cc * CH + sb_ * 512: cc * CH + (sb_ + 1) * 512], start=True, stop=True),
                                     reads=['C.wg', 'C.xcb%d' % cc], writes=[pk])
                                K.op('act', lambda e, ps=ps, dstn=dstn, bcol=bcol, sb_=sb_: e.activation(out=Cc[dstn][:, sb_ * 512:(sb_ + 1) * 512], in_=ps[:, :], func=AF.Sigmoid,
                                                                                                bias=bcol[:, l, d, ct:ct + 1], scale=1.0),
                                     reads=[pk, 'bac', 'bxc'], writes=['C.' + dstn])
                        K.op('act', lambda e: e.activation(out=Cc["a"], in_=Cc["r"], func=AF.Exp, scale=sc1[:, l, d, ct:ct + 1]), reads=['C.r', 'sc1'], writes=['C.a'])
                        K.op('act', lambda e: e.activation(out=Cc["e2"], in_=Cc["r"], func=AF.Exp, scale=sc2[:, l, d, ct:ct + 1]), reads=['C.r', 'sc2'], writes=['C.e2'])
                        K.op('act', lambda e: e.activation(out=Cc["e2"], in_=Cc["e2"], func=AF.Sqrt, bias=onec, scale=-1.0), reads=['C.e2', 'cst'], writes=['C.e2'])
                        K.op('pool', lambda e: e.tensor_tensor(out=Cc["b"], in0=Cc["gi"], in1=Cc["e2"], op=ALU.mult), reads=['C.gi', 'C.e2'], writes=['C.b'])
                        K.op('pool', lambda e, sl=sl: e.tensor_tensor(out=Cc["b"], in0=Cc["b"], in1=Cc["xc"][:, sl], op=ALU.mult), reads=['C.b', 'C.xc%d' % cc], writes=['C.b'])
                        hn = "h%d" % (hctr % 2); hctr += 1
                        hk = 'C.' + hn
                        if d == 0:
                            init = 0.0 if prev is None else Cc[prev][:, CH - 1:CH]
                            K.op('dve', lambda e, hn=hn, init=init: e.tensor_tensor_scan(out=Cc[hn], data0=Cc["a"], data1=Cc["b"], initial=init, op0=ALU.mult, op1=ALU.add),
                                 reads=['C.a', 'C.b'] + ([] if prev is None else ['C.' + prev]), writes=[hk])
                            K.op('act', lambda e, hn=hn, sl=sl: e.copy(out=Cc["hf"][:, sl], in_=Cc[hn]), reads=[hk], writes=['C.hf'])
                        else:
                            init = 0.0 if prev is None else Cc[prev][:, 0:1]
                            K.op('dve', lambda e, hn=hn, init=init: e.tensor_tensor_scan(out=Cc[hn][:, ::-1], data0=Cc["a"][:, ::-1], data1=Cc["b"][:, ::-1], initial=init,
                                                                               op0=ALU.mult, op1=ALU.add),
                                 reads=['C.a', 'C.b'] + ([] if prev is None else ['C.' + prev]), writes=[hk])
                            K.dma('sp', Cc["gg"], gg_s[ct * 128:(ct + 1) * 128, sl], 'C.gg', reads=['gg_s'], writes=['C.gg'])
                            K.op('pool', lambda e, hn=hn, sl=sl: e.tensor_tensor(out=Cc["b"], in0=Cc[hn], in1=Cc["hf"][:, sl], op=ALU.add), reads=[hk, 'C.hf'], writes=['C.b'])
                            K.op('pool', lambda e: e.tensor_tensor(out=Cc["yb"], in0=Cc["b"], in1=Cc["gg"], op=ALU.mult), reads=['C.b', 'C.gg'], writes=['C.yb'])
                            K.dma('pool', yb_s[ct * 128:(ct + 1) * 128, sl], Cc["yb"], 'C.yb', reads=['C.yb'], writes=['yb_s'])
                        prev = hn

            E_ = carve([("ya", [8, 512], BF16), ("yb", [8, 512], BF16), ("yg", [8, 512], BF16), ("z", [512], BF16),
                        ("sga", [16, 512], BF16), ("sgb", [16, 512], BF16), ("mg", [16, 512], BF16),
                        ("xt", [16, 512], F32), ("mf", [512], F32), ("t1", [512], F32), ("t2", [512], F32)])
            for tt in range(NT):
                cs = slice(tt * 512, (tt + 1) * 512)
                K.dma('sp', E_["ya"], ya_s[:, cs].rearrange("(j p) c -> p j c", p=128), 'E.ya', reads=['ya_s'], writes=['E.ya'])
                K.dma('sp', E_["yb"], yb_s[:, cs].rearrange("(j p) c -> p j c", p=128), 'E.yb', reads=['yb_s'], writes=['E.yb'])
                K.dma('sp', E_["sga"], sga_s[:, cs].rearrange("(j p) c -> p j c", p=128), 'E.sga', reads=['sga_s'], writes=['E.sga'])
                K.dma('sp', E_["sgb"], sgb_s[:, cs].rearrange("(j p) c -> p j c", p=128), 'E.sgb', reads=['sgb_s'], writes=['E.sgb'])
                K.dma('sp', E_["xt"], xs[:, 1 + tt * 512: 1 + (tt + 1) * 512].rearrange("(kt p) c -> p kt c", p=128), 'E.xt', reads=[xs_k], writes=['E.xt'])
                K.dma('sp', E_["mf"], maskd[:, 1 + tt * 512: 1 + (tt + 1) * 512], 'E.mf', writes=['E.mf'])
                for j in range(8):
                    wt, wk = wload("s5_w_glu", l, j, 8)
                    ps, pk = nextps()
                    for kt in range(8):
                        K.op('pe', lambda e, ps=ps, wt=wt, kt=kt: e.matmul(ps[:, :], lhsT=wt[:, kt, :], rhs=E_["ya"][:, kt, :], start=(kt == 0), stop=(kt == 7)),
                             reads=[wk, 'E.ya'], writes=[pk], signal=(kt == 7))
                    K.op('act', lambda e, ps=ps: e.activation(out=E_["z"], in_=ps[:, :], func=AF.Sigmoid), reads=[pk], writes=['E.z'])
                    K.op('dve', lambda e, j=j: e.tensor_tensor(out=E_["yg"][:, j, :], in0=E_["ya"][:, j, :], in1=E_["z"], op=ALU.mult), reads=['E.ya', 'E.z'], writes=['E.yg'])
                for n in range(16):
                    wa, wak = wload("w_proj_a", l, n, 8)
                    wb_, wbk = wload("w_proj_b", l, n, 8)
                    psa, pka = nextps()
                    for kt in range(8):
                        K.op('pe', lambda e, psa=psa, wa=wa, kt=kt: e.matmul(psa[:, :], lhsT=wa[:, kt, :], rhs=E_["yg"][:, kt, :], start=(kt == 0), stop=(kt == 7)),
                             reads=[wak, 'E.yg'], writes=[pka], signal=(kt == 7))
                    psb_, pkb = nextps()
                    for kt in range(8):
                        K.op('pe', lambda e, psb_=psb_, wb_=wb_, kt=kt: e.matmul(psb_[:, :], lhsT=wb_[:, kt, :], rhs=E_["yb"][:, kt, :], start=(kt == 0), stop=(kt == 7)),
                             reads=[wbk, 'E.yb'], writes=[pkb], signal=(kt == 7))
                    K.op('dve', lambda e, psa=psa, n=n: e.tensor_tensor(out=E_["t1"], in0=psa[:, :], in1=E_["sga"][:, n, :], op=ALU.mult), reads=[pka, 'E.sga'], writes=['E.t1'])
                    K.op('dve', lambda e, psb_=psb_, n=n: e.tensor_tensor(out=E_["t2"], in0=psb_[:, :], in1=E_["sgb"][:, n, :], op=ALU.mult), reads=[pkb, 'E.sgb'], writes=['E.t2'])
                    K.op('pool', lambda e, n=n: e.tensor_tensor(out=E_["mg"][:, n, :], in0=E_["t1"], in1=E_["t2"], op=ALU.add), reads=['E.t1', 'E.t2'], writes=['E.mg'])
                for n in range(16):
                    wt, wk = wload("w_out", l, n, 16)
                    ps, pk = nextps()
                    for kt in range(16):
                        K.op('pe', lambda e, ps=ps, wt=wt, kt=kt: e.matmul(ps[:, :], lhsT=wt[:, kt, :], rhs=E_["mg"][:, kt, :], start=(kt == 0), stop=(kt == 15)),
                             reads=[wk, 'E.mg'], writes=[pk], signal=(kt == 15))
                    K.op('dve', lambda e, ps=ps: e.tensor_tensor(out=E_["t1"], in0=ps[:, :], in1=E_["mf"], op=ALU.mult), reads=[pk, 'E.mf'], writes=['E.t1'])
                    K.op('pool', lambda e, n=n: e.tensor_tensor(out=E_["xt"][:, n, :], in0=E_["xt"][:, n, :], in1=E_["t1"], op=ALU.add), reads=['E.t1', 'E.xt'], writes=['E.xt'])
                K.dma('pool', xs[:, 1 + tt * 512: 1 + (tt + 1) * 512].rearrange("(kt p) c -> p kt c", p=128), E_["xt"], 'E.xt', reads=['E.xt'], writes=[xs_k])

            Fh = carve([("xt", [16, 514], F32), ("tmp", [16, 514], BF16), ("rs", [514], F32), ("xn", [16, 514], BF16),
                        ("act", [48, 512], BF16), ("hg", [514], F32), ("hv", [514], F32), ("cg", [512], F32), ("cv", [512], F32),
                        ("wdn0", [48, 128], BF16), ("wdn1", [48, 128], BF16), ("mf", [512], F32), ("t1", [512], F32)])
            wdn = [Fh["wdn0"], Fh["wdn1"]]
            for tt in range(NT):
                c0 = tt * 512
                K.dma('sp', Fh["xt"], xs[:, c0:c0 + 514].rearrange("(kt p) c -> p kt c", p=128), 'F.xt', reads=[xs_k], writes=['F.xt'])
                K.dma('sp', Fh["mf"], maskd[:, 1 + tt * 512: 1 + (tt + 1) * 512], 'F.mf', writes=['F.mf'])
                rmsnorm(Fh["xt"], 'F.xt', 514, g2c[:, l, :], Fh["xn"], 'F.xn', Fh["tmp"], 'F.tmp', Fh["rs"], 'F.rs')
                for f in range(48):
                    res = []
                    for (jj, hn) in ((f, "hg"), (48 + f, "hv")):
                        wt, wk = wload("ffn_w_up", l, jj, 16)
                        ps, pk = nextps()
                        for kt in range(16):
                            K.op('pe', lambda e, ps=ps, wt=wt, kt=kt: e.matmul(ps[:, :], lhsT=wt[:, kt, :], rhs=Fh["xn"][:, kt, 1:513], start=(kt == 0), stop=(kt == 15)),
                                 reads=[wk, 'F.xn'], writes=[pk], signal=(kt == 15))
                        ps2, pk2 = nextps()
                        for kt in range(16):
                            K.op('pe', lambda e, ps2=ps2, wt=wt, kt=kt: e.matmul(ps2[:, 0:2], lhsT=wt[:, kt, :], rhs=Fh["xn"][:, kt, 0:514:513], start=(kt == 0), stop=(kt == 15)),
                                 reads=[wk, 'F.xn'], writes=[pk2], signal=(kt == 15))
                        K.op('act', lambda e, ps=ps, hn=hn: e.copy(out=Fh[hn][:, 1:513], in_=ps[:, :]), reads=[pk], writes=['F.' + hn])
                        K.op('act', lambda e, ps2=ps2, hn=hn: e.copy(out=Fh[hn][:, 0:514:513], in_=ps2[:, 0:2]), reads=[pk2], writes=['F.' + hn])
                    for (hn, cn, jj, eng) in (("hg", "cg", f, 'dve'), ("hv", "cv", 48 + f, 'dve')):
                        K.op(eng, lambda e, hn=hn, cn=cn, jj=jj: e.tensor_scalar_mul(out=Fh[cn], in0=Fh[hn][:, 0:512], scalar1=fcw[:, l, 0, jj:jj + 1]),
                             reads=['F.' + hn, 'fcw'], writes=['F.' + cn])
                        for k in (1, 2):
                            K.op(eng, lambda e, hn=hn, cn=cn, jj=jj, k=k: e.scalar_tensor_tensor(out=Fh[cn], in0=Fh[hn][:, k:k + 512], scalar=fcw[:, l, k, jj:jj + 1],
                                                                                   in1=Fh[cn], op0=ALU.mult, op1=ALU.add),
                                 reads=['F.' + hn, 'fcw', 'F.' + cn], writes=['F.' + cn])
                    K.op('act', lambda e: e.activation(out=Fh["cg"], in_=Fh["cg"], func=AF.Gelu), reads=['F.cg'], writes=['F.cg'])
                    K.op('dve', lambda e, f=f: e.tensor_tensor(out=Fh["act"][:, f, :], in0=Fh["cg"], in1=Fh["cv"], op=ALU.mult), reads=['F.cg', 'F.cv'], writes=['F.act'])
                for n in range(16):
                    wi = n % 2
                    wdk = 'wdn%d' % wi
                    K.dma('sp', wdn[wi], WT["ffn_w_down"][l, n], wdk, reads=[('wt', 'ffn_w_down', l)], writes=[wdk])
                    ps, pk = nextps()
                    for kt in range(48):
                        K.op('pe', lambda e, ps=ps, wi=wi, kt=kt: e.matmul(ps[:, :], lhsT=wdn[wi][:, kt, :], rhs=Fh["act"][:, kt, :], start=(kt == 0), stop=(kt == 47)),
                             reads=[wdk, 'F.act'], writes=[pk], signal=(kt == 47))
                    K.op('dve', lambda e, ps=ps: e.tensor_tensor(out=Fh["t1"], in0=ps[:, :], in1=Fh["mf"], op=ALU.mult), reads=[pk, 'F.mf'], writes=['F.t1'])
                    K.op('pool', lambda e, n=n: e.tensor_tensor(out=Fh["xt"][:, n, 1:513], in0=Fh["xt"][:, n, 1:513], in1=Fh["t1"], op=ALU.add), reads=['F.t1', 'F.xt'], writes=['F.xt'])
                K.dma('pool', xs2[:, 1 + tt * 512: 1 + (tt + 1) * 512].rearrange("(kt p) c -> p kt c", p=128), Fh["xt"][:, :, 1:513], 'F.xt', reads=['F.xt'], writes=[xs2_k])
            xs, xs2 = xs2, xs
            xs_k, xs2_k = xs2_k, xs_k
        G = carve([("xt0", [16, 512], F32), ("xt1", [16, 512], F32), ("tmp", [16, 512], BF16), ("rs", [512], F32),
                   ("o0", [16, 512], F32), ("o1", [16, 512], F32)])
        for tt in range(NT):
            xt = G["xt%d" % (tt % 2)]; xkey = 'G.xt%d' % (tt % 2)
            o = G["o%d" % (tt % 2)]; okey = 'G.o%d' % (tt % 2)
            K.dma('sp', xt, xs[:, 1 + tt * 512: 1 + (tt + 1) * 512].rearrange("(kt p) c -> p kt c", p=128), xkey, reads=[xs_k], writes=[xkey])
            rmsnorm(xt, xkey, 512, gfc[:, :], None, okey, G["tmp"], 'G.tmp', G["rs"], 'G.rs', out_f32=o)
            K.dma('sp', yT[:, tt * 512:(tt + 1) * 512].rearrange("(kt p) c -> p kt c", p=128), o, okey, reads=[okey], writes=['yT'])
        K.finish()
        print("instructions emitted:", K.nins, "sems:", len(K.sem))
    return nc


WNAMES = ["norm1_g", "w_in", "s5_a_re", "s5_a_im", "s5_log_dt", "s5_b_re", "s5_b_im", "s5_c_re", "s5_c_im", "s5_d",
          "s5_w_glu", "lru_conv_w", "lru_conv_b", "lru_w_a", "lru_b_a", "lru_w_x", "lru_b_x", "lru_lambda",
          "w_proj_a", "w_proj_b", "w_out", "norm2_g", "ffn_w_up", "ffn_conv_w", "ffn_w_down", "final_g"]


def run_seqs(seqs, weights, T, L, n_cores, debug=False):
    nc = build(T, L, debug)
    consts = make_consts()
    wd = {k: np.ascontiguousarray(np.asarray(weights[k], np.float32)) for k in WNAMES}
    in_maps = []
    for sq in seqs:
        S = sq.shape[0]
        xT = np.zeros((D, T + 2), np.float32)
        xT[:, 1:1 + S] = sq.T
        mask = np.zeros((128, T + 2), np.float32)
        mask[:, 1:1 + S] = 1.0
        m = {"xT": xT, "mask": mask, "consts": consts}
        m.update(wd)
        in_maps.append(m)
    res = run_bass_kernel_spmd(nc, in_maps, core_ids=list(range(n_cores)))
    if debug:
        return res.results
    outs = []
    for sq, r in zip(seqs, res.results):
        S = sq.shape[0]
        outs.append(np.ascontiguousarray(np.asarray(r["yT"])[:, :S].T).astype(np.float32))
    return outs


def kernel(**inputs):
    xp = np.asarray(inputs["x_prompt"], np.float32)
    xsm = np.asarray(inputs["x_sample"], np.float32)
    seqs = [xp[0]] + [xsm[i] for i in range(4)] + [xsm[3]] * 3
    outs = run_seqs(seqs, inputs, 8192, DEPTH, 8)
    y_prompt = outs[0][None]
    y_sample = np.stack(outs[1:5], axis=0)
    return (y_prompt, y_sample)
```

```python
import math
from contextlib import ExitStack
import numpy as np
import ml_dtypes
import concourse.bass as bass
import concourse.mybir as mybir
from concourse.bass_utils import run_bass_kernel_spmd

F32, BF16 = mybir.dt.float32, mybir.dt.bfloat16
AF, ALU = mybir.ActivationFunctionType, mybir.AluOpType

D = 2048; DEPTH = 4; S5W = 1024; LRW = 1024; FH = 6144; INC = 7168
Q = 8
EPS = 1e-6
ARENA = 168 * 1024
C_ID, C_SW, C_BD, C_RM, C_SGN, C_NSG, C_ONE, C_NPI, C_DEPS, C_CM = 0, 128, 256, 384, 392, 393, 394, 395, 396, 400
NCONST = 400 + 1024


def make_consts():
    c = np.zeros((128, NCONST), np.float32)
    k = np.arange(128)
    c[k, C_ID + k] = 1.0
    c[k, C_SW + (k + 64) % 128] = 1.0
    c[:, C_BD:C_BD + 128] = (k[:, None] // 16 == k[None, :] // 16)
    c[:, C_RM:C_RM + 8] = (k[:, None] // 16 == np.arange(8)[None, :])
    c[:, C_SGN] = np.where(k < 64, -1.0, 1.0)
    c[:, C_NSG] = np.where(k < 64, 1.0, -1.0)
    c[:, C_ONE] = 1.0
    c[:, C_NPI] = -math.pi
    c[:, C_DEPS] = D * EPS
    for g in range(8):
        c[:, C_CM + g * 128:C_CM + (g + 1) * 128] = (k[None, :] // 16 == g)
    return c


def small_cols(L):
    return L * 16 * 2 + 16 + L * 8 + L * 32 + L * 8 + 3 * L * 16 + L * 288


def _pl(v, kt):
    v = np.asarray(v, np.float32)
    v = v.reshape(v.shape[:-1] + (kt, 128))
    return np.ascontiguousarray(np.moveaxis(v, -1, 0)).reshape(128, -1)


def make_smalls(w, L):
    parts = [_pl(w["norm1_g"][:L], 16), _pl(w["norm2_g"][:L], 16), _pl(w["final_g"], 16), _pl(w["s5_d"][:L], 8),
             _pl(w["lru_conv_w"][:L], 8), _pl(w["lru_conv_b"][:L], 8), _pl(w["lru_b_a"][:L], 8), _pl(w["lru_b_x"][:L], 8),
             _pl(w["lru_lambda"][:L], 8), _pl(w["ffn_conv_w"][:L], 96)]
    out = np.ascontiguousarray(np.concatenate(parts, axis=1))
    assert out.shape == (128, small_cols(L)), out.shape
    return out


class Ctx:
    def __init__(s, nc, stack):
        s.nc, s.stack = nc, stack
        s.eng = {'pe': nc.tensor, 'act': nc.scalar, 'dve': nc.vector, 'pool': nc.gpsimd, 'sp': nc.sync}
        s.sem, s.cnt, s.st = {}, {}, {}
        s.seen = {e: {} for e in s.eng}
        s.nins = 0

    def getsem(s, key):
        if key not in s.sem:
            s.sem[key] = s.stack.enter_context(s.nc.semaphore("sm%d" % len(s.sem)))
            s.cnt[key] = 0
        return s.sem[key]

    def _wait(s, e, key, val):
        if e == 'pe' and key == 'pe':
            return
        if s.seen[e].get(key, 0) >= val:
            return
        s.eng[e].wait_ge(s.sem[key], val)
        s.seen[e][key] = val

    def _deps(s, e, reads, writes):
        for k in reads:
            for key, val in s.st.setdefault(k, ({}, {}))[0].items():
                s._wait(e, key, val)
        for k in writes:
            w, r = s.st.setdefault(k, ({}, {}))
            for key, val in w.items():
                s._wait(e, key, val)
            for key, val in r.items():
                s._wait(e, key, val)

    def _mark(s, reads, writes, key, val):
        for k in reads:
            r = s.st[k][1]
            r[key] = max(r.get(key, 0), val)
        for k in writes:
            w = s.st[k][0]
            w[key] = max(w.get(key, 0), val)

    def op(s, e, fn, reads=(), writes=(), signal=True):
        s._deps(e, reads, writes)
        ins = fn(s.eng[e])
        s.getsem(e)
        if signal:
            s.cnt[e] += 1
            ins.then_inc(s.sem[e], 1)
            val = s.cnt[e]
        else:
            val = s.cnt[e] + 1
        s._mark(reads, writes, e, val)
        s.nins += 1

    def dma(s, q, out, in_, semkey, reads=(), writes=(), **kw):
        s._deps(q, reads, writes)
        s.getsem(semkey)
        ins = s.eng[q].dma_start(out=out, in_=in_, **kw)
        s.cnt[semkey] += 16
        ins.then_inc(s.sem[semkey], 16)
        s._mark(reads, writes, semkey, s.cnt[semkey])
        s.nins += 1

    def barrier(s):
        for e in s.eng:
            for key, val in s.cnt.items():
                if val > 0:
                    s._wait(e, key, val)

    def finish(s):
        for key, val in s.cnt.items():
            if val > 0:
                s._wait('sp', key, val)


def build(T, L, debug=False):
    assert T % 512 == 0
    NT = T // 512
    NC_ = T // Q
    BW = min(512, NC_)
    NB = NC_ // BW
    NL = int(round(math.log2(NC_)))
    assert 2 ** NL == NC_
    HP = max(BW, NC_ // 2)
    CH = min(2048, T)
    NCH = T // CH
    nc = bass.Bass("TRN2", target_bir_lowering=False)
    def dr(name, shape, dt, kind="ExternalInput"):
        if debug and kind == "Internal" and not name.endswith("_t"):
            kind = "ExternalOutput"
        return nc.dram_tensor(name, list(shape), dt, kind=kind).ap()
    xT = dr("xT", [D, T + 2], F32)
    maskd = dr("mask", [128, T + 2], F32)
    constd = dr("consts", [128, NCONST], F32)
    smd = dr("smalls", [128, small_cols(L)], F32)
    W = {}
    for name, shape in [("w_in", [L, D, INC]), ("s5_a_re", [L, 2, 64, 64]), ("s5_a_im", [L, 2, 64, 64]),
                        ("s5_log_dt", [L, 2, 64]), ("s5_b_re", [L, 2, 64, 64, 16]), ("s5_b_im", [L, 2, 64, 64, 16]),
                        ("s5_c_re", [L, 2, 64, 16, 64]), ("s5_c_im", [L, 2, 64, 16, 64]),
                        ("s5_w_glu", [L, S5W, S5W]),
                        ("lru_w_a", [L, 2, 16, 64, 64]), ("lru_w_x", [L, 2, 16, 64, 64]),
                        ("w_proj_a", [L, S5W, D]),
                        ("w_proj_b", [L, LRW, D]), ("w_out", [L, D, D]), ("ffn_w_up", [L, D, 2 * FH]),
                        ("ffn_w_down", [L, FH, D])]:
        W[name] = dr(name, shape, F32)
    yT = dr("yT", [D, T], F32, kind="ExternalOutput")
    xs = dr("x_scr", [D, T + 2], F32, "Internal")
    xs2 = dr("x_scr2", [D, T + 2], F32, "Internal")
    xs_k, xs2_k = 'xs', 'xs2'
    u_s = dr("u_scr", [S5W, T], BF16, "Internal")
    xr_s = dr("xr_scr", [LRW, T], BF16, "Internal")
    gg_s = dr("gg_scr", [LRW, T], BF16, "Internal")
    sga_s = dr("sga_scr", [D, T], BF16, "Internal")
    sgb_s = dr("sgb_scr", [D, T], BF16, "Internal")
    ya_s = dr("ya_scr", [S5W, T], BF16, "Internal")
    yb_s = dr("yb_scr", [LRW, T], BF16, "Internal")
    WT = {}
    for name, (J, KT) in {"w_in": (56, 16), "s5_w_glu": (8, 8), "w_proj_a": (16, 8), "w_proj_b": (16, 8),
                          "w_out": (16, 16), "ffn_w_up": (96, 16), "ffn_w_down": (16, 48)}.items():
        WT[name] = dr(name + "_t", [L, J, 128, KT, 128], BF16, "Internal")

    with ExitStack() as stack:
        sb = lambda name, shape, dt: stack.enter_context(nc.sbuf_tensor(name, list(shape), dt))
        K = Ctx(nc, stack)
        cst = sb("cst", [128, NCONST], F32)
        idb = sb("idb", [128, 128], BF16); swb = sb("swb", [128, 128], BF16); oneb = sb("oneb", [128, 128], BF16)
        g1c = sb("g1c", [128, L, 16], F32); g2c = sb("g2c", [128, L, 16], F32); gfc = sb("gfc", [128, 16], F32)
        s5dc = sb("s5dc", [128, L, 8], F32)
        cwc = sb("cwc", [128, L, 4, 8], F32); cbc = sb("cbc", [128, L, 8], F32)
        bac = sb("bac", [128, L, 2, 8], F32); bxc = sb("bxc", [128, L, 2, 8], F32)
        lamc = sb("lamc", [128, L, 2, 8], F32); sc1 = sb("sc1", [128, L, 2, 8], F32); sc2 = sb("sc2", [128, L, 2, 8], F32)
        fcw = sb("fcw", [128, L, 3, 96], F32)
        NWS = 6
        wslot = [sb("wslot%d" % i, [128, 16, 128], BF16) for i in range(NWS)]
        big = sb("big", [128, ARENA // 4], F32)
        psb = [stack.enter_context(nc.psum_tensor("ps%d" % i, [128, 512], F32)) for i in range(8)]
        psk = ["ps%d" % i for i in range(8)]

        def carve(specs):
            K.barrier()
            off = 0
            out = {}
            for name, shape, dt in specs:
                n = int(np.prod(shape))
                nb = n * (4 if dt == F32 else 2)
                nb4 = (nb + 3) // 4
                ap = big[:, off:off + nb4]
                if dt != F32:
                    ap = ap.bitcast(BF16)[:, 0:n]
                if len(shape) > 1:
                    names = ["d%d" % i for i in range(len(shape))]
                    ap = ap.rearrange("p (%s) -> p %s" % (" ".join(names), " ".join(names)),
                                      **{names[i]: shape[i] for i in range(len(shape) - 1)})
                out[name] = ap
                off += nb4
            assert off * 4 <= ARENA, off * 4
            return out

        K.dma('sp', cst[:], constd[:, :], 'cst', writes=['cst'])
        K.op('act', lambda e: e.copy(out=idb[:], in_=cst[:, C_ID:C_ID + 128]), reads=['cst'], writes=['idb'])
        K.op('act', lambda e: e.copy(out=swb[:], in_=cst[:, C_SW:C_SW + 128]), reads=['cst'], writes=['swb'])
        K.op('dve', lambda e: e.memset(oneb[:], 1.0), writes=['oneb'])
        onec = cst[:, C_ONE:C_ONE + 1]
        sgnc = cst[:, C_SGN:C_SGN + 1]
        nsgc = cst[:, C_NSG:C_NSG + 1]
        npic = cst[:, C_NPI:C_NPI + 1]

        off_ = 0
        for tname, tl in (("g1c", g1c), ("g2c", g2c), ("gfc", gfc), ("s5dc", s5dc), ("cwc", cwc), ("cbc", cbc),
                          ("bac", bac), ("bxc", bxc), ("lamc", lamc), ("fcw", fcw)):
            n = int(np.prod(tl.shape[1:]))
            dst = tl[:]
            if len(tl.shape) > 2:
                names = ["d%d" % i for i in range(len(tl.shape) - 1)]
                dst = dst.rearrange("p %s -> p (%s)" % (" ".join(names), " ".join(names)))
            K.dma('sp', dst, smd[:, off_:off_ + n], tname, writes=[tname])
            off_ += n
        assert off_ == small_cols(L), (off_, small_cols(L))
        sD = math.sqrt(D)
        K.op('act', lambda e: e.mul(out=g1c[:], in_=g1c[:], mul=sD), reads=['g1c'], writes=['g1c'])
        K.op('act', lambda e: e.mul(out=g2c[:], in_=g2c[:], mul=sD), reads=['g2c'], writes=['g2c'])
        K.op('act', lambda e: e.mul(out=gfc[:], in_=gfc[:], mul=sD), reads=['gfc'], writes=['gfc'])
        K.op('act', lambda e: e.activation(out=sc1[:], in_=lamc[:], func=AF.Exp, scale=-1.0), reads=['lamc'], writes=['sc1'])
        K.op('act', lambda e: e.activation(out=sc1[:], in_=sc1[:], func=AF.Ln, bias=onec, scale=1.0), reads=['sc1', 'cst'], writes=['sc1'])
        K.op('act', lambda e: e.mul(out=sc2[:], in_=sc1[:], mul=-16.0), reads=['sc1'], writes=['sc2'])
        K.op('act', lambda e: e.mul(out=sc1[:], in_=sc1[:], mul=-8.0), reads=['sc1', 'sc2'], writes=['sc1'])

        K.dma('sp', xs[:, :], xT[:, :], 'xcopy', writes=['xs'])
        K.dma('sp', xs2[:, :], xT[:, :], 'xcopy', writes=['xs2'])
        CONV = [("w_in", 56, 16), ("s5_w_glu", 8, 8), ("w_proj_a", 16, 8), ("w_proj_b", 16, 8), ("w_out", 16, 16),
                ("ffn_w_up", 96, 16), ("ffn_w_down", 16, 48)]

        def conv_layer(l):
            for name, J, KT in CONV:
                for j in range(J):
                    src = W[name][l][:, j * 128:(j + 1) * 128].rearrange("(kt p) n -> p kt n", p=128)
                    K.dma('pool', WT[name][l, j], src, 'wconv_%s_%d' % (name, l), writes=[('wt', name, l)])
        for _l in range(L):
            conv_layer(_l)

        wctr = [0]

        def wload(name, l, j, KT):
            i = wctr[0] % NWS
            wctr[0] += 1
            key = 'wslot%d' % i
            K.dma('sp', wslot[i][:, 0:KT, :], WT[name][l, j], key, reads=[('wt', name, l)], writes=[key])
            return wslot[i], key

        pctr = [0]

        def nextps():
            i = pctr[0] % 8
            pctr[0] += 1
            return psb[i], psk[i]

        def rmsnorm(xt, xkey, ncols, gcol, xn, xnkey, tmp, tmpkey, rs, rskey, out_f32=None):
            K.op('act', lambda e: e.activation(out=tmp[:, :, 0:ncols], in_=xt[:, :, 0:ncols], func=AF.Square),
                 reads=[xkey], writes=[tmpkey])
            c0 = 0
            while c0 < ncols:
                cw = min(512, ncols - c0)
                ps, pk = nextps()
                for kt in range(16):
                    K.op('pe', lambda e, kt=kt, c0=c0, cw=cw, ps=ps: e.matmul(ps[:, 0:cw], lhsT=oneb[:], rhs=tmp[:, kt, c0:c0 + cw],
                                                                   start=(kt == 0), stop=(kt == 15)),
                         reads=[tmpkey, 'oneb'], writes=[pk], signal=(kt == 15))
                K.op('act', lambda e, c0=c0, cw=cw, ps=ps: e.activation(out=rs[:, c0:c0 + cw], in_=ps[:, 0:cw], func=AF.Ln,
                                                                 bias=cst[:, C_DEPS:C_DEPS + 1], scale=1.0),
                     reads=[pk, 'cst'], writes=[rskey])
                K.op('act', lambda e, c0=c0, cw=cw: e.activation(out=rs[:, c0:c0 + cw], in_=rs[:, c0:c0 + cw], func=AF.Exp, scale=-0.5),
                     reads=[rskey], writes=[rskey])
                c0 += cw
            for kt in range(16):
                eng = 'dve'
                dst = xn if out_f32 is None else out_f32
                K.op(eng, lambda e, kt=kt, dst=dst: e.scalar_tensor_tensor(out=dst[:, kt, 0:ncols], in0=xt[:, kt, 0:ncols], scalar=gcol[:, kt:kt + 1],
                                                                  in1=rs[:, 0:ncols], op0=ALU.mult, op1=ALU.mult),
                     reads=[xkey, rskey, 'g1c', 'g2c', 'gfc'], writes=[xnkey])

        for l in range(L):
            A = carve([("xt0", [16, 512], F32), ("xt1", [16, 512], F32), ("tmp", [16, 512], BF16), ("rs", [512], F32),
                       ("xn", [16, 512], BF16), ("og0", [8, 512], BF16), ("og1", [8, 512], BF16)])
            for tt in range(NT):
                c0 = 1 + tt * 512
                xt = A["xt%d" % (tt % 2)]; xkey = 'A.xt%d' % (tt % 2)
                K.dma('sp', xt, xs[:, c0:c0 + 512].rearrange("(kt p) c -> p kt c", p=128), xkey, reads=[xs_k], writes=[xkey])
                rmsnorm(xt, xkey, 512, g1c[:, l, :], A["xn"], 'A.xn', A["tmp"], 'A.tmp', A["rs"], 'A.rs')
                for jg in range(7):
                    og = A["og%d" % (jg % 2)]; ogk = 'A.og%d' % (jg % 2)
                    for jj in range(8):
                        j = jg * 8 + jj
                        wt, wk = wload("w_in", l, j, 16)
                        ps, pk = nextps()
                        for kt in range(16):
                            K.op('pe', lambda e, kt=kt, wt=wt, ps=ps: e.matmul(ps[:, :], lhsT=wt[:, kt, :], rhs=A["xn"][:, kt, :],
                                                                      start=(kt == 0), stop=(kt == 15)),
                                 reads=[wk, 'A.xn'], writes=[pk], signal=(kt == 15))
                        func = AF.Copy if jg < 2 else (AF.Gelu if jg == 2 else AF.Sigmoid)
                        K.op('act', lambda e, jj=jj, og=og, ps=ps, func=func: e.activation(out=og[:, jj, :], in_=ps[:, :], func=func),
                             reads=[pk], writes=[ogk])
                    dst = [u_s, xr_s, gg_s, sga_s[0:1024], sga_s[1024:2048], sgb_s[0:1024], sgb_s[1024:2048]][jg]
                    dkey = ['u_s', 'xr_s', 'gg_s', 'sga_s', 'sga_s', 'sgb_s', 'sgb_s'][jg]
                    K.dma('act', dst[:, tt * 512:(tt + 1) * 512].rearrange("(j p) c -> p j c", p=128), og, ogk, reads=[ogk], writes=[dkey])

            NS0 = 4
            B = carve([("uph", [Q, NC_], BF16), ("ya", [T], BF16),
                       ("H", [20, HP + NC_], BF16),
                       ("CAL", [8 * Q * 2, 128], BF16),
                       ("S0L", [NS0, Q, 128], BF16), ("RP", [8, 128], BF16), ("RT", [8, 128], BF16), ("KB", [2 * Q + 1, 128], BF16),
                       ("E", [2, Q, 128], BF16), ("CA", [2, Q + 1, 128], BF16),
                       ("b1", [128], F32), ("b2", [128], F32), ("c1", [128], F32), ("c2", [128], F32),
                       ("nat", [128], F32), ("nat2", [128], F32),
                       ("sm", [40, 8], F32), ("pw", [2, Q + 1, 2, 8], F32), ("zc", [2, Q, 2, 8], F32), ("rp", [2, NL, 2, 8], F32),
                       ("t1", [128], F32), ("t2", [128], F32)])
            K.op('pool', lambda e: e.memset(B["H"][:, :, :], 0.0), writes=['B.H%d' % i for i in range(20)])
            SM = lambda i: B["sm"][:, i, :]
            PE_ = 'dve'

            def tt_(out, a, b, op, rk, wk):
                K.op(PE_, lambda e: e.tensor_tensor(out=out, in0=a, in1=b, op=op), reads=rk, writes=wk)

            def cmul(ore, oim, are, aim, bre, bim, rk, wk):
                tt_(SM(38), are, bre, ALU.mult, rk, ['B.t38'])
                tt_(SM(39), aim, bim, ALU.mult, rk, ['B.t39'])
                tt_(SM(37), are, bim, ALU.mult, rk, ['B.t37'])
                tt_(SM(36), aim, bre, ALU.mult, rk, ['B.t36'])
                tt_(ore, SM(38), SM(39), ALU.subtract, ['B.t38', 'B.t39'], wk)
                tt_(oim, SM(37), SM(36), ALU.add, ['B.t37', 'B.t36'], wk)

            rpctr = 0
            for gt in range(8):
                g0 = gt * 8
                K.dma('sp', B["ya"], u_s[gt * 128:(gt + 1) * 128, :], 'B.ya', reads=['u_s'], writes=['B.ya'])
                K.op('dve', lambda e: e.tensor_copy(out=B["uph"], in_=B["ya"].rearrange("p (c s) -> p s c", s=Q)), reads=['B.ya'], writes=['B.uph'])
                for d in range(2):
                    for nm, dst in (("s5_a_re", 0), ("s5_a_im", 1)):
                        for h in range(2):
                            K.dma('sp', B["nat"][0:8, h * 64:(h + 1) * 64], W[nm][l, d, g0:g0 + 8, :], 'B.nat', writes=['B.nat'])
                        ps, pk = nextps()
                        K.op('pe', lambda e, ps=ps: e.transpose(out=ps[:, 0:8], in_=B["nat"][0:8, :], identity=cst[0:8, C_ID:C_ID + 8]),
                             reads=['B.nat', 'cst'], writes=[pk])
                        K.op('act', lambda e, ps=ps, dst=dst: e.copy(out=SM(dst), in_=ps[:, 0:8]), reads=[pk], writes=['B.sm%d' % dst])
                    K.dma('sp', SM(2), W["s5_log_dt"][l, d, g0:g0 + 8].partition_broadcast(128), 'B.sm2', writes=['B.sm2'])
                    K.op('act', lambda e: e.activation(out=SM(2), in_=SM(2), func=AF.Exp), reads=['B.sm2'], writes=['B.sm2'])
                    k0, k1, k2 = ['B.sm0'], ['B.sm1'], ['B.sm2']
                    tt_(SM(3), SM(0), SM(2), ALU.mult, k0 + k2, ['B.sm3'])
                    tt_(SM(4), SM(1), SM(2), ALU.mult, k1 + k2, ['B.sm4'])
                    K.op('act', lambda e: e.activation(out=SM(5), in_=SM(3), func=AF.Exp), reads=['B.sm3'], writes=['B.sm5'])
                    K.op('act', lambda e: e.activation(out=SM(6), in_=SM(4), func=AF.Sin, scale=1.0 / 16), reads=['B.sm4'], writes=['B.sm6'])
                    K.op('act', lambda e: e.activation(out=SM(7), in_=SM(4), func=AF.Sin, scale=1.0 / 32), reads=['B.sm4'], writes=['B.sm7'])
                    tt_(SM(7), SM(7), SM(7), ALU.mult, ['B.sm7'], ['B.sm7'])
                    K.op(PE_, lambda e: e.tensor_scalar(out=SM(7), in0=SM(7), scalar1=-2.0, scalar2=1.0, op0=ALU.mult, op1=ALU.add),
                         reads=['B.sm7'], writes=['B.sm7'])
                    for _sq in range(4):
                        cmul(SM(7), SM(6), SM(7), SM(6), SM(7), SM(6), ['B.sm7', 'B.sm6'], ['B.sm7', 'B.sm6'])
                    PW = lambda k, c: B["pw"][:, d, k, c, :]
                    pwk = lambda k: 'B.pw%d_%d' % (d, k)
                    tt_(PW(1, 0), SM(5), SM(7), ALU.mult, ['B.sm5', 'B.sm7'], [pwk(1)])
                    tt_(PW(1, 1), SM(5), SM(6), ALU.mult, ['B.sm5', 'B.sm6'], [pwk(1)])
                    K.op(PE_, lambda e: e.memset(PW(0, 0), 1.0), writes=[pwk(0)])
                    K.op(PE_, lambda e: e.memset(PW(0, 1), 0.0), writes=[pwk(0)])
                    K.op(PE_, lambda e: e.tensor_scalar_add(out=SM(8), in0=PW(1, 0), scalar1=-1.0), reads=[pwk(1)], writes=['B.sm8'])
                    tt_(SM(9), SM(0), SM(0), ALU.mult, k0, ['B.sm9'])
                    tt_(SM(10), SM(1), SM(1), ALU.mult, k1, ['B.sm10'])
                    tt_(SM(9), SM(9), SM(10), ALU.add, ['B.sm9', 'B.sm10'], ['B.sm9'])
                    K.op(PE_, lambda e: e.reciprocal(out=SM(9), in_=SM(9)), reads=['B.sm9'], writes=['B.sm9'])
                    tt_(SM(11), SM(0), SM(9), ALU.mult, k0 + ['B.sm9'], ['B.sm11'])
                    tt_(SM(12), SM(1), SM(9), ALU.mult, k1 + ['B.sm9'], ['B.sm12'])
                    K.op(PE_, lambda e: e.tensor_scalar_mul(out=SM(12), in0=SM(12), scalar1=-1.0), reads=['B.sm12'], writes=['B.sm12'])
                    cmul(SM(13), SM(14), SM(8), PW(1, 1), SM(11), SM(12), ['B.sm8', pwk(1), 'B.sm11', 'B.sm12'], ['B.coef'])
                    for k in range(2, Q + 1):
                        cmul(PW(k, 0), PW(k, 1), PW(k - 1, 0), PW(k - 1, 1), PW(1, 0), PW(1, 1), [pwk(k - 1), pwk(1)], [pwk(k)])
                    ZC = lambda k, c: B["zc"][:, d, k, c, :]
                    for k in range(Q):
                        cmul(ZC(k, 0), ZC(k, 1), PW(k, 0), PW(k, 1), SM(13), SM(14), [pwk(k), 'B.coef'], ['B.zc%d' % d])
                        K.op(PE_, lambda e, k=k: e.tensor_scalar_mul(out=ZC(k, 1), in0=ZC(k, 1), scalar1=sgnc), reads=['B.zc%d' % d, 'cst'], writes=['B.zc%d' % d])
                    RPc = lambda lv, c: B["rp"][:, d, lv, c, :]
                    K.op(PE_, lambda e: e.tensor_copy(out=RPc(0, 0), in_=PW(Q, 0)), reads=[pwk(Q)], writes=['B.rp%d' % d])
                    K.op(PE_, lambda e: e.tensor_copy(out=RPc(0, 1), in_=PW(Q, 1)), reads=[pwk(Q)], writes=['B.rp%d' % d])
                    for lv in range(1, NL):
                        cmul(RPc(lv, 0), RPc(lv, 1), RPc(lv - 1, 0), RPc(lv - 1, 1), RPc(lv - 1, 0), RPc(lv - 1, 1), ['B.rp%d' % d], ['B.rp%d' % d])
                    K.op(PE_, lambda e: e.tensor_scalar_mul(out=B["rp"][:, d, :, 1, :], in0=B["rp"][:, d, :, 1, :], scalar1=nsgc),
                         reads=['B.rp%d' % d, 'cst'], writes=['B.rp%d' % d])
                    bre = W["s5_b_re"][l, d, g0:g0 + 8].rearrange("g p j -> p g j")
                    bim = W["s5_b_im"][l, d, g0:g0 + 8].rearrange("g p j -> p g j")
                    v3 = lambda ap: ap.rearrange("p (g j) -> p g j", g=8)
                    K.dma('sp', v3(B["b1"])[0:64], bre, 'B.b1', writes=['B.b1'])
                    K.dma('sp', v3(B["b1"])[64:128], bim, 'B.b1', writes=['B.b1'])
                    K.dma('sp', v3(B["b2"])[0:64], bim, 'B.b2', writes=['B.b2'])
                    K.dma('sp', v3(B["b2"])[64:128], bre, 'B.b2', writes=['B.b2'])
                    cre = W["s5_c_re"][l, d, g0:g0 + 8].rearrange("g i p -> (g i) p")
                    cim = W["s5_c_im"][l, d, g0:g0 + 8].rearrange("g i p -> (g i) p")
                    for (first, second, dst, nat, nk) in ((cre, cim, "c1", "nat", 'B.nat'), (cim, cre, "c2", "nat2", 'B.nat2')):
                        K.dma('sp', B[nat][:, 0:64], first, nk, writes=[nk])
                        K.dma('sp', B[nat][:, 64:128], second, nk, writes=[nk])
                        ps, pk = nextps()
                        K.op('pe', lambda e, ps=ps, nat=nat: e.transpose(out=ps[:, 0:128], in_=B[nat][:, :], identity=cst[:, C_ID:C_ID + 128]),
                             reads=[nk, 'cst'], writes=[pk])
                        K.op('act', lambda e, ps=ps, dst=dst: e.copy(out=B[dst], in_=ps[:, 0:128]), reads=[pk], writes=['B.' + dst])
                    bc = lambda col: col.unsqueeze(2).to_broadcast([128, 8, 16])
                    for k in range(Q):
                        tt_(v3(B["t1"]), v3(B["b1"]), bc(ZC(k, 0)), ALU.mult, ['B.b1', 'B.zc%d' % d], ['B.t1'])
                        tt_(v3(B["t2"]), v3(B["b2"]), bc(ZC(k, 1)), ALU.mult, ['B.b2', 'B.zc%d' % d], ['B.t2'])
                        tt_(B["E"][:, d, k, :], B["t1"], B["t2"], ALU.add, ['B.t1', 'B.t2'], ['B.E%d' % d])
                    for tau in range(Q + 1):
                        K.op(PE_, lambda e, tau=tau: e.tensor_scalar_mul(out=SM(15), in0=PW(tau, 0), scalar1=nsgc), reads=[pwk(tau), 'cst'], writes=['B.sm15'])
                        tt_(v3(B["t1"]), v3(B["c1"]), bc(SM(15)), ALU.mult, ['B.c1', 'B.sm15'], ['B.t1'])
                        tt_(v3(B["t2"]), v3(B["c2"]), bc(PW(tau, 1)), ALU.mult, ['B.c2', pwk(tau)], ['B.t2'])
                        tt_(B["CA"][:, d, tau, :], B["t1"], B["t2"], ALU.subtract, ['B.t1', 'B.t2'], ['B.CA%d' % d])
                    for tau in range(1, Q + 1):
                        for g in range(8):
                            idx = (g * Q + (tau - 1)) * 2 + d
                            K.op('pool', lambda e, idx=idx, tau=tau, g=g: e.tensor_tensor(out=B["CAL"][:, idx, :], in0=B["CA"][:, d, tau, :],
                                                                                   in1=cst[:, C_CM + g * 128:C_CM + (g + 1) * 128], op=ALU.mult),
                                 reads=['B.CA%d' % d, 'cst'], writes=['B.CAL'])
                    for tau in range(Q):
                        ps, pk = nextps()
                        K.op('pe', lambda e, ps=ps, tau=tau: e.matmul(ps[:, 0:128], lhsT=B["E"][:, d, 0, :], rhs=B["CA"][:, d, tau, :], start=True, stop=True),
                             reads=['B.E%d' % d, 'B.CA%d' % d], writes=[pk])
                        K.op('dve', lambda e, ps=ps, tau=tau: e.tensor_tensor(out=B["KB"][:, tau * 2 + d, :], in0=ps[:, 0:128], in1=cst[:, C_BD:C_BD + 128], op=ALU.mult),
                             reads=[pk, 'cst'], writes=['B.KB'])
                K.op('dve', lambda e: e.tensor_scalar_mul(out=B["t1"], in0=cst[:, C_ID:C_ID + 128], scalar1=s5dc[:, l, gt:gt + 1]), reads=['cst', 's5dc'], writes=['B.t1'])
                tt_(B["t2"], B["KB"][:, 0, :], B["KB"][:, 1, :], ALU.add, ['B.KB'], ['B.t2'])
                tt_(B["KB"][:, 2 * Q, :], B["t1"], B["t2"], ALU.add, ['B.t1', 'B.t2'], ['B.KB'])

                NCHN = 4
                for d in range(2):
                    off = HP if d == 0 else 0
                    padlo = 0 if d == 0 else NC_
                    K.op('pool', lambda e, padlo=padlo: e.memset(B["H"][:, 16:16 + NCHN, padlo:padlo + HP], 0.0),
                         writes=['B.H%d' % (16 + i) for i in range(NCHN)])
                    for gg in range(8 // NCHN):
                        chains = []
                        for gi in range(NCHN):
                            g = gg * NCHN + gi
                            sl = gi
                            slk = 'B.S0L%d' % sl
                            for k in range(Q):
                                ps, pk = nextps()
                                K.op('pe', lambda e, ps=ps, k=k: e.transpose(out=ps[:, 0:128].bitcast(BF16)[:, 0:128], in_=B["E"][:, d, k, :], identity=idb[:]),
                                     reads=['B.E%d' % d, 'idb'], writes=[pk])
                                K.op('dve', lambda e, ps=ps, k=k, sl=sl, g=g: e.tensor_scalar_mul(out=B["S0L"][:, sl, k, :], in0=ps[:, 0:128].bitcast(BF16)[:, 0:128],
                                                                                      scalar1=cst[:, C_RM + g:C_RM + g + 1]),
                                     reads=[pk, 'cst'], writes=[slk])
                            hi = g * 2 + d
                            Hf = B["H"][:, hi, :]; Hfk = 'B.H%d' % hi
                            Hs = B["H"][:, 16 + gi, :]; Hsk = 'B.H%d' % (16 + gi)
                            for blk in range(NB):
                                ps, pk = nextps()
                                for s_ in range(Q):
                                    kk = (Q - 1 - s_) if d == 0 else s_
                                    c_lo = blk * BW
                                    K.op('pe', lambda e, ps=ps, s_=s_, kk=kk, c_lo=c_lo, sl=sl: e.matmul(ps[:, 0:BW], lhsT=B["S0L"][:, sl, kk, :],
                                                                                          rhs=B["uph"][:, s_, c_lo:c_lo + BW],
                                                                                          start=(s_ == 0), stop=(s_ == Q - 1)),
                                         reads=[slk, 'B.uph'], writes=[pk], signal=(s_ == Q - 1))
                                K.op('act', lambda e, ps=ps, blk=blk, Hf=Hf: e.copy(out=Hf[:, off + blk * BW: off + (blk + 1) * BW], in_=ps[:, 0:BW]),
                                     reads=[pk], writes=[Hfk])
                            chains.append([g, Hf, Hfk, Hs, Hsk])
                        evc = 0
                        for lv in range(NL):
                            sh = 2 ** lv
                            for chn in chains:
                                g, src, srck, dst, dstk = chn
                                ri = rpctr % 8; rpctr += 1
                                rk = 'B.RP%d' % ri
                                rtk = 'B.RT%d' % ri
                                K.op('act', lambda e, lv=lv, g=g, ri=ri: e.mul(out=B["RT"][:, ri, :], in_=swb[:], mul=B["rp"][:, d, lv, 1, g:g + 1]),
                                     reads=['swb', 'B.rp%d' % d], writes=[rtk])
                                K.op('dve', lambda e, lv=lv, g=g, ri=ri: e.scalar_tensor_tensor(out=B["RP"][:, ri, :], in0=idb[:], scalar=B["rp"][:, d, lv, 0, g:g + 1],
                                                                                     in1=B["RT"][:, ri, :], op0=ALU.mult, op1=ALU.add),
                                     reads=['idb', 'B.rp%d' % d, rtk], writes=[rk])
                                for blk in range(NB):
                                    ps, pk = nextps()
                                    a0 = off + blk * BW
                                    sa = a0 - sh if d == 0 else a0 + sh
                                    K.op('pe', lambda e, ps=ps, a0=a0, src=src: e.matmul(ps[:, 0:BW], lhsT=idb[:], rhs=src[:, a0:a0 + BW], start=True, stop=False),
                                         reads=['idb', srck], writes=[pk], signal=False)
                                    K.op('pe', lambda e, ps=ps, sa=sa, src=src, ri=ri: e.matmul(ps[:, 0:BW], lhsT=B["RP"][:, ri, :], rhs=src[:, sa:sa + BW], start=False, stop=True),
                                         reads=[rk, srck], writes=[pk])
                                    evc += 1
                                    if evc % 2 == 0:
                                        K.op('act', lambda e, ps=ps, a0=a0, dst=dst: e.copy(out=dst[:, a0:a0 + BW], in_=ps[:, 0:BW]), reads=[pk], writes=[dstk])
                                    else:
                                        K.op('dve', lambda e, ps=ps, a0=a0, dst=dst: e.tensor_copy(out=dst[:, a0:a0 + BW], in_=ps[:, 0:BW]), reads=[pk], writes=[dstk])
                                chn[1], chn[2], chn[3], chn[4] = dst, dstk, src, srck
                        if NL % 2 == 1:
                            for chn in chains:
                                g, cur, curk, oth, othk = chn
                                K.op('pool', lambda e, cur=cur, oth=oth: e.tensor_copy(out=oth[:, off:off + NC_], in_=cur[:, off:off + NC_]), reads=[curk], writes=[othk])
                for r in range(Q):
                    for blk in range(NB):
                        ps, pk = nextps()
                        c_lo = blk * BW
                        first = True
                        for g in range(8):
                            idx = (g * Q + r) * 2 + 0
                            Hf = B["H"][:, g * 2, :]
                            K.op('pe', lambda e, ps=ps, idx=idx, Hf=Hf, c_lo=c_lo, first=first: e.matmul(ps[:, 0:BW], lhsT=B["CAL"][:, idx, :], rhs=Hf[:, HP + c_lo - 1: HP + c_lo - 1 + BW],
                                                                                          start=first, stop=False),
                                 reads=['B.CAL', 'B.H%d' % (g * 2)], writes=[pk], signal=False)
                            first = False
                            idx = (g * Q + (Q - r - 1)) * 2 + 1
                            Hb = B["H"][:, g * 2 + 1, :]
                            K.op('pe', lambda e, ps=ps, idx=idx, Hb=Hb, c_lo=c_lo: e.matmul(ps[:, 0:BW], lhsT=B["CAL"][:, idx, :], rhs=Hb[:, c_lo + 1: c_lo + 1 + BW],
                                                                              start=False, stop=False),
                                 reads=['B.CAL', 'B.H%d' % (g * 2 + 1)], writes=[pk], signal=False)
                        for s in range(Q):
                            kb = (r - s) * 2 if s < r else ((s - r) * 2 + 1 if s > r else 2 * Q)
                            K.op('pe', lambda e, ps=ps, kb=kb, s=s, c_lo=c_lo: e.matmul(ps[:, 0:BW], lhsT=B["KB"][:, kb, :],
                                                                           rhs=B["uph"][:, s, c_lo:c_lo + BW], start=False, stop=(s == Q - 1)),
                                 reads=['B.KB', 'B.uph'], writes=[pk], signal=(s == Q - 1))
                        K.op('act', lambda e, ps=ps, r=r, c_lo=c_lo: e.activation(out=B["ya"][:, r + Q * c_lo: r + Q * (c_lo + BW - 1) + 1: Q], in_=ps[:, 0:BW], func=AF.Gelu),
                             reads=[pk], writes=['B.ya'])
                K.dma('act', ya_s[gt * 128:(gt + 1) * 128, :], B["ya"], 'B.ya', reads=['B.ya'], writes=['ya_s'])

            Cc = carve([("xp", [T + 3], BF16), ("xc", [T], F32), ("xcb", [T], BF16), ("hf", [T], BF16), ("mk", [T], BF16),
                        ("wst", [4, 128], F32), ("wg", [4, 128], BF16),
                        ("r", [CH], F32), ("gi", [CH], F32), ("a", [CH], F32), ("e2", [CH], F32), ("b", [CH], F32),
                        ("h0", [CH], F32), ("h1", [CH], F32), ("gg", [CH], BF16), ("yb", [CH], BF16), ("mf", [512], F32)])
            K.op('pool', lambda e: e.memset(Cc["xp"], 0.0), writes=['C.xp'])
            K.op('pool', lambda e: e.memset(Cc["wst"], 0.0), writes=['C.wst'])
            for tt in range(NT):
                K.dma('sp', Cc["mf"], maskd[:, 1 + tt * 512: 1 + (tt + 1) * 512], 'C.mf', writes=['C.mf'])
                K.op('act', lambda e, tt=tt: e.copy(out=Cc["mk"][:, tt * 512:(tt + 1) * 512], in_=Cc["mf"]), reads=['C.mf'], writes=['C.mk'])
            for ct in range(8):
                K.dma('sp', Cc["xp"][:, 2:2 + T], xr_s[ct * 128:(ct + 1) * 128, :], 'C.xp', reads=['xr_s'], writes=['C.xp'])
                for d in range(2):
                    for wi, nm in enumerate(("lru_w_a", "lru_w_x")):
                        m = d * 2 + wi
                        for hb in range(2):
                            K.dma('sp', Cc["wst"][hb * 64:(hb + 1) * 64, m, hb * 64:(hb + 1) * 64], W[nm][l, d, 2 * ct + hb], 'C.wst', writes=['C.wst'])
                K.op('act', lambda e: e.copy(out=Cc["wg"], in_=Cc["wst"]), reads=['C.wst'], writes=['C.wg'])
                for cc in range(NCH):
                    sl = slice(cc * CH, (cc + 1) * CH)
                    eng = 'dve'
                    K.op(eng, lambda e, cc=cc, sl=sl: e.tensor_scalar(out=Cc["xc"][:, sl], in0=Cc["xp"][:, cc * CH: cc * CH + CH], scalar1=cwc[:, l, 0, ct:ct + 1],
                                                            scalar2=cbc[:, l, ct:ct + 1], op0=ALU.mult, op1=ALU.add),
                         reads=['C.xp', 'cwc', 'cbc'], writes=['C.xc%d' % cc])
                    for k in range(1, 4):
                        K.op(eng, lambda e, cc=cc, sl=sl, k=k: e.scalar_tensor_tensor(out=Cc["xc"][:, sl], in0=Cc["xp"][:, cc * CH + k: cc * CH + k + CH],
                                                                            scalar=cwc[:, l, k, ct:ct + 1], in1=Cc["xc"][:, sl], op0=ALU.mult, op1=ALU.add),
                             reads=['C.xp', 'cwc', 'C.xc%d' % cc], writes=['C.xc%d' % cc])
                    K.op(eng, lambda e, sl=sl: e.tensor_tensor(out=Cc["xc"][:, sl], in0=Cc["xc"][:, sl], in1=Cc["mk"][:, sl], op=ALU.mult),
                         reads=['C.xc%d' % cc, 'C.mk'], writes=['C.xc%d' % cc])
                    K.op('act', lambda e, sl=sl: e.copy(out=Cc["xcb"][:, sl], in_=Cc["xc"][:, sl]), reads=['C.xc%d' % cc], writes=['C.xcb%d' % cc])
                hctr = 0
                for d in range(2):
                    order = range(NCH) if d == 0 else range(NCH - 1, -1, -1)
                    prev = None
                    for cc in order:
                        sl = slice(cc * CH, (cc + 1) * CH)
                        for wi, (dstn, bcol) in enumerate((("r", bac), ("gi", bxc))):
                            for sb_ in range(CH // 512):
                                ps, pk = nextps()
                                K.op('pe', lambda e, ps=ps, wi=wi, cc=cc, sb_=sb_: e.matmul(ps[:, :], lhsT=Cc["wg"][:, d * 2 + wi, :],
                                                                                 rhs=Cc["xcb"][:, cc * CH + sb_ * 512: cc * CH + (sb_ + 1) * 512], start=True, stop=True),
                                     reads=['C.wg', 'C.xcb%d' % cc], writes=[pk])
                                K.op('act', lambda e, ps=ps, dstn=dstn, bcol=bcol, sb_=sb_: e.activation(out=Cc[dstn][:, sb_ * 512:(sb_ + 1) * 512], in_=ps[:, :], func=AF.Sigmoid,
                                                                                                bias=bcol[:, l, d, ct:ct + 1], scale=1.0),
                                     reads=[pk, 'bac', 'bxc'], writes=['C.' + dstn])
                        K.op('act', lambda e: e.activation(out=Cc["a"], in_=Cc["r"], func=AF.Exp, scale=sc1[:, l, d, ct:ct + 1]), reads=['C.r', 'sc1'], writes=['C.a'])
                        K.op('act', lambda e: e.activation(out=Cc["e2"], in_=Cc["r"], func=AF.Exp, scale=sc2[:, l, d, ct:ct + 1]), reads=['C.r', 'sc2'], writes=['C.e2'])
                        K.op('act', lambda e: e.activation(out=Cc["e2"], in_=Cc["e2"], func=AF.Sqrt, bias=onec, scale=-1.0), reads=['C.e2', 'cst'], writes=['C.e2'])
                        K.op('pool', lambda e: e.tensor_tensor(out=Cc["b"], in0=Cc["gi"], in1=Cc["e2"], op=ALU.mult), reads=['C.gi', 'C.e2'], writes=['C.b'])
                        K.op('pool', lambda e, sl=sl: e.tensor_tensor(out=Cc["b"], in0=Cc["b"], in1=Cc["xc"][:, sl], op=ALU.mult), reads=['C.b', 'C.xc%d' % cc], writes=['C.b'])
                        hn = "h%d" % (hctr % 2); hctr += 1
                        hk = 'C.' + hn
                        if d == 0:
                            init = 0.0 if prev is None else Cc[prev][:, CH - 1:CH]
                            K.op('dve', lambda e, hn=hn, init=init: e.tensor_tensor_scan(out=Cc[hn], data0=Cc["a"], data1=Cc["b"], initial=init, op0=ALU.mult, op1=ALU.add),
                                 reads=['C.a', 'C.b'] + ([] if prev is None else ['C.' + prev]), writes=[hk])
                            K.op('act', lambda e, hn=hn, sl=sl: e.copy(out=Cc["hf"][:, sl], in_=Cc[hn]), reads=[hk], writes=['C.hf'])
                        else:
                            init = 0.0 if prev is None else Cc[prev][:, 0:1]
                            K.op('dve', lambda e, hn=hn, init=init: e.tensor_tensor_scan(out=Cc[hn][:, ::-1], data0=Cc["a"][:, ::-1], data1=Cc["b"][:, ::-1], initial=init,
                                                                               op0=ALU.mult, op1=ALU.add),
                                 reads=['C.a', 'C.b'] + ([] if prev is None else ['C.' + prev]), writes=[hk])
                            K.dma('sp', Cc["gg"], gg_s[ct * 128:(ct + 1) * 128, sl], 'C.gg', reads=['gg_s'], writes=['C.gg'])
                            K.op('pool', lambda e, hn=hn, sl=sl: e.tensor_tensor(out=Cc["b"], in0=Cc[hn], in1=Cc["hf"][:, sl], op=ALU.add), reads=[hk, 'C.hf'], writes=['C.b'])
                            K.op('pool', lambda e: e.tensor_tensor(out=Cc["yb"], in0=Cc["b"], in1=Cc["gg"], op=ALU.mult), reads=['C.b', 'C.gg'], writes=['C.yb'])
                            K.dma('pool', yb_s[ct * 128:(ct + 1) * 128, sl], Cc["yb"], 'C.yb', reads=['C.yb'], writes=['yb_s'])
                        prev = hn

            E_ = carve([("ya", [8, 512], BF16), ("yb", [8, 512], BF16), ("yg", [8, 512], BF16), ("z", [512], BF16),
                        ("sga", [16, 512], BF16), ("sgb", [16, 512], BF16), ("mg", [16, 512], BF16),
                        ("xt", [16, 512], F32), ("mf", [512], F32), ("t1", [512], F32), ("t2", [512], F32)])
            for tt in range(NT):
                cs = slice(tt * 512, (tt + 1) * 512)
                K.dma('sp', E_["ya"], ya_s[:, cs].rearrange("(j p) c -> p j c", p=128), 'E.ya', reads=['ya_s'], writes=['E.ya'])
                K.dma('sp', E_["yb"], yb_s[:, cs].rearrange("(j p) c -> p j c", p=128), 'E.yb', reads=['yb_s'], writes=['E.yb'])
                K.dma('sp', E_["sga"], sga_s[:, cs].rearrange("(j p) c -> p j c", p=128), 'E.sga', reads=['sga_s'], writes=['E.sga'])
                K.dma('sp', E_["sgb"], sgb_s[:, cs].rearrange("(j p) c -> p j c", p=128), 'E.sgb', reads=['sgb_s'], writes=['E.sgb'])
                K.dma('sp', E_["xt"], xs[:, 1 + tt * 512: 1 + (tt + 1) * 512].rearrange("(kt p) c -> p kt c", p=128), 'E.xt', reads=[xs_k], writes=['E.xt'])
                K.dma('sp', E_["mf"], maskd[:, 1 + tt * 512: 1 + (tt + 1) * 512], 'E.mf', writes=['E.mf'])
                for j in range(8):
                    wt, wk = wload("s5_w_glu", l, j, 8)
                    ps, pk = nextps()
                    for kt in range(8):
                        K.op('pe', lambda e, ps=ps, wt=wt, kt=kt: e.matmul(ps[:, :], lhsT=wt[:, kt, :], rhs=E_["ya"][:, kt, :], start=(kt == 0), stop=(kt == 7)),
                             reads=[wk, 'E.ya'], writes=[pk], signal=(kt == 7))
                    K.op('act', lambda e, ps=ps: e.activation(out=E_["z"], in_=ps[:, :], func=AF.Sigmoid), reads=[pk], writes=['E.z'])
                    K.op('dve', lambda e, j=j: e.tensor_tensor(out=E_["yg"][:, j, :], in0=E_["ya"][:, j, :], in1=E_["z"], op=ALU.mult), reads=['E.ya', 'E.z'], writes=['E.yg'])
                for n in range(16):
                    wa, wak = wload("w_proj_a", l, n, 8)
                    wb_, wbk = wload("w_proj_b", l, n, 8)
                    psa, pka = nextps()
                    for kt in range(8):
                        K.op('pe', lambda e, psa=psa, wa=wa, kt=kt: e.matmul(psa[:, :], lhsT=wa[:, kt, :], rhs=E_["yg"][:, kt, :], start=(kt == 0), stop=(kt == 7)),
                             reads=[wak, 'E.yg'], writes=[pka], signal=(kt == 7))
                    psb_, pkb = nextps()
                    for kt in range(8):
                        K.op('pe', lambda e, psb_=psb_, wb_=wb_, kt=kt: e.matmul(psb_[:, :], lhsT=wb_[:, kt, :], rhs=E_["yb"][:, kt, :], start=(kt == 0), stop=(kt == 7)),
                             reads=[wbk, 'E.yb'], writes=[pkb], signal=(kt == 7))
                    K.op('dve', lambda e, psa=psa, n=n: e.tensor_tensor(out=E_["t1"], in0=psa[:, :], in1=E_["sga"][:, n, :], op=ALU.mult), reads=[pka, 'E.sga'], writes=['E.t1'])
                    K.op('dve', lambda e, psb_=psb_, n=n: e.tensor_tensor(out=E_["t2"], in0=psb_[:, :], in1=E_["sgb"][:, n, :], op=ALU.mult), reads=[pkb, 'E.sgb'], writes=['E.t2'])
                    K.op('pool', lambda e, n=n: e.tensor_tensor(out=E_["mg"][:, n, :], in0=E_["t1"], in1=E_["t2"], op=ALU.add), reads=['E.t1', 'E.t2'], writes=['E.mg'])
                for n in range(16):
                    wt, wk = wload("w_out", l, n, 16)
                    ps, pk = nextps()
                    for kt in range(16):
                        K.op('pe', lambda e, ps=ps, wt=wt, kt=kt: e.matmul(ps[:, :], lhsT=wt[:, kt, :], rhs=E_["mg"][:, kt, :], start=(kt == 0), stop=(kt == 15)),
                             reads=[wk, 'E.mg'], writes=[pk], signal=(kt == 15))
                    K.op('dve', lambda e, ps=ps: e.tensor_tensor(out=E_["t1"], in0=ps[:, :], in1=E_["mf"], op=ALU.mult), reads=[pk, 'E.mf'], writes=['E.t1'])
                    K.op('pool', lambda e, n=n: e.tensor_tensor(out=E_["xt"][:, n, :], in0=E_["xt"][:, n, :], in1=E_["t1"], op=ALU.add), reads=['E.t1', 'E.xt'], writes=['E.xt'])
                K.dma('pool', xs[:, 1 + tt * 512: 1 + (tt + 1) * 512].rearrange("(kt p) c -> p kt c", p=128), E_["xt"], 'E.xt', reads=['E.xt'], writes=[xs_k])

            Fh = carve([("xt", [16, 514], F32), ("tmp", [16, 514], BF16), ("rs", [514], F32), ("xn", [16, 514], BF16),
                        ("act", [48, 512], BF16), ("hg", [514], F32), ("hv", [514], F32), ("cg", [512], F32), ("cv", [512], F32),
                        ("wdn0", [48, 128], BF16), ("wdn1", [48, 128], BF16), ("mf", [512], F32), ("t1", [512], F32)])
            wdn = [Fh["wdn0"], Fh["wdn1"]]
            for tt in range(NT):
                c0 = tt * 512
                K.dma('sp', Fh["xt"], xs[:, c0:c0 + 514].rearrange("(kt p) c -> p kt c", p=128), 'F.xt', reads=[xs_k], writes=['F.xt'])
                K.dma('sp', Fh["mf"], maskd[:, 1 + tt * 512: 1 + (tt + 1) * 512], 'F.mf', writes=['F.mf'])
                rmsnorm(Fh["xt"], 'F.xt', 514, g2c[:, l, :], Fh["xn"], 'F.xn', Fh["tmp"], 'F.tmp', Fh["rs"], 'F.rs')
                for f in range(48):
                    res = []
                    for (jj, hn) in ((f, "hg"), (48 + f, "hv")):
                        wt, wk = wload("ffn_w_up", l, jj, 16)
                        ps, pk = nextps()
                        for kt in range(16):
                            K.op('pe', lambda e, ps=ps, wt=wt, kt=kt: e.matmul(ps[:, :], lhsT=wt[:, kt, :], rhs=Fh["xn"][:, kt, 1:513], start=(kt == 0), stop=(kt == 15)),
                                 reads=[wk, 'F.xn'], writes=[pk], signal=(kt == 15))
                        ps2, pk2 = nextps()
                        for kt in range(16):
                            K.op('pe', lambda e, ps2=ps2, wt=wt, kt=kt: e.matmul(ps2[:, 0:2], lhsT=wt[:, kt, :], rhs=Fh["xn"][:, kt, 0:514:513], start=(kt == 0), stop=(kt == 15)),
                                 reads=[wk, 'F.xn'], writes=[pk2], signal=(kt == 15))
                        K.op('act', lambda e, ps=ps, hn=hn: e.copy(out=Fh[hn][:, 1:513], in_=ps[:, :]), reads=[pk], writes=['F.' + hn])
                        K.op('act', lambda e, ps2=ps2, hn=hn: e.copy(out=Fh[hn][:, 0:514:513], in_=ps2[:, 0:2]), reads=[pk2], writes=['F.' + hn])
                    for (hn, cn, jj, eng) in (("hg", "cg", f, 'dve'), ("hv", "cv", 48 + f, 'dve')):
                        K.op(eng, lambda e, hn=hn, cn=cn, jj=jj: e.tensor_scalar_mul(out=Fh[cn], in0=Fh[hn][:, 0:512], scalar1=fcw[:, l, 0, jj:jj + 1]),
                             reads=['F.' + hn, 'fcw'], writes=['F.' + cn])
                        for k in (1, 2):
                            K.op(eng, lambda e, hn=hn, cn=cn, jj=jj, k=k: e.scalar_tensor_tensor(out=Fh[cn], in0=Fh[hn][:, k:k + 512], scalar=fcw[:, l, k, jj:jj + 1],
                                                                                   in1=Fh[cn], op0=ALU.mult, op1=ALU.add),
                                 reads=['F.' + hn, 'fcw', 'F.' + cn], writes=['F.' + cn])
                    K.op('act', lambda e: e.activation(out=Fh["cg"], in_=Fh["cg"], func=AF.Gelu), reads=['F.cg'], writes=['F.cg'])
                    K.op('dve', lambda e, f=f: e.tensor_tensor(out=Fh["act"][:, f, :], in0=Fh["cg"], in1=Fh["cv"], op=ALU.mult), reads=['F.cg', 'F.cv'], writes=['F.act'])
                for n in range(16):
                    wi = n % 2
                    wdk = 'wdn%d' % wi
                    K.dma('sp', wdn[wi], WT["ffn_w_down"][l, n], wdk, reads=[('wt', 'ffn_w_down', l)], writes=[wdk])
                    ps, pk = nextps()
                    for kt in range(48):
                        K.op('pe', lambda e, ps=ps, wi=wi, kt=kt: e.matmul(ps[:, :], lhsT=wdn[wi][:, kt, :], rhs=Fh["act"][:, kt, :], start=(kt == 0), stop=(kt == 47)),
                             reads=[wdk, 'F.act'], writes=[pk], signal=(kt == 47))
                    K.op('dve', lambda e, ps=ps: e.tensor_tensor(out=Fh["t1"], in0=ps[:, :], in1=Fh["mf"], op=ALU.mult), reads=[pk, 'F.mf'], writes=['F.t1'])
                    K.op('pool', lambda e, n=n: e.tensor_tensor(out=Fh["xt"][:, n, 1:513], in0=Fh["xt"][:, n, 1:513], in1=Fh["t1"], op=ALU.add), reads=['F.t1', 'F.xt'], writes=['F.xt'])
                K.dma('pool', xs2[:, 1 + tt * 512: 1 + (tt + 1) * 512].rearrange("(kt p) c -> p kt c", p=128), Fh["xt"][:, :, 1:513], 'F.xt', reads=['F.xt'], writes=[xs2_k])
            xs, xs2 = xs2, xs
            xs_k, xs2_k = xs2_k, xs_k
        G = carve([("xt0", [16, 512], F32), ("xt1", [16, 512], F32), ("tmp", [16, 512], BF16), ("rs", [512], F32),
                   ("o0", [16, 512], F32), ("o1", [16, 512], F32)])
        for tt in range(NT):
            xt = G["xt%d" % (tt % 2)]; xkey = 'G.xt%d' % (tt % 2)
            o = G["o%d" % (tt % 2)]; okey = 'G.o%d' % (tt % 2)
            K.dma('sp', xt, xs[:, 1 + tt * 512: 1 + (tt + 1) * 512].rearrange("(kt p) c -> p kt c", p=128), xkey, reads=[xs_k], writes=[xkey])
            rmsnorm(xt, xkey, 512, gfc[:, :], None, okey, G["tmp"], 'G.tmp', G["rs"], 'G.rs', out_f32=o)
            K.dma('sp', yT[:, tt * 512:(tt + 1) * 512].rearrange("(kt p) c -> p kt c", p=128), o, okey, reads=[okey], writes=['yT'])
        K.finish()
        print("instructions emitted:", K.nins, "sems:", len(K.sem))
    return nc


WNAMES = ["norm1_g", "w_in", "s5_a_re", "s5_a_im", "s5_log_dt", "s5_b_re", "s5_b_im", "s5_c_re", "s5_c_im", "s5_d",
          "s5_w_glu", "lru_conv_w", "lru_conv_b", "lru_w_a", "lru_b_a", "lru_w_x", "lru_b_x", "lru_lambda",
          "w_proj_a", "w_proj_b", "w_out", "norm2_g", "ffn_w_up", "ffn_conv_w", "ffn_w_down", "final_g"]


def run_seqs(seqs, weights, T, L, n_cores, debug=False):
    nc = build(T, L, debug)
    consts = make_consts()
    SMALL = {"norm1_g", "norm2_g", "final_g", "s5_d", "lru_conv_w", "lru_conv_b", "lru_b_a", "lru_b_x", "lru_lambda", "ffn_conv_w"}
    wd = {k: np.ascontiguousarray(np.asarray(weights[k], np.float32)) for k in WNAMES if k not in SMALL}
    wd["smalls"] = make_smalls(weights, L)
    in_maps = []
    for sq in seqs:
        S = sq.shape[0]
        xT = np.zeros((D, T + 2), np.float32)
        xT[:, 1:1 + S] = sq.T
        mask = np.zeros((128, T + 2), np.float32)
        mask[:, 1:1 + S] = 1.0
        m = {"xT": xT, "mask": mask, "consts": consts}
        m.update(wd)
        in_maps.append(m)
    res = run_bass_kernel_spmd(nc, in_maps, core_ids=list(range(n_cores)))
    if debug:
        return res.results
    outs = []
    for sq, r in zip(seqs, res.results):
        S = sq.shape[0]
        outs.append(np.ascontiguousarray(np.asarray(r["yT"])[:, :S].T).astype(np.float32))
    return outs


def kernel(**inputs):
    xp = np.asarray(inputs["x_prompt"], np.float32)
    xsm = np.asarray(inputs["x_sample"], np.float32)
    seqs = [xp[0]] + [xsm[i] for i in range(4)] + [xsm[3]] * 3
    outs = run_seqs(seqs, inputs, 8192, DEPTH, 8)
    y_prompt = outs[0][None]
    y_sample = np.stack(outs[1:5], axis=0)
    return (y_prompt, y_sample)
```

```python
import math
from contextlib import ExitStack
import numpy as np
import ml_dtypes
import concourse.bass as bass
import concourse.mybir as mybir
from concourse.bass_utils import run_bass_kernel_spmd

F32, BF16 = mybir.dt.float32, mybir.dt.bfloat16
AF, ALU = mybir.ActivationFunctionType, mybir.AluOpType

D = 2048; DEPTH = 4; S5W = 1024; LRW = 1024; FH = 6144; INC = 7168
Q = 8
EPS = 1e-6
ARENA = 168 * 1024
C_ID, C_SW, C_BD, C_RM, C_SGN, C_NSG, C_ONE, C_NPI, C_DEPS, C_CM = 0, 128, 256, 384, 392, 393, 394, 395, 396, 400
NCONST = 400 + 1024


def make_consts():
    c = np.zeros((128, NCONST), np.float32)
    k = np.arange(128)
    c[k, C_ID + k] = 1.0
    c[k, C_SW + (k + 64) % 128] = 1.0
    c[:, C_BD:C_BD + 128] = (k[:, None] // 16 == k[None, :] // 16)
    c[:, C_RM:C_RM + 8] = (k[:, None] // 16 == np.arange(8)[None, :])
    c[:, C_SGN] = np.where(k < 64, -1.0, 1.0)
    c[:, C_NSG] = np.where(k < 64, 1.0, -1.0)
    c[:, C_ONE] = 1.0
    c[:, C_NPI] = -math.pi
    c[:, C_DEPS] = D * EPS
    for g in range(8):
        c[:, C_CM + g * 128:C_CM + (g + 1) * 128] = (k[None, :] // 16 == g)
    return c


def small_cols(L):
    return L * 16 * 2 + 16 + L * 8 + L * 32 + L * 8 + 3 * L * 16 + L * 288


def _pl(v, kt):
    v = np.asarray(v, np.float32)
    v = v.reshape(v.shape[:-1] + (kt, 128))
    return np.ascontiguousarray(np.moveaxis(v, -1, 0)).reshape(128, -1)


def make_smalls(w, L):
    parts = [_pl(w["norm1_g"][:L], 16), _pl(w["norm2_g"][:L], 16), _pl(w["final_g"], 16), _pl(w["s5_d"][:L], 8),
             _pl(w["lru_conv_w"][:L], 8), _pl(w["lru_conv_b"][:L], 8), _pl(w["lru_b_a"][:L], 8), _pl(w["lru_b_x"][:L], 8),
             _pl(w["lru_lambda"][:L], 8), _pl(w["ffn_conv_w"][:L], 96)]
    out = np.ascontiguousarray(np.concatenate(parts, axis=1))
    assert out.shape == (128, small_cols(L)), out.shape
    return out


class Ctx:
    def __init__(s, nc, stack):
        s.nc, s.stack = nc, stack
        s.eng = {'pe': nc.tensor, 'act': nc.scalar, 'dve': nc.vector, 'pool': nc.gpsimd, 'sp': nc.sync}
        s.sem, s.cnt, s.st = {}, {}, {}
        s.seen = {e: {} for e in s.eng}
        s.nins = 0

    def getsem(s, key):
        if key not in s.sem:
            s.sem[key] = s.stack.enter_context(s.nc.semaphore("sm%d" % len(s.sem)))
            s.cnt[key] = 0
        return s.sem[key]

    def _wait(s, e, key, val):
        if e == 'pe' and key == 'pe':
            return
        if s.seen[e].get(key, 0) >= val:
            return
        s.eng[e].wait_ge(s.sem[key], val)
        s.seen[e][key] = val

    def _deps(s, e, reads, writes):
        for k in reads:
            for key, val in s.st.setdefault(k, ({}, {}))[0].items():
                s._wait(e, key, val)
        for k in writes:
            w, r = s.st.setdefault(k, ({}, {}))
            for key, val in w.items():
                s._wait(e, key, val)
            for key, val in r.items():
                s._wait(e, key, val)

    def _mark(s, reads, writes, key, val):
        for k in reads:
            r = s.st[k][1]
            r[key] = max(r.get(key, 0), val)
        for k in writes:
            w = s.st[k][0]
            w[key] = max(w.get(key, 0), val)

    def op(s, e, fn, reads=(), writes=(), signal=True):
        s._deps(e, reads, writes)
        ins = fn(s.eng[e])
        s.getsem(e)
        if signal:
            s.cnt[e] += 1
            ins.then_inc(s.sem[e], 1)
            val = s.cnt[e]
        else:
            val = s.cnt[e] + 1
        s._mark(reads, writes, e, val)
        s.nins += 1

    def dma(s, q, out, in_, semkey, reads=(), writes=(), **kw):
        s._deps(q, reads, writes)
        s.getsem(semkey)
        ins = s.eng[q].dma_start(out=out, in_=in_, **kw)
        s.cnt[semkey] += 16
        ins.then_inc(s.sem[semkey], 16)
        s._mark(reads, writes, semkey, s.cnt[semkey])
        s.nins += 1

    def barrier(s):
        for e in s.eng:
            for key, val in s.cnt.items():
                if val > 0 and not (isinstance(key, str) and key.startswith('wconv')):
                    s._wait(e, key, val)

    def finish(s):
        for key, val in s.cnt.items():
            if val > 0:
                s._wait('sp', key, val)


def build(T, L, debug=False):
    assert T % 512 == 0
    NT = T // 512
    NC_ = T // Q
    BW = min(512, NC_)
    NB = NC_ // BW
    NL = int(round(math.log2(NC_)))
    assert 2 ** NL == NC_
    HP = max(BW, NC_ // 2)
    CH = min(2048, T)
    NCH = T // CH
    nc = bass.Bass("TRN2", target_bir_lowering=False)
    def dr(name, shape, dt, kind="ExternalInput"):
        if debug and kind == "Internal" and not name.endswith("_t"):
            kind = "ExternalOutput"
        return nc.dram_tensor(name, list(shape), dt, kind=kind).ap()
    xT = dr("xT", [D, T + 2], F32)
    maskd = dr("mask", [128, T + 2], F32)
    constd = dr("consts", [128, NCONST], F32)
    smd = dr("smalls", [128, small_cols(L)], F32)
    W = {}
    for name, shape in [("w_in", [L, D, INC]), ("s5_a_re", [L, 2, 64, 64]), ("s5_a_im", [L, 2, 64, 64]),
                        ("s5_log_dt", [L, 2, 64]), ("s5_b_re", [L, 2, 64, 64, 16]), ("s5_b_im", [L, 2, 64, 64, 16]),
                        ("s5_c_re", [L, 2, 64, 16, 64]), ("s5_c_im", [L, 2, 64, 16, 64]),
                        ("s5_w_glu", [L, S5W, S5W]),
                        ("lru_w_a", [L, 2, 16, 64, 64]), ("lru_w_x", [L, 2, 16, 64, 64]),
                        ("w_proj_a", [L, S5W, D]),
                        ("w_proj_b", [L, LRW, D]), ("w_out", [L, D, D]), ("ffn_w_up", [L, D, 2 * FH]),
                        ("ffn_w_down", [L, FH, D])]:
        W[name] = dr(name, shape, F32)
    yT = dr("yT", [D, T], F32, kind="ExternalOutput")
    xs = dr("x_scr", [D, T + 2], F32, "Internal")
    xs2 = dr("x_scr2", [D, T + 2], F32, "Internal")
    xs_k, xs2_k = 'xs', 'xs2'
    u_s = dr("u_scr", [S5W, T], BF16, "Internal")
    xr_s = dr("xr_scr", [LRW, T], BF16, "Internal")
    gg_s = dr("gg_scr", [LRW, T], BF16, "Internal")
    sga_s = dr("sga_scr", [D, T], BF16, "Internal")
    sgb_s = dr("sgb_scr", [D, T], BF16, "Internal")
    ya_s = dr("ya_scr", [S5W, T], BF16, "Internal")
    yb_s = dr("yb_scr", [LRW, T], BF16, "Internal")
    WT = {}
    for name, (J, KT) in {"w_in": (56, 16), "s5_w_glu": (8, 8), "w_proj_a": (16, 8), "w_proj_b": (16, 8),
                          "w_out": (16, 16), "ffn_w_up": (96, 16), "ffn_w_down": (16, 48)}.items():
        WT[name] = dr(name + "_t", [L, J, 128, KT, 128], BF16, "Internal")

    with ExitStack() as stack:
        sb = lambda name, shape, dt: stack.enter_context(nc.sbuf_tensor(name, list(shape), dt))
        K = Ctx(nc, stack)
        cst = sb("cst", [128, NCONST], F32)
        idb = sb("idb", [128, 128], BF16); swb = sb("swb", [128, 128], BF16); oneb = sb("oneb", [128, 128], BF16)
        g1c = sb("g1c", [128, L, 16], F32); g2c = sb("g2c", [128, L, 16], F32); gfc = sb("gfc", [128, 16], F32)
        s5dc = sb("s5dc", [128, L, 8], F32)
        cwc = sb("cwc", [128, L, 4, 8], F32); cbc = sb("cbc", [128, L, 8], F32)
        bac = sb("bac", [128, L, 2, 8], F32); bxc = sb("bxc", [128, L, 2, 8], F32)
        lamc = sb("lamc", [128, L, 2, 8], F32); sc1 = sb("sc1", [128, L, 2, 8], F32); sc2 = sb("sc2", [128, L, 2, 8], F32)
        fcw = sb("fcw", [128, L, 3, 96], F32)
        NWS = 6
        wslot = [sb("wslot%d" % i, [128, 16, 128], BF16) for i in range(NWS)]
        big = sb("big", [128, ARENA // 4], F32)
        psb = [stack.enter_context(nc.psum_tensor("ps%d" % i, [128, 512], F32)) for i in range(8)]
        psk = ["ps%d" % i for i in range(8)]

        def carve(specs):
            K.barrier()
            off = 0
            out = {}
            for name, shape, dt in specs:
                n = int(np.prod(shape))
                nb = n * (4 if dt == F32 else 2)
                nb4 = (nb + 3) // 4
                ap = big[:, off:off + nb4]
                if dt != F32:
                    ap = ap.bitcast(BF16)[:, 0:n]
                if len(shape) > 1:
                    names = ["d%d" % i for i in range(len(shape))]
                    ap = ap.rearrange("p (%s) -> p %s" % (" ".join(names), " ".join(names)),
                                      **{names[i]: shape[i] for i in range(len(shape) - 1)})
                out[name] = ap
                off += nb4
            assert off * 4 <= ARENA, off * 4
            return out

        K.dma('sp', cst[:], constd[:, :], 'cst', writes=['cst'])
        K.op('act', lambda e: e.copy(out=idb[:], in_=cst[:, C_ID:C_ID + 128]), reads=['cst'], writes=['idb'])
        K.op('act', lambda e: e.copy(out=swb[:], in_=cst[:, C_SW:C_SW + 128]), reads=['cst'], writes=['swb'])
        K.op('dve', lambda e: e.memset(oneb[:], 1.0), writes=['oneb'])
        onec = cst[:, C_ONE:C_ONE + 1]
        sgnc = cst[:, C_SGN:C_SGN + 1]
        nsgc = cst[:, C_NSG:C_NSG + 1]
        npic = cst[:, C_NPI:C_NPI + 1]

        off_ = 0
        for tname, tl in (("g1c", g1c), ("g2c", g2c), ("gfc", gfc), ("s5dc", s5dc), ("cwc", cwc), ("cbc", cbc),
                          ("bac", bac), ("bxc", bxc), ("lamc", lamc), ("fcw", fcw)):
            n = int(np.prod(tl.shape[1:]))
            dst = tl[:]
            if len(tl.shape) > 2:
                names = ["d%d" % i for i in range(len(tl.shape) - 1)]
                dst = dst.rearrange("p %s -> p (%s)" % (" ".join(names), " ".join(names)))
            K.dma('sp', dst, smd[:, off_:off_ + n], tname, writes=[tname])
            off_ += n
        assert off_ == small_cols(L), (off_, small_cols(L))
        sD = math.sqrt(D)
        K.op('act', lambda e: e.mul(out=g1c[:], in_=g1c[:], mul=sD), reads=['g1c'], writes=['g1c'])
        K.op('act', lambda e: e.mul(out=g2c[:], in_=g2c[:], mul=sD), reads=['g2c'], writes=['g2c'])
        K.op('act', lambda e: e.mul(out=gfc[:], in_=gfc[:], mul=sD), reads=['gfc'], writes=['gfc'])
        K.op('act', lambda e: e.activation(out=sc1[:], in_=lamc[:], func=AF.Exp, scale=-1.0), reads=['lamc'], writes=['sc1'])
        K.op('act', lambda e: e.activation(out=sc1[:], in_=sc1[:], func=AF.Ln, bias=onec, scale=1.0), reads=['sc1', 'cst'], writes=['sc1'])
        K.op('act', lambda e: e.mul(out=sc2[:], in_=sc1[:], mul=-16.0), reads=['sc1'], writes=['sc2'])
        K.op('act', lambda e: e.mul(out=sc1[:], in_=sc1[:], mul=-8.0), reads=['sc1', 'sc2'], writes=['sc1'])

        K.dma('sp', xs[:, :], xT[:, :], 'xcopy', writes=['xs'])
        K.dma('sp', xs2[:, :], xT[:, :], 'xcopy', writes=['xs2'])
        CONV = [("w_in", 56, 16), ("s5_w_glu", 8, 8), ("w_proj_a", 16, 8), ("w_proj_b", 16, 8), ("w_out", 16, 16),
                ("ffn_w_up", 96, 16), ("ffn_w_down", 16, 48)]

        def conv_layer(l):
            for name, J, KT in CONV:
                for j in range(J):
                    src = W[name][l][:, j * 128:(j + 1) * 128].rearrange("(kt p) n -> p kt n", p=128)
                    K.dma('pool', WT[name][l, j], src, 'wconv_%s_%d' % (name, l), writes=[('wt', name, l)])
        for _l in range(L):
            conv_layer(_l)

        wctr = [0]

        def wload(name, l, j, KT):
            i = wctr[0] % NWS
            wctr[0] += 1
            key = 'wslot%d' % i
            K.dma('sp', wslot[i][:, 0:KT, :], WT[name][l, j], key, reads=[('wt', name, l)], writes=[key])
            return wslot[i], key

        pctr = [0]

        def nextps():
            i = pctr[0] % 8
            pctr[0] += 1
            return psb[i], psk[i]

        def rmsnorm(xt, xkey, ncols, gcol, xn, xnkey, tmp, tmpkey, rs, rskey, out_f32=None):
            K.op('act', lambda e: e.activation(out=tmp[:, :, 0:ncols], in_=xt[:, :, 0:ncols], func=AF.Square),
                 reads=[xkey], writes=[tmpkey])
            c0 = 0
            while c0 < ncols:
                cw = min(512, ncols - c0)
                ps, pk = nextps()
                for kt in range(16):
                    K.op('pe', lambda e, kt=kt, c0=c0, cw=cw, ps=ps: e.matmul(ps[:, 0:cw], lhsT=oneb[:], rhs=tmp[:, kt, c0:c0 + cw],
                                                                   start=(kt == 0), stop=(kt == 15)),
                         reads=[tmpkey, 'oneb'], writes=[pk], signal=(kt == 15))
                K.op('act', lambda e, c0=c0, cw=cw, ps=ps: e.activation(out=rs[:, c0:c0 + cw], in_=ps[:, 0:cw], func=AF.Ln,
                                                                 bias=cst[:, C_DEPS:C_DEPS + 1], scale=1.0),
                     reads=[pk, 'cst'], writes=[rskey])
                K.op('act', lambda e, c0=c0, cw=cw: e.activation(out=rs[:, c0:c0 + cw], in_=rs[:, c0:c0 + cw], func=AF.Exp, scale=-0.5),
                     reads=[rskey], writes=[rskey])
                c0 += cw
            for kt in range(16):
                eng = 'dve'
                dst = xn if out_f32 is None else out_f32
                K.op(eng, lambda e, kt=kt, dst=dst: e.scalar_tensor_tensor(out=dst[:, kt, 0:ncols], in0=xt[:, kt, 0:ncols], scalar=gcol[:, kt:kt + 1],
                                                                  in1=rs[:, 0:ncols], op0=ALU.mult, op1=ALU.mult),
                     reads=[xkey, rskey, 'g1c', 'g2c', 'gfc'], writes=[xnkey])

        for l in range(L):
            A = carve([("xt0", [16, 512], F32), ("xt1", [16, 512], F32), ("tmp", [16, 512], BF16), ("rs", [512], F32),
                       ("xn", [16, 512], BF16), ("og0", [8, 512], BF16), ("og1", [8, 512], BF16)])
            for tt in range(NT):
                c0 = 1 + tt * 512
                xt = A["xt%d" % (tt % 2)]; xkey = 'A.xt%d' % (tt % 2)
                K.dma('sp', xt, xs[:, c0:c0 + 512].rearrange("(kt p) c -> p kt c", p=128), xkey, reads=[xs_k], writes=[xkey])
                rmsnorm(xt, xkey, 512, g1c[:, l, :], A["xn"], 'A.xn', A["tmp"], 'A.tmp', A["rs"], 'A.rs')
                for jg in range(7):
                    og = A["og%d" % (jg % 2)]; ogk = 'A.og%d' % (jg % 2)
                    for jj in range(8):
                        j = jg * 8 + jj
                        wt, wk = wload("w_in", l, j, 16)
                        ps, pk = nextps()
                        for kt in range(16):
                            K.op('pe', lambda e, kt=kt, wt=wt, ps=ps: e.matmul(ps[:, :], lhsT=wt[:, kt, :], rhs=A["xn"][:, kt, :],
                                                                      start=(kt == 0), stop=(kt == 15)),
                                 reads=[wk, 'A.xn'], writes=[pk], signal=(kt == 15))
                        func = AF.Copy if jg < 2 else (AF.Gelu if jg == 2 else AF.Sigmoid)
                        K.op('act', lambda e, jj=jj, og=og, ps=ps, func=func: e.activation(out=og[:, jj, :], in_=ps[:, :], func=func),
                             reads=[pk], writes=[ogk])
                    dst = [u_s, xr_s, gg_s, sga_s[0:1024], sga_s[1024:2048], sgb_s[0:1024], sgb_s[1024:2048]][jg]
                    dkey = ['u_s', 'xr_s', 'gg_s', 'sga_s', 'sga_s', 'sgb_s', 'sgb_s'][jg]
                    K.dma('act', dst[:, tt * 512:(tt + 1) * 512].rearrange("(j p) c -> p j c", p=128), og, ogk, reads=[ogk], writes=[dkey])

            NS0 = 4
            B = carve([("uph", [Q, NC_], BF16), ("ya", [T], BF16),
                       ("H", [20, HP + NC_], BF16),
                       ("CAL", [8 * Q * 2, 128], BF16),
                       ("S0L", [NS0, Q, 128], BF16), ("RP", [8, 128], BF16), ("RT", [8, 128], BF16), ("KB", [2 * Q + 1, 128], BF16),
                       ("E", [2, Q, 128], BF16), ("CA", [2, Q + 1, 128], BF16),
                       ("b1", [128], F32), ("b2", [128], F32), ("c1", [128], F32), ("c2", [128], F32),
                       ("nat", [128], F32), ("nat2", [128], F32),
                       ("sm", [40, 8], F32), ("pw", [2, Q + 1, 2, 8], F32), ("zc", [2, Q, 2, 8], F32), ("rp", [2, NL, 2, 8], F32),
                       ("t1", [128], F32), ("t2", [128], F32)])
            K.op('pool', lambda e: e.memset(B["H"][:, :, :], 0.0), writes=['B.H%d' % i for i in range(20)])
            SM = lambda i: B["sm"][:, i, :]
            PE_ = 'dve'

            def tt_(out, a, b, op, rk, wk):
                K.op(PE_, lambda e: e.tensor_tensor(out=out, in0=a, in1=b, op=op), reads=rk, writes=wk)

            def cmul(ore, oim, are, aim, bre, bim, rk, wk):
                tt_(SM(38), are, bre, ALU.mult, rk, ['B.t38'])
                tt_(SM(39), aim, bim, ALU.mult, rk, ['B.t39'])
                tt_(SM(37), are, bim, ALU.mult, rk, ['B.t37'])
                tt_(SM(36), aim, bre, ALU.mult, rk, ['B.t36'])
                tt_(ore, SM(38), SM(39), ALU.subtract, ['B.t38', 'B.t39'], wk)
                tt_(oim, SM(37), SM(36), ALU.add, ['B.t37', 'B.t36'], wk)

            rpc = [0]
            for gt in range(8):
                g0 = gt * 8
                K.dma('sp', B["ya"], u_s[gt * 128:(gt + 1) * 128, :], 'B.ya', reads=['u_s'], writes=['B.ya'])
                K.op('dve', lambda e: e.tensor_copy(out=B["uph"], in_=B["ya"].rearrange("p (c s) -> p s c", s=Q)), reads=['B.ya'], writes=['B.uph'])
                for d in range(2):
                    for nm, dst in (("s5_a_re", 0), ("s5_a_im", 1)):
                        for h in range(2):
                            K.dma('sp', B["nat"][0:8, h * 64:(h + 1) * 64], W[nm][l, d, g0:g0 + 8, :], 'B.nat', writes=['B.nat'])
                        ps, pk = nextps()
                        K.op('pe', lambda e, ps=ps: e.transpose(out=ps[:, 0:8], in_=B["nat"][0:8, :], identity=cst[0:8, C_ID:C_ID + 8]),
                             reads=['B.nat', 'cst'], writes=[pk])
                        K.op('act', lambda e, ps=ps, dst=dst: e.copy(out=SM(dst), in_=ps[:, 0:8]), reads=[pk], writes=['B.sm%d' % dst])
                    K.dma('sp', SM(2), W["s5_log_dt"][l, d, g0:g0 + 8].partition_broadcast(128), 'B.sm2', writes=['B.sm2'])
                    K.op('act', lambda e: e.activation(out=SM(2), in_=SM(2), func=AF.Exp), reads=['B.sm2'], writes=['B.sm2'])
                    k0, k1, k2 = ['B.sm0'], ['B.sm1'], ['B.sm2']
                    tt_(SM(3), SM(0), SM(2), ALU.mult, k0 + k2, ['B.sm3'])
                    tt_(SM(4), SM(1), SM(2), ALU.mult, k1 + k2, ['B.sm4'])
                    K.op('act', lambda e: e.activation(out=SM(5), in_=SM(3), func=AF.Exp), reads=['B.sm3'], writes=['B.sm5'])
                    K.op('act', lambda e: e.activation(out=SM(6), in_=SM(4), func=AF.Sin, scale=1.0 / 16), reads=['B.sm4'], writes=['B.sm6'])
                    K.op('act', lambda e: e.activation(out=SM(7), in_=SM(4), func=AF.Sin, scale=1.0 / 32), reads=['B.sm4'], writes=['B.sm7'])
                    tt_(SM(7), SM(7), SM(7), ALU.mult, ['B.sm7'], ['B.sm7'])
                    K.op(PE_, lambda e: e.tensor_scalar(out=SM(7), in0=SM(7), scalar1=-2.0, scalar2=1.0, op0=ALU.mult, op1=ALU.add),
                         reads=['B.sm7'], writes=['B.sm7'])
                    for _sq in range(4):
                        cmul(SM(7), SM(6), SM(7), SM(6), SM(7), SM(6), ['B.sm7', 'B.sm6'], ['B.sm7', 'B.sm6'])
                    PW = lambda k, c: B["pw"][:, d, k, c, :]
                    pwk = lambda k: 'B.pw%d_%d' % (d, k)
                    tt_(PW(1, 0), SM(5), SM(7), ALU.mult, ['B.sm5', 'B.sm7'], [pwk(1)])
                    tt_(PW(1, 1), SM(5), SM(6), ALU.mult, ['B.sm5', 'B.sm6'], [pwk(1)])
                    K.op(PE_, lambda e: e.memset(PW(0, 0), 1.0), writes=[pwk(0)])
                    K.op(PE_, lambda e: e.memset(PW(0, 1), 0.0), writes=[pwk(0)])
                    K.op(PE_, lambda e: e.tensor_scalar_add(out=SM(8), in0=PW(1, 0), scalar1=-1.0), reads=[pwk(1)], writes=['B.sm8'])
                    tt_(SM(9), SM(0), SM(0), ALU.mult, k0, ['B.sm9'])
                    tt_(SM(10), SM(1), SM(1), ALU.mult, k1, ['B.sm10'])
                    tt_(SM(9), SM(9), SM(10), ALU.add, ['B.sm9', 'B.sm10'], ['B.sm9'])
                    K.op(PE_, lambda e: e.reciprocal(out=SM(9), in_=SM(9)), reads=['B.sm9'], writes=['B.sm9'])
                    tt_(SM(11), SM(0), SM(9), ALU.mult, k0 + ['B.sm9'], ['B.sm11'])
                    tt_(SM(12), SM(1), SM(9), ALU.mult, k1 + ['B.sm9'], ['B.sm12'])
                    K.op(PE_, lambda e: e.tensor_scalar_mul(out=SM(12), in0=SM(12), scalar1=-1.0), reads=['B.sm12'], writes=['B.sm12'])
                    cmul(SM(13), SM(14), SM(8), PW(1, 1), SM(11), SM(12), ['B.sm8', pwk(1), 'B.sm11', 'B.sm12'], ['B.coef'])
                    for k in range(2, Q + 1):
                        cmul(PW(k, 0), PW(k, 1), PW(k - 1, 0), PW(k - 1, 1), PW(1, 0), PW(1, 1), [pwk(k - 1), pwk(1)], [pwk(k)])
                    ZC = lambda k, c: B["zc"][:, d, k, c, :]
                    for k in range(Q):
                        cmul(ZC(k, 0), ZC(k, 1), PW(k, 0), PW(k, 1), SM(13), SM(14), [pwk(k), 'B.coef'], ['B.zc%d' % d])
                        K.op(PE_, lambda e, k=k: e.tensor_scalar_mul(out=ZC(k, 1), in0=ZC(k, 1), scalar1=sgnc), reads=['B.zc%d' % d, 'cst'], writes=['B.zc%d' % d])
                    RPc = lambda lv, c: B["rp"][:, d, lv, c, :]
                    K.op(PE_, lambda e: e.tensor_copy(out=RPc(0, 0), in_=PW(Q, 0)), reads=[pwk(Q)], writes=['B.rp%d' % d])
                    K.op(PE_, lambda e: e.tensor_copy(out=RPc(0, 1), in_=PW(Q, 1)), reads=[pwk(Q)], writes=['B.rp%d' % d])
                    for lv in range(1, NL):
                        cmul(RPc(lv, 0), RPc(lv, 1), RPc(lv - 1, 0), RPc(lv - 1, 1), RPc(lv - 1, 0), RPc(lv - 1, 1), ['B.rp%d' % d], ['B.rp%d' % d])
                    K.op(PE_, lambda e: e.tensor_scalar_mul(out=B["rp"][:, d, :, 1, :], in0=B["rp"][:, d, :, 1, :], scalar1=nsgc),
                         reads=['B.rp%d' % d, 'cst'], writes=['B.rp%d' % d])
                    bre = W["s5_b_re"][l, d, g0:g0 + 8].rearrange("g p j -> p g j")
                    bim = W["s5_b_im"][l, d, g0:g0 + 8].rearrange("g p j -> p g j")
                    v3 = lambda ap: ap.rearrange("p (g j) -> p g j", g=8)
                    K.dma('sp', v3(B["b1"])[0:64], bre, 'B.b1', writes=['B.b1'])
                    K.dma('sp', v3(B["b1"])[64:128], bim, 'B.b1', writes=['B.b1'])
                    K.dma('sp', v3(B["b2"])[0:64], bim, 'B.b2', writes=['B.b2'])
                    K.dma('sp', v3(B["b2"])[64:128], bre, 'B.b2', writes=['B.b2'])
                    cre = W["s5_c_re"][l, d, g0:g0 + 8].rearrange("g i p -> (g i) p")
                    cim = W["s5_c_im"][l, d, g0:g0 + 8].rearrange("g i p -> (g i) p")
                    for (first, second, dst, nat, nk) in ((cre, cim, "c1", "nat", 'B.nat'), (cim, cre, "c2", "nat2", 'B.nat2')):
                        K.dma('sp', B[nat][:, 0:64], first, nk, writes=[nk])
                        K.dma('sp', B[nat][:, 64:128], second, nk, writes=[nk])
                        ps, pk = nextps()
                        K.op('pe', lambda e, ps=ps, nat=nat: e.transpose(out=ps[:, 0:128], in_=B[nat][:, :], identity=cst[:, C_ID:C_ID + 128]),
                             reads=[nk, 'cst'], writes=[pk])
                        K.op('act', lambda e, ps=ps, dst=dst: e.copy(out=B[dst], in_=ps[:, 0:128]), reads=[pk], writes=['B.' + dst])
                    bc = lambda col: col.unsqueeze(2).to_broadcast([128, 8, 16])
                    for k in range(Q):
                        tt_(v3(B["t1"]), v3(B["b1"]), bc(ZC(k, 0)), ALU.mult, ['B.b1', 'B.zc%d' % d], ['B.t1'])
                        tt_(v3(B["t2"]), v3(B["b2"]), bc(ZC(k, 1)), ALU.mult, ['B.b2', 'B.zc%d' % d], ['B.t2'])
                        tt_(B["E"][:, d, k, :], B["t1"], B["t2"], ALU.add, ['B.t1', 'B.t2'], ['B.E%d' % d])
                    for tau in range(Q + 1):
                        K.op(PE_, lambda e, tau=tau: e.tensor_scalar_mul(out=SM(15), in0=PW(tau, 0), scalar1=nsgc), reads=[pwk(tau), 'cst'], writes=['B.sm15'])
                        tt_(v3(B["t1"]), v3(B["c1"]), bc(SM(15)), ALU.mult, ['B.c1', 'B.sm15'], ['B.t1'])
                        tt_(v3(B["t2"]), v3(B["c2"]), bc(PW(tau, 1)), ALU.mult, ['B.c2', pwk(tau)], ['B.t2'])
                        tt_(B["CA"][:, d, tau, :], B["t1"], B["t2"], ALU.subtract, ['B.t1', 'B.t2'], ['B.CA%d' % d])
                    for tau in range(1, Q + 1):
                        for g in range(8):
                            idx = (g * Q + (tau - 1)) * 2 + d
                            K.op('pool', lambda e, idx=idx, tau=tau, g=g: e.tensor_tensor(out=B["CAL"][:, idx, :], in0=B["CA"][:, d, tau, :],
                                                                                   in1=cst[:, C_CM + g * 128:C_CM + (g + 1) * 128], op=ALU.mult),
                                 reads=['B.CA%d' % d, 'cst'], writes=['B.CAL'])
                    for tau in range(Q):
                        ps, pk = nextps()
                        K.op('pe', lambda e, ps=ps, tau=tau: e.matmul(ps[:, 0:128], lhsT=B["E"][:, d, 0, :], rhs=B["CA"][:, d, tau, :], start=True, stop=True),
                             reads=['B.E%d' % d, 'B.CA%d' % d], writes=[pk])
                        K.op('dve', lambda e, ps=ps, tau=tau: e.tensor_tensor(out=B["KB"][:, tau * 2 + d, :], in0=ps[:, 0:128], in1=cst[:, C_BD:C_BD + 128], op=ALU.mult),
                             reads=[pk, 'cst'], writes=['B.KB'])
                K.op('dve', lambda e: e.tensor_scalar_mul(out=B["t1"], in0=cst[:, C_ID:C_ID + 128], scalar1=s5dc[:, l, gt:gt + 1]), reads=['cst', 's5dc'], writes=['B.t1'])
                tt_(B["t2"], B["KB"][:, 0, :], B["KB"][:, 1, :], ALU.add, ['B.KB'], ['B.t2'])
                tt_(B["KB"][:, 2 * Q, :], B["t1"], B["t2"], ALU.add, ['B.t1', 'B.t2'], ['B.KB'])

                NCHN = 4
                for d in range(2):
                    off = HP if d == 0 else 0
                    padlo = 0 if d == 0 else NC_
                    K.op('pool', lambda e, padlo=padlo: e.memset(B["H"][:, 16:16 + NCHN, padlo:padlo + HP], 0.0),
                         writes=['B.H%d' % (16 + i) for i in range(NCHN)])
                    for gg in range(8 // NCHN):
                        chains = []
                        for gi in range(NCHN):
                            g = gg * NCHN + gi
                            sl = gi
                            slk = 'B.S0L%d' % sl
                            for k in range(Q):
                                ps, pk = nextps()
                                K.op('pe', lambda e, ps=ps, k=k: e.transpose(out=ps[:, 0:128].bitcast(BF16)[:, 0:128], in_=B["E"][:, d, k, :], identity=idb[:]),
                                     reads=['B.E%d' % d, 'idb'], writes=[pk])
                                K.op('dve', lambda e, ps=ps, k=k, sl=sl, g=g: e.tensor_scalar_mul(out=B["S0L"][:, sl, k, :], in0=ps[:, 0:128].bitcast(BF16)[:, 0:128],
                                                                                      scalar1=cst[:, C_RM + g:C_RM + g + 1]),
                                     reads=[pk, 'cst'], writes=[slk])
                            hi = g * 2 + d
                            Hf = B["H"][:, hi, :]; Hfk = 'B.H%d' % hi
                            Hs = B["H"][:, 16 + gi, :]; Hsk = 'B.H%d' % (16 + gi)
                            for blk in range(NB):
                                ps, pk = nextps()
                                for s_ in range(Q):
                                    kk = (Q - 1 - s_) if d == 0 else s_
                                    c_lo = blk * BW
                                    K.op('pe', lambda e, ps=ps, s_=s_, kk=kk, c_lo=c_lo, sl=sl: e.matmul(ps[:, 0:BW], lhsT=B["S0L"][:, sl, kk, :],
                                                                                          rhs=B["uph"][:, s_, c_lo:c_lo + BW],
                                                                                          start=(s_ == 0), stop=(s_ == Q - 1)),
                                         reads=[slk, 'B.uph'], writes=[pk], signal=(s_ == Q - 1))
                                K.op('act', lambda e, ps=ps, blk=blk, Hf=Hf: e.copy(out=Hf[:, off + blk * BW: off + (blk + 1) * BW], in_=ps[:, 0:BW]),
                                     reads=[pk], writes=[Hfk])
                            chains.append([g, Hf, Hfk, Hs, Hsk])
                        evc = 0

                        def build_rp(lv):
                            nonlocal_rp = []
                            for chn in chains:
                                g = chn[0]
                                ri = rpc[0] % 8; rpc[0] += 1
                                rk = 'B.RP%d' % ri
                                rtk = 'B.RT%d' % ri
                                K.op('act', lambda e, lv=lv, g=g, ri=ri: e.mul(out=B["RT"][:, ri, :], in_=swb[:], mul=B["rp"][:, d, lv, 1, g:g + 1]),
                                     reads=['swb', 'B.rp%d' % d], writes=[rtk])
                                K.op('dve', lambda e, lv=lv, g=g, ri=ri: e.scalar_tensor_tensor(out=B["RP"][:, ri, :], in0=idb[:], scalar=B["rp"][:, d, lv, 0, g:g + 1],
                                                                                     in1=B["RT"][:, ri, :], op0=ALU.mult, op1=ALU.add),
                                     reads=['idb', 'B.rp%d' % d, rtk], writes=[rk])
                                nonlocal_rp.append((ri, rk))
                            return nonlocal_rp
                        cur_rp = build_rp(0)
                        for lv in range(NL):
                            sh = 2 ** lv
                            pend = []
                            for ci, chn in enumerate(chains):
                                g, src, srck, dst, dstk = chn
                                ri, rk = cur_rp[ci]
                                for blk in range(NB):
                                    ps, pk = nextps()
                                    a0 = off + blk * BW
                                    sa = a0 - sh if d == 0 else a0 + sh
                                    K.op('pe', lambda e, ps=ps, a0=a0, src=src: e.matmul(ps[:, 0:BW], lhsT=idb[:], rhs=src[:, a0:a0 + BW], start=True, stop=False),
                                         reads=['idb', srck], writes=[pk], signal=False)
                                    K.op('pe', lambda e, ps=ps, sa=sa, src=src, ri=ri: e.matmul(ps[:, 0:BW], lhsT=B["RP"][:, ri, :], rhs=src[:, sa:sa + BW], start=False, stop=True),
                                         reads=[rk, srck], writes=[pk])
                                    pend.append((ps, pk, a0, dst, dstk))
                                chn[1], chn[2], chn[3], chn[4] = dst, dstk, src, srck
                            if lv + 1 < NL:
                                cur_rp = build_rp(lv + 1)
                            for (ps, pk, a0, dst, dstk) in pend:
                                evc += 1
                                if evc % 2 == 0:
                                    K.op('act', lambda e, ps=ps, a0=a0, dst=dst: e.copy(out=dst[:, a0:a0 + BW], in_=ps[:, 0:BW]), reads=[pk], writes=[dstk])
                                else:
                                    K.op('dve', lambda e, ps=ps, a0=a0, dst=dst: e.tensor_copy(out=dst[:, a0:a0 + BW], in_=ps[:, 0:BW]), reads=[pk], writes=[dstk])
                        if NL % 2 == 1:
                            for chn in chains:
                                g, cur, curk, oth, othk = chn
                                K.op('pool', lambda e, cur=cur, oth=oth: e.tensor_copy(out=oth[:, off:off + NC_], in_=cur[:, off:off + NC_]), reads=[curk], writes=[othk])
                for r in range(Q):
                    for blk in range(NB):
                        ps, pk = nextps()
                        c_lo = blk * BW
                        first = True
                        for g in range(8):
                            idx = (g * Q + r) * 2 + 0
                            Hf = B["H"][:, g * 2, :]
                            K.op('pe', lambda e, ps=ps, idx=idx, Hf=Hf, c_lo=c_lo, first=first: e.matmul(ps[:, 0:BW], lhsT=B["CAL"][:, idx, :], rhs=Hf[:, HP + c_lo - 1: HP + c_lo - 1 + BW],
                                                                                          start=first, stop=False),
                                 reads=['B.CAL', 'B.H%d' % (g * 2)], writes=[pk], signal=False)
                            first = False
                            idx = (g * Q + (Q - r - 1)) * 2 + 1
                            Hb = B["H"][:, g * 2 + 1, :]
                            K.op('pe', lambda e, ps=ps, idx=idx, Hb=Hb, c_lo=c_lo: e.matmul(ps[:, 0:BW], lhsT=B["CAL"][:, idx, :], rhs=Hb[:, c_lo + 1: c_lo + 1 + BW],
                                                                              start=False, stop=False),
                                 reads=['B.CAL', 'B.H%d' % (g * 2 + 1)], writes=[pk], signal=False)
                        for s in range(Q):
                            kb = (r - s) * 2 if s < r else ((s - r) * 2 + 1 if s > r else 2 * Q)
                            K.op('pe', lambda e, ps=ps, kb=kb, s=s, c_lo=c_lo: e.matmul(ps[:, 0:BW], lhsT=B["KB"][:, kb, :],
                                                                           rhs=B["uph"][:, s, c_lo:c_lo + BW], start=False, stop=(s == Q - 1)),
                                 reads=['B.KB', 'B.uph'], writes=[pk], signal=(s == Q - 1))
                        K.op('act', lambda e, ps=ps, r=r, c_lo=c_lo: e.activation(out=B["ya"][:, r + Q * c_lo: r + Q * (c_lo + BW - 1) + 1: Q], in_=ps[:, 0:BW], func=AF.Gelu),
                             reads=[pk], writes=['B.ya'])
                K.dma('act', ya_s[gt * 128:(gt + 1) * 128, :], B["ya"], 'B.ya', reads=['B.ya'], writes=['ya_s'])

            Cc = carve([("xp", [T + 3], BF16), ("xc", [T], F32), ("xcb", [T], BF16), ("hf", [T], BF16), ("mk", [T], BF16),
                        ("wst", [4, 128], F32), ("wg", [4, 128], BF16),
                        ("r", [CH], F32), ("gi", [CH], F32), ("a", [CH], F32), ("e2", [CH], F32), ("b", [CH], F32),
                        ("h0", [CH], F32), ("h1", [CH], F32), ("gg", [CH], BF16), ("yb", [CH], BF16), ("mf", [512], F32)])
            K.op('pool', lambda e: e.memset(Cc["xp"], 0.0), writes=['C.xp'])
            K.op('pool', lambda e: e.memset(Cc["wst"], 0.0), writes=['C.wst'])
            for tt in range(NT):
                K.dma('sp', Cc["mf"], maskd[:, 1 + tt * 512: 1 + (tt + 1) * 512], 'C.mf', writes=['C.mf'])
                K.op('act', lambda e, tt=tt: e.copy(out=Cc["mk"][:, tt * 512:(tt + 1) * 512], in_=Cc["mf"]), reads=['C.mf'], writes=['C.mk'])
            for ct in range(8):
                K.dma('sp', Cc["xp"][:, 2:2 + T], xr_s[ct * 128:(ct + 1) * 128, :], 'C.xp', reads=['xr_s'], writes=['C.xp'])
                for d in range(2):
                    for wi, nm in enumerate(("lru_w_a", "lru_w_x")):
                        m = d * 2 + wi
                        for hb in range(2):
                            K.dma('sp', Cc["wst"][hb * 64:(hb + 1) * 64, m, hb * 64:(hb + 1) * 64], W[nm][l, d, 2 * ct + hb], 'C.wst', writes=['C.wst'])
                K.op('act', lambda e: e.copy(out=Cc["wg"], in_=Cc["wst"]), reads=['C.wst'], writes=['C.wg'])
                for cc in range(NCH):
                    sl = slice(cc * CH, (cc + 1) * CH)
                    eng = 'dve'
                    K.op(eng, lambda e, cc=cc, sl=sl: e.tensor_scalar(out=Cc["xc"][:, sl], in0=Cc["xp"][:, cc * CH: cc * CH + CH], scalar1=cwc[:, l, 0, ct:ct + 1],
                                                            scalar2=cbc[:, l, ct:ct + 1], op0=ALU.mult, op1=ALU.add),
                         reads=['C.xp', 'cwc', 'cbc'], writes=['C.xc%d' % cc])
                    for k in range(1, 4):
                        K.op(eng, lambda e, cc=cc, sl=sl, k=k: e.scalar_tensor_tensor(out=Cc["xc"][:, sl], in0=Cc["xp"][:, cc * CH + k: cc * CH + k + CH],
                                                                            scalar=cwc[:, l, k, ct:ct + 1], in1=Cc["xc"][:, sl], op0=ALU.mult, op1=ALU.add),
                             reads=['C.xp', 'cwc', 'C.xc%d' % cc], writes=['C.xc%d' % cc])
                    K.op(eng, lambda e, sl=sl: e.tensor_tensor(out=Cc["xc"][:, sl], in0=Cc["xc"][:, sl], in1=Cc["mk"][:, sl], op=ALU.mult),
                         reads=['C.xc%d' % cc, 'C.mk'], writes=['C.xc%d' % cc])
                    K.op('act', lambda e, sl=sl: e.copy(out=Cc["xcb"][:, sl], in_=Cc["xc"][:, sl]), reads=['C.xc%d' % cc], writes=['C.xcb%d' % cc])
                hctr = 0
                for d in range(2):
                    order = range(NCH) if d == 0 else range(NCH - 1, -1, -1)
                    prev = None
                    for cc in order:
                        sl = slice(cc * CH, (cc + 1) * CH)
                        for wi, (dstn, bcol) in enumerate((("r", bac), ("gi", bxc))):
                            for sb_ in range(CH // 512):
                                ps, pk = nextps()
                                K.op('pe', lambda e, ps=ps, wi=wi, cc=cc, sb_=sb_: e.matmul(ps[:, :], lhsT=Cc["wg"][:, d * 2 + wi, :],
                                                                                 rhs=Cc["xcb"][:, cc * CH + sb_ * 512: cc * CH + (sb_ + 1) * 512], start=True, stop=True),
                                     reads=['C.wg', 'C.xcb%d' % cc], writes=[pk])
                                K.op('act', lambda e, ps=ps, dstn=dstn, bcol=bcol, sb_=sb_: e.activation(out=Cc[dstn][:, sb_ * 512:(sb_ + 1) * 512], in_=ps[:, :], func=AF.Sigmoid,
                                                                                                bias=bcol[:, l, d, ct:ct + 1], scale=1.0),
                                     reads=[pk, 'bac', 'bxc'], writes=['C.' + dstn])
                        K.op('act', lambda e: e.activation(out=Cc["a"], in_=Cc["r"], func=AF.Exp, scale=sc1[:, l, d, ct:ct + 1]), reads=['C.r', 'sc1'], writes=['C.a'])
                        K.op('act', lambda e: e.activation(out=Cc["e2"], in_=Cc["r"], func=AF.Exp, scale=sc2[:, l, d, ct:ct + 1]), reads=['C.r', 'sc2'], writes=['C.e2'])
                        K.op('act', lambda e: e.activation(out=Cc["e2"], in_=Cc["e2"], func=AF.Sqrt, bias=onec, scale=-1.0), reads=['C.e2', 'cst'], writes=['C.e2'])
                        K.op('pool', lambda e: e.tensor_tensor(out=Cc["b"], in0=Cc["gi"], in1=Cc["e2"], op=ALU.mult), reads=['C.gi', 'C.e2'], writes=['C.b'])
                        K.op('pool', lambda e, sl=sl: e.tensor_tensor(out=Cc["b"], in0=Cc["b"], in1=Cc["xc"][:, sl], op=ALU.mult), reads=['C.b', 'C.xc%d' % cc], writes=['C.b'])
                        hn = "h%d" % (hctr % 2); hctr += 1
                        hk = 'C.' + hn
                        if d == 0:
                            init = 0.0 if prev is None else Cc[prev][:, CH - 1:CH]
                            K.op('dve', lambda e, hn=hn, init=init: e.tensor_tensor_scan(out=Cc[hn], data0=Cc["a"], data1=Cc["b"], initial=init, op0=ALU.mult, op1=ALU.add),
                                 reads=['C.a', 'C.b'] + ([] if prev is None else ['C.' + prev]), writes=[hk])
                            K.op('act', lambda e, hn=hn, sl=sl: e.copy(out=Cc["hf"][:, sl], in_=Cc[hn]), reads=[hk], writes=['C.hf'])
                        else:
                            init = 0.0 if prev is None else Cc[prev][:, 0:1]
                            K.op('dve', lambda e, hn=hn, init=init: e.tensor_tensor_scan(out=Cc[hn][:, ::-1], data0=Cc["a"][:, ::-1], data1=Cc["b"][:, ::-1], initial=init,
                                                                               op0=ALU.mult, op1=ALU.add),
                                 reads=['C.a', 'C.b'] + ([] if prev is None else ['C.' + prev]), writes=[hk])
                            K.dma('sp', Cc["gg"], gg_s[ct * 128:(ct + 1) * 128, sl], 'C.gg', reads=['gg_s'], writes=['C.gg'])
                            K.op('pool', lambda e, hn=hn, sl=sl: e.tensor_tensor(out=Cc["b"], in0=Cc[hn], in1=Cc["hf"][:, sl], op=ALU.add), reads=[hk, 'C.hf'], writes=['C.b'])
                            K.op('pool', lambda e: e.tensor_tensor(out=Cc["yb"], in0=Cc["b"], in1=Cc["gg"], op=ALU.mult), reads=['C.b', 'C.gg'], writes=['C.yb'])
                            K.dma('pool', yb_s[ct * 128:(ct + 1) * 128, sl], Cc["yb"], 'C.yb', reads=['C.yb'], writes=['yb_s'])
                        prev = hn

            E_ = carve([("ya", [8, 512], BF16), ("yb", [8, 512], BF16), ("yg", [8, 512], BF16), ("z", [512], BF16),
                        ("sga", [16, 512], BF16), ("sgb", [16, 512], BF16), ("mg", [16, 512], BF16),
                        ("xt", [16, 512], F32), ("mf", [512], F32), ("t1", [512], F32), ("t2", [512], F32)])
            for tt in range(NT):
                cs = slice(tt * 512, (tt + 1) * 512)
                K.dma('sp', E_["ya"], ya_s[:, cs].rearrange("(j p) c -> p j c", p=128), 'E.ya', reads=['ya_s'], writes=['E.ya'])
                K.dma('sp', E_["yb"], yb_s[:, cs].rearrange("(j p) c -> p j c", p=128), 'E.yb', reads=['yb_s'], writes=['E.yb'])
                K.dma('sp', E_["sga"], sga_s[:, cs].rearrange("(j p) c -> p j c", p=128), 'E.sga', reads=['sga_s'], writes=['E.sga'])
                K.dma('sp', E_["sgb"], sgb_s[:, cs].rearrange("(j p) c -> p j c", p=128), 'E.sgb', reads=['sgb_s'], writes=['E.sgb'])
                K.dma('sp', E_["xt"], xs[:, 1 + tt * 512: 1 + (tt + 1) * 512].rearrange("(kt p) c -> p kt c", p=128), 'E.xt', reads=[xs_k], writes=['E.xt'])
                K.dma('sp', E_["mf"], maskd[:, 1 + tt * 512: 1 + (tt + 1) * 512], 'E.mf', writes=['E.mf'])
                for j in range(8):
                    wt, wk = wload("s5_w_glu", l, j, 8)
                    ps, pk = nextps()
                    for kt in range(8):
                        K.op('pe', lambda e, ps=ps, wt=wt, kt=kt: e.matmul(ps[:, :], lhsT=wt[:, kt, :], rhs=E_["ya"][:, kt, :], start=(kt == 0), stop=(kt == 7)),
                             reads=[wk, 'E.ya'], writes=[pk], signal=(kt == 7))
                    K.op('act', lambda e, ps=ps: e.activation(out=E_["z"], in_=ps[:, :], func=AF.Sigmoid), reads=[pk], writes=['E.z'])
                    K.op('dve', lambda e, j=j: e.tensor_tensor(out=E_["yg"][:, j, :], in0=E_["ya"][:, j, :], in1=E_["z"], op=ALU.mult), reads=['E.ya', 'E.z'], writes=['E.yg'])
                for n in range(16):
                    wa, wak = wload("w_proj_a", l, n, 8)
                    wb_, wbk = wload("w_proj_b", l, n, 8)
                    psa, pka = nextps()
                    for kt in range(8):
                        K.op('pe', lambda e, psa=psa, wa=wa, kt=kt: e.matmul(psa[:, :], lhsT=wa[:, kt, :], rhs=E_["yg"][:, kt, :], start=(kt == 0), stop=(kt == 7)),
                             reads=[wak, 'E.yg'], writes=[pka], signal=(kt == 7))
                    psb_, pkb = nextps()
                    for kt in range(8):
                        K.op('pe', lambda e, psb_=psb_, wb_=wb_, kt=kt: e.matmul(psb_[:, :], lhsT=wb_[:, kt, :], rhs=E_["yb"][:, kt, :], start=(kt == 0), stop=(kt == 7)),
                             reads=[wbk, 'E.yb'], writes=[pkb], signal=(kt == 7))
                    K.op('dve', lambda e, psa=psa, n=n: e.tensor_tensor(out=E_["t1"], in0=psa[:, :], in1=E_["sga"][:, n, :], op=ALU.mult), reads=[pka, 'E.sga'], writes=['E.t1'])
                    K.op('dve', lambda e, psb_=psb_, n=n: e.tensor_tensor(out=E_["t2"], in0=psb_[:, :], in1=E_["sgb"][:, n, :], op=ALU.mult), reads=[pkb, 'E.sgb'], writes=['E.t2'])
                    K.op('pool', lambda e, n=n: e.tensor_tensor(out=E_["mg"][:, n, :], in0=E_["t1"], in1=E_["t2"], op=ALU.add), reads=['E.t1', 'E.t2'], writes=['E.mg'])
                for n in range(16):
                    wt, wk = wload("w_out", l, n, 16)
                    ps, pk = nextps()
                    for kt in range(16):
                        K.op('pe', lambda e, ps=ps, wt=wt, kt=kt: e.matmul(ps[:, :], lhsT=wt[:, kt, :], rhs=E_["mg"][:, kt, :], start=(kt == 0), stop=(kt == 15)),
                             reads=[wk, 'E.mg'], writes=[pk], signal=(kt == 15))
                    K.op('dve', lambda e, ps=ps: e.tensor_tensor(out=E_["t1"], in0=ps[:, :], in1=E_["mf"], op=ALU.mult), reads=[pk, 'E.mf'], writes=['E.t1'])
                    K.op('pool', lambda e, n=n: e.tensor_tensor(out=E_["xt"][:, n, :], in0=E_["xt"][:, n, :], in1=E_["t1"], op=ALU.add), reads=['E.t1', 'E.xt'], writes=['E.xt'])
                K.dma('pool', xs[:, 1 + tt * 512: 1 + (tt + 1) * 512].rearrange("(kt p) c -> p kt c", p=128), E_["xt"], 'E.xt', reads=['E.xt'], writes=[xs_k])

            Fh = carve([("xt", [16, 514], F32), ("tmp", [16, 514], BF16), ("rs", [514], F32), ("xn", [16, 514], BF16),
                        ("act", [48, 512], BF16), ("hg", [514], F32), ("hv", [514], F32), ("cg", [512], F32), ("cv", [512], F32),
                        ("wdn0", [48, 128], BF16), ("wdn1", [48, 128], BF16), ("mf", [512], F32), ("t1", [512], F32)])
            wdn = [Fh["wdn0"], Fh["wdn1"]]
            for tt in range(NT):
                c0 = tt * 512
                K.dma('sp', Fh["xt"], xs[:, c0:c0 + 514].rearrange("(kt p) c -> p kt c", p=128), 'F.xt', reads=[xs_k], writes=['F.xt'])
                K.dma('sp', Fh["mf"], maskd[:, 1 + tt * 512: 1 + (tt + 1) * 512], 'F.mf', writes=['F.mf'])
                rmsnorm(Fh["xt"], 'F.xt', 514, g2c[:, l, :], Fh["xn"], 'F.xn', Fh["tmp"], 'F.tmp', Fh["rs"], 'F.rs')
                for f in range(48):
                    res = []
                    for (jj, hn) in ((f, "hg"), (48 + f, "hv")):
                        wt, wk = wload("ffn_w_up", l, jj, 16)
                        ps, pk = nextps()
                        for kt in range(16):
                            K.op('pe', lambda e, ps=ps, wt=wt, kt=kt: e.matmul(ps[:, :], lhsT=wt[:, kt, :], rhs=Fh["xn"][:, kt, 1:513], start=(kt == 0), stop=(kt == 15)),
                                 reads=[wk, 'F.xn'], writes=[pk], signal=(kt == 15))
                        ps2, pk2 = nextps()
                        for kt in range(16):
                            K.op('pe', lambda e, ps2=ps2, wt=wt, kt=kt: e.matmul(ps2[:, 0:2], lhsT=wt[:, kt, :], rhs=Fh["xn"][:, kt, 0:514:513], start=(kt == 0), stop=(kt == 15)),
                                 reads=[wk, 'F.xn'], writes=[pk2], signal=(kt == 15))
                        K.op('act', lambda e, ps=ps, hn=hn: e.copy(out=Fh[hn][:, 1:513], in_=ps[:, :]), reads=[pk], writes=['F.' + hn])
                        K.op('act', lambda e, ps2=ps2, hn=hn: e.copy(out=Fh[hn][:, 0:514:513], in_=ps2[:, 0:2]), reads=[pk2], writes=['F.' + hn])
                    for (hn, cn, jj, eng) in (("hg", "cg", f, 'dve'), ("hv", "cv", 48 + f, 'dve')):
                        K.op(eng, lambda e, hn=hn, cn=cn, jj=jj: e.tensor_scalar_mul(out=Fh[cn], in0=Fh[hn][:, 0:512], scalar1=fcw[:, l, 0, jj:jj + 1]),
                             reads=['F.' + hn, 'fcw'], writes=['F.' + cn])
                        for k in (1, 2):
                            K.op(eng, lambda e, hn=hn, cn=cn, jj=jj, k=k: e.scalar_tensor_tensor(out=Fh[cn], in0=Fh[hn][:, k:k + 512], scalar=fcw[:, l, k, jj:jj + 1],
                                                                                   in1=Fh[cn], op0=ALU.mult, op1=ALU.add),
                                 reads=['F.' + hn, 'fcw', 'F.' + cn], writes=['F.' + cn])
                    K.op('act', lambda e: e.activation(out=Fh["cg"], in_=Fh["cg"], func=AF.Gelu), reads=['F.cg'], writes=['F.cg'])
                    K.op('dve', lambda e, f=f: e.tensor_tensor(out=Fh["act"][:, f, :], in0=Fh["cg"], in1=Fh["cv"], op=ALU.mult), reads=['F.cg', 'F.cv'], writes=['F.act'])
                for n in range(16):
                    wi = n % 2
                    wdk = 'wdn%d' % wi
                    K.dma('sp', wdn[wi], WT["ffn_w_down"][l, n], wdk, reads=[('wt', 'ffn_w_down', l)], writes=[wdk])
                    ps, pk = nextps()
                    for kt in range(48):
                        K.op('pe', lambda e, ps=ps, wi=wi, kt=kt: e.matmul(ps[:, :], lhsT=wdn[wi][:, kt, :], rhs=Fh["act"][:, kt, :], start=(kt == 0), stop=(kt == 47)),
                             reads=[wdk, 'F.act'], writes=[pk], signal=(kt == 47))
                    K.op('dve', lambda e, ps=ps: e.tensor_tensor(out=Fh["t1"], in0=ps[:, :], in1=Fh["mf"], op=ALU.mult), reads=[pk, 'F.mf'], writes=['F.t1'])
                    K.op('pool', lambda e, n=n: e.tensor_tensor(out=Fh["xt"][:, n, 1:513], in0=Fh["xt"][:, n, 1:513], in1=Fh["t1"], op=ALU.add), reads=['F.t1', 'F.xt'], writes=['F.xt'])
                K.dma('pool', xs2[:, 1 + tt * 512: 1 + (tt + 1) * 512].rearrange("(kt p) c -> p kt c", p=128), Fh["xt"][:, :, 1:513], 'F.xt', reads=['F.xt'], writes=[xs2_k])
            xs, xs2 = xs2, xs
            xs_k, xs2_k = xs2_k, xs_k
        G = carve([("xt0", [16, 512], F32), ("xt1", [16, 512], F32), ("tmp", [16, 512], BF16), ("rs", [512], F32),
                   ("o0", [16, 512], F32), ("o1", [16, 512], F32)])
        for tt in range(NT):
            xt = G["xt%d" % (tt % 2)]; xkey = 'G.xt%d' % (tt % 2)
            o = G["o%d" % (tt % 2)]; okey = 'G.o%d' % (tt % 2)
            K.dma('sp', xt, xs[:, 1 + tt * 512: 1 + (tt + 1) * 512].rearrange("(kt p) c -> p kt c", p=128), xkey, reads=[xs_k], writes=[xkey])
            rmsnorm(xt, xkey, 512, gfc[:, :], None, okey, G["tmp"], 'G.tmp', G["rs"], 'G.rs', out_f32=o)
            K.dma('sp', yT[:, tt * 512:(tt + 1) * 512].rearrange("(kt p) c -> p kt c", p=128), o, okey, reads=[okey], writes=['yT'])
        K.finish()
        print("instructions emitted:", K.nins, "sems:", len(K.sem))
    return nc


WNAMES = ["norm1_g", "w_in", "s5_a_re", "s5_a_im", "s5_log_dt", "s5_b_re", "s5_b_im", "s5_c_re", "s5_c_im", "s5_d",
          "s5_w_glu", "lru_conv_w", "lru_conv_b", "lru_w_a", "lru_b_a", "lru_w_x", "lru_b_x", "lru_lambda",
          "w_proj_a", "w_proj_b", "w_out", "norm2_g", "ffn_w_up", "ffn_conv_w", "ffn_w_down", "final_g"]


def run_seqs(seqs, weights, T, L, n_cores, debug=False):
    nc = build(T, L, debug)
    consts = make_consts()
    SMALL = {"norm1_g", "norm2_g", "final_g", "s5_d", "lru_conv_w", "lru_conv_b", "lru_b_a", "lru_b_x", "lru_lambda", "ffn_conv_w"}
    wd = {k: np.ascontiguousarray(np.asarray(weights[k], np.float32)) for k in WNAMES if k not in SMALL}
    wd["smalls"] = make_smalls(weights, L)
    in_maps = []
    for sq in seqs:
        S = sq.shape[0]
        xT = np.zeros((D, T + 2), np.float32)
        xT[:, 1:1 + S] = sq.T
        mask = np.zeros((128, T + 2), np.float32)
        mask[:, 1:1 + S] = 1.0
        m = {"xT": xT, "mask": mask, "consts": consts}
        m.update(wd)
        in_maps.append(m)
    res = run_bass_kernel_spmd(nc, in_maps, core_ids=list(range(n_cores)))
    if debug:
        return res.results
    outs = []
    for sq, r in zip(seqs, res.results):
        S = sq.shape[0]
        outs.append(np.ascontiguousarray(np.asarray(r["yT"])[:, :S].T).astype(np.float32))
    return outs


def kernel(**inputs):
    xp = np.asarray(inputs["x_prompt"], np.float32)
    xsm = np.asarray(inputs["x_sample"], np.float32)
    seqs = [xp[0]] + [xsm[i] for i in range(4)] + [xsm[3]] * 3
    outs = run_seqs(seqs, inputs, 8192, DEPTH, 8)
    y_prompt = outs[0][None]
    y_sample = np.stack(outs[1:5], axis=0)
    return (y_prompt, y_sample)
```
